# Optimizing a Trainium2 kernel written in Bass

```python
import jax, jax.numpy as jnp
from jax import lax
import numpy as np

D_MODEL = 1024
BATCH = 4
SEQ = 8192
DEPTH = 1
DEC_BATCH = 16
DEC_SEQ = 16
PAST_LEN = 1024

CHUNK = 64
D_POOL = 512
POOL_WINDOWS = (2, 4, 8, 16)
N_POOL_GROUPS = len(POOL_WINDOWS)
POOL_GROUP = D_POOL // N_POOL_GROUPS
POOL_BUF = max(POOL_WINDOWS) - 1
N_HG_HEADS = 4
HG_KEY = 128
HG_VAL = 128
D_HG_K = N_HG_HEADS * HG_KEY
D_HG_V = N_HG_HEADS * HG_VAL
D_MIX = D_POOL + D_HG_V
D_IN = 2 * D_POOL + 2 * D_HG_K + 2 * D_HG_V
SCAN_BLK = CHUNK // 4
D_PLE = 256
EPS = 1e-6

kernel_name = "hybrid_pool_hgrn2_stream_step"


def rmsnorm(x, g):
    xf = x.astype(jnp.float32)
    y = xf * lax.rsqrt(jnp.mean(xf * xf, axis=-1, keepdims=True) + EPS)
    return (y * g.astype(jnp.float32)).astype(x.dtype)


def pool_mixer(u, buf, start_pos, w_grp, scale):
    b, t, _ = u.shape
    ext = jnp.concatenate([buf.astype(jnp.float32), u.astype(jnp.float32)], axis=1)
    cs = jnp.concatenate([jnp.zeros((b, 1, D_POOL), jnp.float32), jnp.cumsum(ext, axis=1)], axis=1)
    pos = start_pos + jnp.arange(t)
    outs = []
    for gi, w in enumerate(POOL_WINDOWS):
        sl = slice(gi * POOL_GROUP, (gi + 1) * POOL_GROUP)
        s = cs[:, POOL_BUF + 1:, sl] - cs[:, POOL_BUF + 1 - w:POOL_BUF + 1 - w + t, sl]
        cnt = jnp.minimum(w, pos + 1).astype(jnp.float32)[None, :, None]
        outs.append(s / cnt - ext[:, POOL_BUF:, sl])
    pooled = jnp.stack(outs, axis=2).astype(u.dtype)
    mixed = jnp.einsum('btgc,gcd->btgd', pooled, w_grp).reshape(b, t, D_POOL)
    new_buf = ext[:, -POOL_BUF:].astype(u.dtype)
    return mixed * scale, new_buf


def hgrn2(q, f_logit, v, lb, s0):
    f32 = jnp.float32
    b, t = q.shape[:2]
    q = q.astype(f32).reshape(b, t, N_HG_HEADS, HG_KEY)
    fl = f_logit.astype(f32).reshape(b, t, N_HG_HEADS, HG_KEY)
    v = v.astype(f32).reshape(b, t, N_HG_HEADS, HG_VAL)
    lbh = lb.reshape(N_HG_HEADS, HG_KEY)
    f = lbh + (1.0 - lbh) * jax.nn.sigmoid(fl)
    g = jnp.log(f)
    k = 1.0 - f
    n_blk = -(-t // SCAN_BLK)
    tp = n_blk * SCAN_BLK
    pad = ((0, 0), (0, tp - t), (0, 0), (0, 0))
    q, k, g, v = [jnp.pad(arr, pad).reshape(b, n_blk, SCAN_BLK, N_HG_HEADS, -1) for arr in (q, k, g, v)]
    bc = jnp.cumsum(g, axis=2)
    bl = bc[:, :, -1:]
    qe = q * jnp.exp(bc)
    ke = k * jnp.exp(-bc)
    kd = k * jnp.exp(bl - bc)
    mask = jnp.tril(jnp.ones((SCAN_BLK, SCAN_BLK), bool))
    att = jnp.where(mask, jnp.einsum('bnthk,bnshk->bnhts', qe, ke), 0.0)
    o_intra = jnp.einsum('bnhts,bnshv->bnthv', att, v)
    decay = jnp.exp(bl[:, :, 0])

    def step(s, xs):
        qe_n, kd_n, v_n, dec_n = xs
        o_n = jnp.einsum('bthk,bhkv->bthv', qe_n, s)
        s = dec_n[..., None] * s + jnp.einsum('bthk,bthv->bhkv', kd_n, v_n)
        return s, o_n

    xs = tuple(jnp.moveaxis(arr, 1, 0) for arr in (qe, kd, v, decay))
    s_fin, o_inter = lax.scan(step, s0.astype(f32), xs)
    o = o_intra + jnp.moveaxis(o_inter, 0, 1)
    o = o.reshape(b, tp, N_HG_HEADS, HG_VAL)[:, :t]
    return o, s_fin


def layer(x, p, pool_buf, s0, start_pos, lb, w_in, w_pool, pool_scale, hg_norm,
          w_out, norm_pre, norm_post, w_ple, w_ple_gate):
    b, t, _ = x.shape
    h = rmsnorm(x, norm_pre)
    z = h @ w_in
    u, g_pool, q, f_logit, v_in, g_hg = jnp.split(
        z, [D_POOL, 2 * D_POOL, 2 * D_POOL + D_HG_K, 2 * D_POOL + 2 * D_HG_K,
            2 * D_POOL + 2 * D_HG_K + D_HG_V], axis=-1)
    pool_out, new_buf = pool_mixer(u, pool_buf, start_pos, w_pool, pool_scale)
    pool_out = pool_out * jax.nn.silu(g_pool)
    o, s_fin = hgrn2(q, f_logit, v_in, lb, s0)
    o = rmsnorm(o, hg_norm).reshape(b, t, D_HG_V).astype(x.dtype) * jax.nn.silu(g_hg)
    y = jnp.concatenate([pool_out, o], axis=-1) @ w_out
    x = x + rmsnorm(y, norm_post)
    x = x + jax.nn.sigmoid(x @ w_ple_gate) * (p @ w_ple)
    return x, new_buf, s_fin.astype(x.dtype)


def setup_inputs(seed: int = 0) -> dict:
    key = jax.random.key(seed)
    ks = jax.random.split(key, 20)
    nrm = jax.random.normal
    f32 = jnp.float32
    return {
        "x_prompt": nrm(ks[0], (BATCH, SEQ, D_MODEL), f32),
        "x_sample": nrm(ks[1], (DEC_BATCH, DEC_SEQ, D_MODEL), f32),
        "cache_pool": nrm(ks[2], (DEPTH, DEC_BATCH, POOL_BUF, D_POOL), f32),
        "state_hgrn": 0.5 * nrm(ks[3], (DEPTH, DEC_BATCH, N_HG_HEADS, HG_KEY, HG_VAL), f32),
        "p_prompt": nrm(ks[4], (DEPTH, BATCH, SEQ, D_PLE), f32),
        "p_sample": nrm(ks[5], (DEPTH, DEC_BATCH, DEC_SEQ, D_PLE), f32),
        "lb_logits": 0.5 * nrm(ks[6], (DEPTH + 1, D_HG_K), f32),
        "w_in": nrm(ks[7], (DEPTH, D_MODEL, D_IN), f32) * D_MODEL ** -0.5,
        "w_pool": nrm(ks[8], (DEPTH, N_POOL_GROUPS, POOL_GROUP, POOL_GROUP), f32) * POOL_GROUP ** -0.5,
        "pool_scale": 1.0 + 0.1 * nrm(ks[9], (DEPTH, D_POOL), f32),
        "hg_norm": 1.0 + 0.1 * nrm(ks[10], (DEPTH, HG_VAL), f32),
        "w_out": nrm(ks[11], (DEPTH, D_MIX, D_MODEL), f32) * D_MIX ** -0.5,
        "norm_pre": 1.0 + 0.1 * nrm(ks[12], (DEPTH, D_MODEL), f32),
        "norm_post": 1.0 + 0.1 * nrm(ks[13], (DEPTH, D_MODEL), f32),
        "w_ple": nrm(ks[14], (DEPTH, D_PLE, D_MODEL), f32) * D_PLE ** -0.5,
        "w_ple_gate": nrm(ks[15], (DEPTH, D_MODEL, D_MODEL), f32) * D_MODEL ** -0.5,
    }


def reference(x_prompt, x_sample, cache_pool, state_hgrn, p_prompt, p_sample, lb_logits,
              w_in, w_pool, pool_scale, hg_norm, w_out, norm_pre, norm_post, w_ple, w_ple_gate):
    lb_all = jnp.cumsum(jax.nn.softmax(lb_logits.astype(jnp.float32), axis=0), axis=0)
    xp, xs = x_prompt, x_sample
    pool_p, hg_p, pool_s, hg_s = [], [], [], []
    for l in range(DEPTH):
        wl = (lb_all[l], w_in[l], w_pool[l], pool_scale[l], hg_norm[l], w_out[l],
              norm_pre[l], norm_post[l], w_ple[l], w_ple_gate[l])
        zero_buf = jnp.zeros((xp.shape[0], POOL_BUF, D_POOL), xp.dtype)
        zero_s = jnp.zeros((xp.shape[0], N_HG_HEADS, HG_KEY, HG_VAL), jnp.float32)
        xp, bp, sp = layer(xp, p_prompt[l], zero_buf, zero_s, 0, *wl)
        xs, bs, ss = layer(xs, p_sample[l], cache_pool[l], state_hgrn[l], PAST_LEN, *wl)
        pool_p.append(bp); hg_p.append(sp); pool_s.append(bs); hg_s.append(ss)
    new_pool_prompt = jnp.stack(pool_p)
    new_hgrn_prompt = jnp.stack(hg_p)
    new_pool_sample = jnp.stack(pool_s)
    new_hgrn_sample = jnp.stack(hg_s)
    return (xp, xs, new_pool_prompt, new_hgrn_prompt, new_pool_sample, new_hgrn_sample)
```

```python
import contextlib
import os
import numpy as np
import concourse.bass as bass
import concourse.mybir as mybir
from concourse.bass_utils import run_bass_kernel_spmd

F32 = mybir.dt.float32
BF16 = mybir.dt.bfloat16
AF = mybir.ActivationFunctionType
ALU = mybir.AluOpType

P = 128
D = 1024
KD_ = 8
DIN = 3072
DPLE = 256
EPS = 1e-6
SEQ = 8192
HALF = 4096
NCORES = 8
C_U, C_GP, C_Q, C_F, C_V, C_GH = 0, 512, 1024, 1536, 2048, 2560


class _StopBuild(Exception):
    pass


_KSTOP = float(os.environ.get("KSTOP", "999"))


def ck(n):
    if n >= _KSTOP:
        raise _StopBuild()


class Sched:
    EPOCH = 30000

    def __init__(self, nc):
        self.nc = nc
        self.eng = {"pe": nc.tensor, "act": nc.scalar, "dve": nc.vector,
                    "pool": nc.gpsimd, "sp": nc.sync}
        self.cur_sem = {}
        self.cnt = {}
        self.nsem = 0
        self.sem_objs = {}
        for e in self.eng:
            self._new_epoch(e)
        self.waited = {}
        self.lastw = {}
        self.readers = {}
        self.children = {}
        self.pend_r = {e: [] for e in self.eng}
        self.pend_w = {e: [] for e in self.eng}
        self.dma_sem = {}
        self.dma_cnt = {}
        self.n_wait = 0
        self.n_ops = 0

    def _alloc(self, name):
        h = self.nc.alloc_semaphore(name=name)
        self.nsem += 1
        self.sem_objs[name] = h
        return name

    def _new_epoch(self, e):
        k = self._alloc("pg_%s_%d" % (e, self.nsem))
        self.cur_sem[e] = k
        self.cnt[e] = 0

    def _rel(self, name):
        if "/" in name:
            par = name.split("/")[0]
            ch = self.children.setdefault(par, [])
            if name not in ch:
                ch.append(name)
            return (name, par)
        return (name,) + tuple(self.children.get(name, ()))

    def _wait(self, e, ticket):
        key, val, src = ticket
        if src == e and e == "pe":
            return
        if self.waited.get((e, key), 0) >= val:
            return
        self.eng[e].wait_ge(self.sem_objs[key], val)
        self.waited[(e, key)] = val
        self.n_wait += 1

    def _deps(self, e, reads, writes):
        deps = []
        for r in reads:
            for nm in self._rel(r):
                t = self.lastw.get(nm)
                if t is not None:
                    deps.append(t)
                if nm[0] == "P" and nm[1] == "B" and nm[2:].isdigit():
                    for t in self.readers.get(nm, ()):
                        if t[2] != e:
                            deps.append(t)
        for w in writes:
            for nm in self._rel(w):
                t = self.lastw.get(nm)
                if t is not None:
                    deps.append(t)
                deps.extend(self.readers.get(nm, ()))
        return deps

    def _check_pending(self, e, reads, writes):
        for oe in self.eng:
            if oe == e:
                continue
            for r in list(reads) + list(writes):
                for nm in self._rel(r):
                    if nm in self.pend_w[oe]:
                        raise RuntimeError("dependency on pending write %s (%s)" % (nm, oe))
            for w in writes:
                for nm in self._rel(w):
                    if nm in self.pend_r[oe]:
                        raise RuntimeError("WAR on pending read %s (%s)" % (nm, oe))

    def _record(self, t, reads, writes):
        for w in writes:
            self.lastw[w] = t
            self.readers[w] = []
            if "/" not in w:
                for c in self.children.get(w, ()):
                    self.readers[c] = []
        for r in reads:
            if r not in writes:
                self.readers.setdefault(r, []).append(t)

    def op(self, e, fn, reads=(), writes=(), inc=True):
        self.n_ops += 1
        self._check_pending(e, reads, writes)
        for t in self._deps(e, reads, writes):
            self._wait(e, t)
        ins = fn()
        self.pend_r[e].extend(reads)
        self.pend_w[e].extend(writes)
        if inc:
            if self.cnt[e] >= self.EPOCH:
                self._new_epoch(e)
            self.cnt[e] += 1
            key = self.cur_sem[e]
            ins.then_inc(self.sem_objs[key], 1)
            t = (key, self.cnt[e], e)
            self._record(t, self.pend_r[e], self.pend_w[e])
            self.pend_r[e] = []
            self.pend_w[e] = []
        return ins

    def dma(self, chan, out, in_, reads=(), writes=(), eng="sp", nowait=False, **kw):
        self.n_ops += 1
        if chan not in self.dma_sem:
            self.dma_sem[chan] = self._alloc("dma_%s" % chan)
            self.dma_cnt[chan] = 0
        self._check_pending(eng, reads, writes)
        if not nowait:
            for t in self._deps("dma", reads, writes):
                self._wait(eng, t)
        key = self.dma_sem[chan]
        ins = self.eng[eng].dma_start(out=out, in_=in_, **kw)
        self.dma_cnt[chan] += 16
        ins.then_inc(self.sem_objs[key], 16)
        t = (key, self.dma_cnt[chan], "dma:" + chan)
        self._record(t, reads, writes)
        return t

    def finish(self, e="sp"):
        for chan, key in self.dma_sem.items():
            if self.dma_cnt[chan] > 0:
                self._wait(e, (key, self.dma_cnt[chan], "dma:" + chan))


class Runner:
    LAT = 180.0
    SLACK = 1000.0

    def __init__(self, s, banks):
        self.s = s
        self.free = list(banks)
        self.events = set()
        self.t_eng = {e: 0.0 for e in s.eng}
        self.t_w = {}
        self.t_r = {}

    def _est(self, op):
        kind = op[0]
        if kind == "group":
            return self._est(op[1][0])
        if kind == "dma":
            _, chan, out, in_, reads, writes, eng, kw, cost = op
            e = eng
        else:
            _, e, fn, reads, writes, inc, cost = op
        t = self.t_eng[e]
        for r in reads:
            t = max(t, self.t_w.get(r, 0.0))
        for w in writes:
            t = max(t, self.t_w.get(w, 0.0), self.t_r.get(w, 0.0))
        return t

    def _emit(self, op):
        kind = op[0]
        if kind == "group":
            for o in op[1]:
                self._emit(o)
            return
        t = self._est(op)
        if kind == "dma":
            _, chan, out, in_, reads, writes, eng, kw, cost = op
            self.s.dma(chan, out, in_, reads=reads, writes=writes, eng=eng, **kw)
            self.t_eng[eng] = t + 60.0
            end = t + cost
        else:
            _, e, fn, reads, writes, inc, cost = op
            self.s.op(e, fn, reads=reads, writes=writes, inc=inc)
            self.t_eng[e] = t + cost
            end = t + cost + self.LAT
        for w in writes:
            self.t_w[w] = end
        for r in reads:
            self.t_r[r] = max(self.t_r.get(r, 0.0), end)

    def run(self, gens):
        ths = [dict(g=g, head=None, send=None, done=False, name=n) for n, g in gens]

        def step(th):
            try:
                th["head"] = th["g"].send(th["send"])
            except StopIteration:
                th["head"] = None
                th["done"] = True
            th["send"] = None

        for th in ths:
            step(th)
        while True:
            progressed = True
            while progressed:
                progressed = False
                for th in ths:
                    while not th["done"]:
                        h = th["head"]
                        k = h[0]
                        if k == "free":
                            self.free.extend(h[1])
                            step(th)
                            progressed = True
                        elif k == "set":
                            self.events.add(h[1])
                            step(th)
                            progressed = True
                        elif k == "wait":
                            if all(ev in self.events for ev in h[1]):
                                step(th)
                                progressed = True
                            else:
                                break
                        elif k == "alloc":
                            if len(self.free) >= h[1]:
                                th["send"] = [self.free.pop(0) for _ in range(h[1])]
                                step(th)
                                progressed = True
                            else:
                                break
                        else:
                            break
            cands = [th for th in ths if not th["done"] and th["head"][0] in ("op", "dma", "group")]
            if not cands:
                if all(th["done"] for th in ths):
                    return
                raise RuntimeError("runner deadlock: " + str([(th["name"], th["head"][:2]) for th in ths if not th["done"]]))
            ests = [(self._est(th["head"]), i, th) for i, th in enumerate(cands)]
            tmin = min(e[0] for e in ests)
            th = min((e for e in ests if e[0] <= tmin + self.SLACK), key=lambda e: e[1])[2]
            self._emit(th["head"])
            step(th)


def OP(e, fn, reads=(), writes=(), inc=True, cost=300.0):
    return ("op", e, fn, tuple(reads), tuple(writes), inc, cost)


def DMA(chan, out, in_, reads=(), writes=(), eng="sp", cost=3000.0, **kw):
    return ("dma", chan, out, in_, tuple(reads), tuple(writes), eng, kw, cost)


def c_pe(n):
    return 30.0 + 0.45 * n


def c_act(f):
    return 260.0 + 0.85 * f


def c_dve(f, psum=False, fast=1.0):
    return (220.0 if psum else 160.0) + 1.05 * f / fast


def c_pool(f):
    return 300.0 + 2.3 * f


def build_program(npre, nmain, do_sample=True):
    nc = bass.Bass("TRN2", target_bir_lowering=False)

    def din(name, shape):
        return nc.dram_tensor(name, list(shape), F32, kind="ExternalInput").ap()

    def dout(name, shape):
        return nc.dram_tensor(name, list(shape), F32, kind="ExternalOutput").ap()

    x_pre = din("x_pre", [max(npre, P), D])
    x_main = din("x_main", [nmain, D])
    p_main = din("p_main", [nmain, DPLE])
    x_smp = din("x_smp", [64, D])
    p_smp = din("p_smp", [64, DPLE])
    cache_smp = din("cache_smp", [64, 512])
    state_smp = din("state_smp", [2, 4, P, P])
    w_in = din("w_in", [D, DIN])
    w_pool = din("w_pool", [4, P, P])
    pool_scale = din("pool_scale", [512])
    hg_norm = din("hg_norm", [P])
    w_out = din("w_out", [D, D])
    norm_pre = din("norm_pre", [D])
    norm_post = din("norm_post", [D])
    w_ple = din("w_ple", [DPLE, D])
    w_gate = din("w_gate", [D, D])
    lb_logits = din("lb_logits", [2, 512])
    band_cur = din("band_cur", [P, 4, P])
    band_prev = din("band_prev", [P, 4, P])
    band_first = din("band_first", [P, 4, P])
    band_scur = din("band_scur", [64, 4, 64])
    band_sprev = din("band_sprev", [64, 4, 64])
    mask_att = din("mask_att", [P, P])
    mask_satt = din("mask_satt", [64, 64])
    rmask = din("rmask", [P, 512])
    rmask_s = din("rmask_s", [P, 64])

    y_main = dout("y_main", [nmain, D])
    y_smp = dout("y_smp", [64, D])
    pool_p = dout("pool_p", [P, 512])
    pool_s = dout("pool_s", [64, 512])
    state_p = dout("state_p", [4, P, P])
    state_s = dout("state_s", [2, 4, P, P])

    s = Sched(nc)
    es = contextlib.ExitStack()

    def T(nm, shp, dt=F32):
        return es.enter_context(nc.sbuf_tensor(nm, list(shp), dt))

    def PS(nm, shp, dt=F32):
        return es.enter_context(nc.psum_tensor(nm, list(shp), dt))

    V, A_, G_, PE_ = nc.vector, nc.scalar, nc.gpsimd, nc.tensor

    with es:
        WIN = T("WIN", [P, KD_, DIN], BF16)
        WOUT = T("WOUT", [P, KD_, D], BF16)
        WG = T("WG", [P, KD_, D], BF16)
        WPLE = T("WPLE", [P, 2, D], BF16)
        WPOOL = T("WPOOL", [P, 4, P], BF16)
        NPRE = T("NPRE", [P, D])
        GPOST = T("GPOST", [P, D])
        PSC = T("PSC", [P, 4])
        HGN = T("HGN", [P, 1])
        LBT = T("LBT", [P, 8])
        OML = T("OML", [P, 4])
        HSC = T("HSC", [P, 4])
        HBI = T("HBI", [P, 4])
        IOML = T("IOML", [P, 4])
        MH = T("MH", [P, 4])
        IDB = T("IDB", [P, P], BF16)
        BANDC = T("BANDC", [P, 4, P], BF16)
        BANDP = T("BANDP", [P, 4, P], BF16)
        BANDF = T("BANDF", [P, 4, P], BF16)
        BANDSC = T("BANDSC", [64, 4, 64], BF16)
        BANDSP = T("BANDSP", [64, 4, 64], BF16)
        MATT = T("MATT", [P, P], BF16)
        MSATT = T("MSATT", [64, 64], BF16)
        RMASK = T("RMASK", [P, 512], BF16)
        RMASKS = T("RMASKS", [P, 64], BF16)
        CACHE = T("CACHE", [64, 512], BF16)

        X = [T("X%d" % i, [P, D]) for i in range(2)]
        XN = [T("XN%d" % i, [P, D], BF16) for i in range(2)]
        SMA = T("SMA", [P, 8])
        XNT = T("XNT", [P, KD_, 512], BF16)
        UTOK = T("UTOK", [P, 4, 512], BF16)
        UPREV = T("UPREV", [P, 512], BF16)
        TH = [T("TH%d" % i, [P, 512]) for i in range(2)]
        GD = [T("GD%d" % i, [P, 512]) for i in range(2)]
        BA = [T("BA%d" % i, [P, 512]) for i in range(2)]
        QE = T("QE", [P, 4, 512], BF16)
        KE = T("KE", [P, 4, 512], BF16)
        KDT = T("KDT", [P, 4, 512], BF16)
        DEC = T("DEC", [P, 4, 8])
        EMID = T("EMID", [P, 4, 8])
        SGL = [T("SGL%d" % i, [P, 512], BF16) for i in range(2)]
        PTBL = [T("PTBL%d" % i, [P, 512], BF16) for i in range(2)]
        MIXT = T("MIXT", [P, 8, 512], BF16)
        VTOK = [T("VTOK%d" % i, [P, 512], BF16) for i in range(2)]
        SH = [T("SH%d" % i, [P, 512], BF16) for i in range(2)]
        KD = [T("KD%d" % i, [P, 512], BF16) for i in range(2)]
        ATT = [T("ATT%d" % i, [P, 4, P], BF16) for i in range(2)]
        OG = T("OG", [P, 512], BF16)
        SMH = T("SMH", [P, 12])
        XR = [T("XR%d" % i, [P, D]) for i in range(2)]
        T1 = T("T1", [P, D])
        X1B = T("X1B", [P, D], BF16)
        X1T = [T("X1T%d" % i, [P, KD_, P], BF16) for i in range(2)]
        S32 = [T("S32_0", [P, 4, P])[:], XR[1][:, 0:512].rearrange("p (h v) -> p h v", h=4),
               XR[1][:, 512:1024].rearrange("p (h v) -> p h v", h=4)]
        SB = [T("SB_0", [P, 4, P], BF16)[:], X1T[1][:, 0:4, :], X1T[1][:, 4:8, :]]

        def snm(sid, h):
            return "S32_0/h%d" % h if sid == 0 else "XR1/s%dh%d" % (sid, h)

        def sbnm(sid, h):
            return "SB_0/h%d" % h if sid == 0 else "X1T1/s%dh%d" % (sid, h)
        SMY = T("SMY", [P, 8])
        PIN = T("PIN", [P, 4, DPLE])
        PB = [T("PB%d" % i, [P, DPLE], BF16) for i in range(2)]
        PTT = [T("PTT%d" % i, [P, 2, P], BF16) for i in range(2)]
        T2 = T("T2", [P, D])

        NB = 8
        BK = [PS("BK%d" % i, [P, 512]) for i in range(NB)]
        BKB = [b[:].bitcast(BF16) for b in BK]

        def bn(i):
            return "PB%d" % i

        s.op("pool", lambda: G_.memset(IDB[:], 1.0), writes=["IDB"])
        s.op("pool", lambda: G_.affine_select(out=IDB[:], in_=IDB[:], pattern=[[-1, P]], compare_op=ALU.is_equal,
                                              fill=0.0, base=0, channel_multiplier=1), reads=["IDB"], writes=["IDB"])
        s.op("pool", lambda: G_.memset(MH[:], -0.5), writes=["MH"])
        s.op("pool", lambda: G_.memset(S32[0], 0.0), writes=["S32_0"])
        s.op("pool", lambda: G_.memset(UPREV[:], 0.0), writes=["UPREV"])
        for nm, t_, src in (("RMASK", RMASK, rmask), ("RMASKS", RMASKS, rmask_s), ("MATT", MATT, mask_att), ("MSATT", MSATT, mask_satt)):
            s.dma(nm, t_[:], src, writes=[nm], eng="pool", nowait=True)
        SEGN = ["WIN/u", "WIN/gp", "WIN/q", "WIN/f", "WIN/v", "WIN/gh"]
        w_in_v = w_in.rearrange("(k p) n -> p k n", p=P)
        WLOAD = []
        tarr = 0.0
        for seg in (3, 4, 0, 2, 1, 5):
            for k0 in (0, 4):
                s.dma("WIN%d" % seg, WIN[:, k0:k0 + 4, seg * 512:(seg + 1) * 512], w_in_v[:, k0:k0 + 4, seg * 512:(seg + 1) * 512],
                      writes=[SEGN[seg]], eng="pool", nowait=True)
            tarr += 22000.0
            WLOAD.append((SEGN[seg], tarr))
        for nm, t_, src in (("BANDC", BANDC, band_cur), ("BANDP", BANDP, band_prev), ("BANDF", BANDF, band_first),
                            ("BANDSC", BANDSC, band_scur), ("BANDSP", BANDSP, band_sprev), ("CACHE", CACHE, cache_smp)):
            s.dma(nm, t_[:], src, writes=[nm], eng="pool", nowait=True)
        for g in range(4):
            s.dma("WPOOL", WPOOL[:, g, :], w_pool[g], writes=["WPOOL"], eng="pool", nowait=True)
        for nm, t_, src, nk in (("WOUT", WOUT, w_out, KD_), ("WG", WG, w_gate, KD_), ("WPLE", WPLE, w_ple, 2)):
            for k in range(nk):
                s.dma(nm, t_[:, k, :], src[k * P:(k + 1) * P, :], writes=[nm], eng="pool", nowait=True)
            tarr += 5000.0 * nk
            WLOAD.append((nm, tarr))
        s.dma("NPRE", NPRE[:], norm_pre.partition_broadcast(P), writes=["NPRE"])
        s.dma("GPOST", GPOST[:], norm_post.partition_broadcast(P), writes=["GPOST"])
        s.dma("PSC", PSC[:], pool_scale.rearrange("(g c) -> c g", c=P), writes=["PSC"], allow_slow_non_contiguous=True)
        s.dma("HGN", HGN[:], hg_norm.rearrange("(c o) -> c o", o=1), writes=["HGN"])
        s.dma("LBT", LBT[:, 0:4], lb_logits[0].rearrange("(h k) -> k h", k=P), writes=["LBT"], allow_slow_non_contiguous=True)
        s.dma("LBT", LBT[:, 4:8], lb_logits[1].rearrange("(h k) -> k h", k=P), writes=["LBT"], allow_slow_non_contiguous=True)
        s.op("dve", lambda: V.tensor_tensor(out=SMA[:, 0:4], in0=LBT[:, 0:4], in1=LBT[:, 4:8], op=ALU.subtract),
             reads=["LBT"], writes=["SMA"])
        s.op("act", lambda: A_.activation(out=SMA[:, 4:8], in_=SMA[:, 0:4], func=AF.Tanh, scale=0.5),
             reads=["SMA"], writes=["SMA"])
        s.op("dve", lambda: V.tensor_scalar(out=OML[:], in0=SMA[:, 4:8], scalar1=-0.5, scalar2=0.5, op0=ALU.mult, op1=ALU.add),
             reads=["SMA"], writes=["OML"])
        s.op("dve", lambda: V.tensor_scalar(out=HSC[:], in0=OML[:], scalar1=0.5, scalar2=None, op0=ALU.mult),
             reads=["OML"], writes=["HSC"])
        s.op("dve", lambda: V.tensor_scalar(out=HBI[:], in0=OML[:], scalar1=-0.5, scalar2=1.0, op0=ALU.mult, op1=ALU.add),
             reads=["OML"], writes=["HBI"])
        s.op("dve", lambda: V.reciprocal(out=IOML[:], in_=OML[:]), reads=["OML"], writes=["IOML"])
        pre_sts, main_sts, smp_sts = [], [], []
        for i in range(npre // 512):
            pre_sts.append(dict(kind="pre", x=x_pre[i * 512:(i + 1) * 512, :], p=None, y=None, ntok=512, TW=P, BL=64,
                                blocks=[[(0, 64, 0), (64, 64, 0)]] * 4, first=False, sample=False,
                                want_u=(3 if i == npre // 512 - 1 else None), udst=None))
        for i in range(nmain // 512):
            last = i == nmain // 512 - 1
            main_sts.append(dict(kind="main", x=x_main[i * 512:(i + 1) * 512, :], p=p_main[i * 512:(i + 1) * 512, :],
                                 y=y_main[i * 512:(i + 1) * 512, :], ntok=512, TW=P, BL=64,
                                 blocks=[[(0, 64, 0), (64, 64, 0)]] * 4, first=(i == 0), sample=False,
                                 want_u=(3 if last else None), udst=(pool_p if last else None), final_state=last))
        if do_sample:
            smp_sts.append(dict(kind="main", x=x_smp, p=p_smp, y=y_smp, ntok=64, TW=64, BL=16,
                                blocks=[[(0, 16, 1), (32, 16, 2)]], first=False, sample=True, want_u=0, udst=pool_s,
                                final_state=False))
        cut = min(3, len(pre_sts))
        STs = pre_sts[:cut] + smp_sts + pre_sts[cut:] + main_sts
        NS = len(STs)
        for i, st in enumerate(pre_sts):
            st["prei"] = i
        XNTB, XNTN = [XNT, MIXT], ["XNT", "MIXT"]
        KDTB, KDTN = [KDT, QE], ["KDT", "QE"]
        DECB, DECN = [DEC, EMID], ["DEC", "EMID"]
        for n, st in enumerate(STs):
            st["n"] = n
            st["ntile"] = st["ntok"] // st["TW"]
            st["pre"] = st["kind"] == "pre"
            st["bs"] = (st["prei"] % 2) if st["pre"] else 0

        def prev_same(n):
            for m in range(n - 1, -1, -1):
                if STs[m]["bs"] == STs[n]["bs"]:
                    return m
            return None

        def prev_alt(n):
            for m in range(n - 1, -1, -1):
                if STs[m]["bs"] == 1:
                    return m
            return None

        def prev_mainkind(n):
            for m in range(n - 1, -1, -1):
                if not STs[m]["pre"]:
                    return m
            return None

        def tile_back2(n, j, mainonly=False):
            jj = j - 2
            m = n
            while True:
                if jj >= 0:
                    return m, jj
                m -= 1
                while m >= 0 and mainonly and STs[m]["pre"]:
                    m -= 1
                if m < 0:
                    return None
                nt = STs[m]["ntile"]
                jj = nt - 1 if (nt - 1) % 2 == j % 2 else nt - 2

        def ev(*a):
            return "_".join(str(x) for x in a)

        def fm_block(bank, st, col0):
            ntok = st["ntok"]
            return [OP("pe", (lambda k=k: PE_.matmul(BK[bank][:, :ntok], lhsT=WIN[:, k, col0:col0 + P], rhs=XNTB[st["bs"]][:, k, :ntok],
                                                     start=(k == 0), stop=(k == KD_ - 1))),
                       reads=[SEGN[col0 // 512], XNTN[st["bs"]]], writes=[bn(bank)], inc=(k == KD_ - 1), cost=c_pe(ntok)) for k in range(KD_)]

        def tm_block(bank, st, j, col0):
            TW = st["TW"]
            c0 = j * TW
            return [OP("pe", (lambda k=k: PE_.matmul(BK[bank][:TW, :], lhsT=XNTB[st["bs"]][:, k, c0:c0 + TW], rhs=WIN[:, k, col0:col0 + 512],
                                                     start=(k == 0), stop=(k == KD_ - 1))),
                       reads=[SEGN[col0 // 512], XNTN[st["bs"]]], writes=[bn(bank)], inc=(k == KD_ - 1), cost=c_pe(512)) for k in range(KD_)]

        def lane_A(lane):
            for st in STs:
                n, TW, ntile = st["n"], st["TW"], st["ntile"]
                m = prev_same(n)
                if m is not None:
                    yield ("wait", [ev("xntfree", m), ev("POOLall", m), ev("U", m)])
                if st["pre"] and st["bs"] == 1 and prev_mainkind(n) is not None:
                    yield ("wait", [ev("Ymm", prev_mainkind(n)), ev("Hdone", prev_mainkind(n))])
                for j in range(lane, ntile, 2):
                    c0 = j * TW
                    xn, xnn = XN[lane], "XN%d" % lane
                    a0 = 4 * lane
                    ss, rt, rx = SMA[:TW, a0:a0 + 1], SMA[:TW, a0 + 1:a0 + 2], SMA[:TW, a0 + 2:a0 + 3]
                    sn = "SMA/l%d" % lane
                    xx, xxn = X[lane], "X%d" % lane
                    yield DMA("X%d" % lane, xx[:TW], st["x"][c0:c0 + TW, :], writes=[xxn])
                    yield OP("act", lambda: A_.activation(out=xn[:TW], in_=xx[:TW], func=AF.Square, accum_out=ss),
                             reads=[xxn], writes=[xnn, sn + "s"], cost=c_act(D))
                    yield OP("dve", lambda: V.tensor_scalar(out=rt, in0=ss, scalar1=1.0 / D, scalar2=EPS, op0=ALU.mult, op1=ALU.add),
                             reads=[sn + "s"], writes=[sn + "r"], cost=200)
                    yield OP("pool", lambda: G_.tensor_tensor(out=rx, in0=rt, in1=MH[:TW, 0:1], op=ALU.pow),
                             reads=[sn + "r", "MH"], writes=[sn + "x"], cost=500)
                    yield OP("dve", lambda: V.scalar_tensor_tensor(out=xn[:TW], in0=xx[:TW], scalar=rx, in1=NPRE[:TW],
                                                                   op0=ALU.mult, op1=ALU.mult),
                             reads=[xxn, sn + "x", "NPRE"], writes=[xnn], cost=c_dve(D))
                    bank, = yield ("alloc", 1)
                    ops = [OP("pe", (lambda k=k: PE_.transpose(out=BKB[bank][:, k * TW:(k + 1) * TW], in_=xn[:TW, k * P:(k + 1) * P],
                                                               identity=IDB[:TW, :TW])),
                              reads=[xnn, "IDB"], writes=[bn(bank)], inc=(k == KD_ - 1), cost=c_pe(TW)) for k in range(KD_)]
                    ops.append(OP("act", lambda: A_.copy(out=XNTB[st["bs"]][:, :, c0:c0 + TW],
                                                         in_=BKB[bank][:, 0:KD_ * TW].rearrange("p (k t) -> p k t", k=KD_)),
                                  reads=[bn(bank)], writes=[XNTN[st["bs"]]], cost=c_act(KD_ * TW)))
                    yield ("group", ops)
                    yield ("free", [bank])
                    yield ("set", ev("A", n, j))

        def v3(t, st):
            return t[:, :st["ntok"]].rearrange("p (b c) -> p b c", c=st["BL"])

        def lane_F(lane):
            th, gd, ba = TH[lane], GD[lane], BA[lane]
            thn, gdn, ban = "TH%d" % lane, "GD%d" % lane, "BA%d" % lane
            e2, e2n = gd, gdn
            for st in STs:
                n, ntok, BL, pre = st["n"], st["ntok"], st["BL"], st["pre"]
                nblk = ntok // BL
                mid = BL // 2 - 1
                rm, rmn = (RMASKS, "RMASKS") if st["sample"] else (RMASK, "RMASK")
                wl = [ev("A", n, j) for j in range(st["ntile"])]
                if prev_same(n) is not None:
                    wl.append(ev("qkdfree", prev_same(n)))
                if not pre and prev_alt(n) is not None:
                    wl += [ev("qkdfree", prev_alt(n)), ev("xntfree", prev_alt(n)), ev("U", prev_alt(n))]
                if pre and st["bs"] == 1 and prev_mainkind(n) is not None:
                    wl += [ev("Hdone", prev_mainkind(n)), ev("Ymm", prev_mainkind(n))]
                yield ("wait", wl)
                kdt, kdtp, dec, decp = KDTB[st["bs"]], KDTN[st["bs"]], DECB[st["bs"]], DECN[st["bs"]]
                for h in range(lane, 4, 2):
                    bank, = yield ("alloc", 1)
                    ops = fm_block(bank, st, C_F + h * P)
                    ops.append(OP("act", lambda: A_.activation(out=th[:, :ntok], in_=BK[bank][:, :ntok], func=AF.Tanh, scale=0.5),
                                  reads=[bn(bank)], writes=[thn], cost=c_act(ntok)))
                    yield ("group", ops)
                    yield ("free", [bank])
                    yield OP("act", lambda: A_.activation(out=gd[:, :ntok], in_=th[:, :ntok], func=AF.Ln,
                                                          scale=HSC[:, h:h + 1], bias=HBI[:, h:h + 1]),
                             reads=[thn, "HSC", "HBI"], writes=[gdn], cost=c_act(ntok))
                    yield OP("dve", lambda: V.tensor_tensor_scan(out=ba[:, :ntok], data0=rm[:, :ntok], data1=gd[:, :ntok],
                                                                 initial=0.0, op0=ALU.mult, op1=ALU.add),
                             reads=[gdn, rmn], writes=[ban], cost=2.1 * ntok + 160)
                    yield OP("dve", lambda: V.tensor_tensor(out=v3(gd, st), in0=v3(ba, st)[:, :, BL - 1:BL].to_broadcast([P, nblk, BL]),
                                                            in1=v3(ba, st), op=ALU.subtract),
                             reads=[ban, gdn], writes=[gdn], cost=c_dve(ntok))
                    yield OP("act", lambda: A_.activation(out=dec[:, h, 0:nblk], in_=v3(ba, st)[:, :, BL - 1], func=AF.Exp),
                             reads=[ban], writes=[decp + "/%d" % h], cost=300)
                    yield OP("act", lambda: A_.activation(out=gd[:, :ntok], in_=gd[:, :ntok], func=AF.Exp),
                             reads=[gdn], writes=[gdn], cost=c_act(ntok))
                    yield OP("dve", lambda: V.scalar_tensor_tensor(out=kdt[:, h, :ntok], in0=th[:, :ntok], scalar=1.0,
                                                                   in1=gd[:, :ntok], op0=ALU.subtract, op1=ALU.mult),
                             reads=[thn, gdn], writes=[kdtp + "/%d" % h], cost=c_dve(ntok))
                    if not pre:
                        yield OP("act", lambda: A_.activation(out=EMID[:, h, 0:nblk], in_=v3(ba, st)[:, :, mid], func=AF.Exp),
                                 reads=[ban], writes=["EMID/%d" % h], cost=300)
                        yield OP("dve", lambda: V.tensor_tensor(out=v3(gd, st), in0=v3(ba, st),
                                                                in1=v3(ba, st)[:, :, mid:mid + 1].to_broadcast([P, nblk, BL]),
                                                                op=ALU.subtract), reads=[ban, gdn], writes=[gdn], cost=c_dve(ntok))
                        yield OP("act", lambda: A_.activation(out=ba[:, :ntok], in_=gd[:, :ntok], func=AF.Exp, scale=-1.0),
                                 reads=[gdn, ban], writes=[ban], cost=c_act(ntok))
                        yield OP("act", lambda: A_.activation(out=gd[:, :ntok], in_=gd[:, :ntok], func=AF.Exp),
                                 reads=[gdn], writes=[gdn], cost=c_act(ntok))
                        yield OP("dve", lambda: V.scalar_tensor_tensor(out=KE[:, h, :ntok], in0=th[:, :ntok], scalar=1.0,
                                                                       in1=ba[:, :ntok], op0=ALU.subtract, op1=ALU.mult),
                                 reads=[thn, ban], writes=["KE/%d" % h], cost=c_dve(ntok))
                        bank, = yield ("alloc", 1)
                        ops = fm_block(bank, st, C_Q + h * P)
                        ops.append(OP("dve", lambda: V.scalar_tensor_tensor(out=QE[:, h, :ntok], in0=BK[bank][:, :ntok],
                                                                            scalar=OML[:, h:h + 1], in1=gd[:, :ntok],
                                                                            op0=ALU.mult, op1=ALU.mult),
                                      reads=[bn(bank), "OML", gdn], writes=["QE/%d" % h], cost=c_dve(ntok, True)))
                        yield ("group", ops)
                        yield ("free", [bank])
                    yield ("set", ev("F", n, h))

        def lane_U():
            for st in STs:
                n, TW, ntile, pre = st["n"], st["TW"], st["ntile"], st["pre"]
                yield ("wait", [ev("A", n, j) for j in range(ntile)] + ([ev("POOLall", n - 1)] if n > 0 else []))
                for j in range(ntile):
                    if pre and st["want_u"] != j:
                        continue
                    bank, = yield ("alloc", 1)
                    ops = tm_block(bank, st, j, C_U)
                    ops.append(OP("act", lambda: A_.copy(out=UTOK[:TW, j, :], in_=BK[bank][:TW, :]),
                                  reads=[bn(bank)], writes=["UTOK"], cost=c_act(512)))
                    if st["want_u"] == j and st["udst"] is not None:
                        yield ("wait", [ev("Zall", n - 1)] if n > 0 else [])
                        ops.append(OP("act", lambda: A_.copy(out=T2[:TW, 0:512], in_=BK[bank][:TW, :]),
                                      reads=[bn(bank)], writes=["T2"], cost=c_act(512)))
                        ops.append(DMA("UD%d" % n, st["udst"], T2[:TW, 0:512], reads=["T2"]))
                    yield ("group", ops)
                    yield ("free", [bank])
                if pre and st["want_u"] is not None:
                    yield OP("pool", lambda: G_.tensor_copy(out=UPREV[:], in_=UTOK[:, 3, :]), reads=["UTOK"], writes=["UPREV"],
                             cost=c_pool(512))
                yield ("set", ev("U", n))

        def lane_POOL(lane):
            sg, sgn, ptb, ptn = SGL[lane], "SGL%d" % lane, PTBL[lane], "PTBL%d" % lane
            for st in STs:
                n, TW, ntile, ntok = st["n"], st["TW"], st["ntile"], st["ntok"]
                if st["pre"]:
                    yield ("set", ev("POOL", n, lane))
                    continue
                yield ("wait", [ev("U", n)] + ([ev("Ymm", prev_mainkind(n))] if prev_mainkind(n) is not None else []) + ([ev("xntfree", prev_alt(n)), ev("U", prev_alt(n))] if prev_alt(n) is not None else []))
                for g in range(lane, 4, 2):
                    bank, = yield ("alloc", 1)
                    ops = fm_block(bank, st, C_GP + g * P)
                    ops.append(OP("act", lambda: A_.activation(out=sg[:, :ntok], in_=BK[bank][:, :ntok], func=AF.Silu),
                                  reads=[bn(bank)], writes=[sgn], cost=c_act(ntok)))
                    yield ("group", ops)
                    yield ("free", [bank])
                    bank, = yield ("alloc", 1)
                    ops = []
                    for j in range(ntile):
                        c0 = j * TW
                        if st["sample"]:
                            bc_, bcn = BANDSC, "BANDSC"
                        elif st["first"] and j == 0:
                            bc_, bcn = BANDF, "BANDF"
                        else:
                            bc_, bcn = BANDC, "BANDC"
                        ops.append(OP("pe", (lambda j=j, c0=c0, bc_=bc_: PE_.matmul(BK[bank][:, c0:c0 + TW], lhsT=UTOK[:TW, j, g * P:(g + 1) * P],
                                                                                    rhs=bc_[:TW, g, :TW], start=True, stop=False)),
                                      reads=["UTOK", bcn], writes=[bn(bank)], inc=False, cost=c_pe(TW)))
                        if st["sample"]:
                            ops.append(OP("pe", (lambda c0=c0: PE_.matmul(BK[bank][:, c0:c0 + TW], lhsT=CACHE[0:64, g * P:(g + 1) * P],
                                                                          rhs=BANDSP[0:64, g, :TW], start=False, stop=True)),
                                          reads=["CACHE", "BANDSP"], writes=[bn(bank)], inc=(j == ntile - 1), cost=c_pe(TW)))
                        else:
                            if j == 0:
                                pv, pvn = UPREV[64:128, g * P:(g + 1) * P], "UPREV"
                            else:
                                pv, pvn = UTOK[64:128, j - 1, g * P:(g + 1) * P], "UTOK"
                            ops.append(OP("pe", (lambda c0=c0, pv=pv: PE_.matmul(BK[bank][:, c0:c0 + TW], lhsT=pv, rhs=BANDP[64:128, g, :TW],
                                                                                 start=False, stop=True)),
                                          reads=[pvn, "BANDP"], writes=[bn(bank)], inc=(j == ntile - 1), cost=c_pe(TW)))
                    ops.append(OP("act", lambda: A_.copy(out=ptb[:, :ntok], in_=BK[bank][:, :ntok]), reads=[bn(bank)], writes=[ptn],
                                  cost=c_act(ntok)))
                    yield ("group", ops)
                    yield ("free", [bank])
                    bank, = yield ("alloc", 1)
                    yield ("group", [
                        OP("pe", lambda: PE_.matmul(BK[bank][:, :ntok], lhsT=WPOOL[:, g, :], rhs=ptb[:, :ntok], start=True, stop=True),
                           reads=["WPOOL", ptn], writes=[bn(bank)], cost=c_pe(ntok)),
                        OP("dve", lambda: V.scalar_tensor_tensor(out=MIXT[:, g, :ntok], in0=BK[bank][:, :ntok], scalar=PSC[:, g:g + 1],
                                                                 in1=sg[:, :ntok], op0=ALU.mult, op1=ALU.mult),
                           reads=[bn(bank), "PSC", sgn], writes=["MIXT/p%d" % g], cost=c_dve(ntok, True)),
                    ])
                    yield ("free", [bank])
                yield ("set", ev("POOL", n, lane))

        def lane_POOLJOIN():
            for st in STs:
                n = st["n"]
                yield ("wait", [ev("POOL", n, 0), ev("POOL", n, 1)])
                if not st["pre"] and not st["sample"]:
                    yield OP("pool", lambda: G_.tensor_copy(out=UPREV[:], in_=UTOK[:, 3, :]), reads=["UTOK"], writes=["UPREV"],
                             cost=c_pool(512))
                yield ("set", ev("POOLall", n))

        def lane_H1():
            for st in STs:
                n, TW, ntile, ntok, BL, pre = st["n"], st["TW"], st["ntile"], st["ntok"], st["BL"], st["pre"]
                matt, mattn = (MSATT, "MSATT") if st["sample"] else (MATT, "MATT")
                kdt, kdtp = KDTB[st["bs"]], KDTN[st["bs"]]
                yield ("wait", [ev("F", n, h) for h in range(4)])
                for j in range(ntile):
                    c0 = j * TW
                    par = j % 2
                    vt, vtn, sh, shn, kd, kdn, att, attn = VTOK[par], "VTOK%d" % par, SH[par], "SH%d" % par, KD[par], "KD%d" % par, ATT[par], "ATT%d" % par
                    tb = tile_back2(n, j)
                    if tb is not None:
                        yield ("wait", [ev("H", tb[0], tb[1])])
                    bank, = yield ("alloc", 1)
                    ops = tm_block(bank, st, j, C_V)
                    ops.append(OP("act", lambda: A_.mul(out=vt[:TW, :], in_=BK[bank][:TW, :], mul=-0.5), reads=[bn(bank)],
                                  writes=[vtn], cost=c_act(512)))
                    yield ("group", ops)
                    yield ("free", [bank])
                    if not pre:
                        bank, = yield ("alloc", 1)
                        ops = tm_block(bank, st, j, C_GH)
                        ops.append(OP("act", lambda: A_.activation(out=sh[:TW, :], in_=BK[bank][:TW, :], func=AF.Silu),
                                      reads=[bn(bank)], writes=[shn], cost=c_act(512)))
                        yield ("group", ops)
                        yield ("free", [bank])
                    if j == ntile - 1:
                        yield ("set", ev("xntfree", n))
                    bank, = yield ("alloc", 1)
                    ops = [OP("pe", (lambda h=h: PE_.transpose(out=BKB[bank][:TW, h * P:(h + 1) * P], in_=kdt[:, h, c0:c0 + TW], identity=IDB[:, :])),
                              reads=[kdtp + "/%d" % h, "IDB"], writes=[bn(bank)], inc=(h == 3), cost=c_pe(P)) for h in range(4)]
                    ops.append(OP("act", lambda: A_.copy(out=kd[:TW, :], in_=BKB[bank][:TW, 0:512]), reads=[bn(bank)], writes=[kdn],
                                  cost=c_act(512)))
                    yield ("group", ops)
                    yield ("free", [bank])
                    if not pre:
                        bank, = yield ("alloc", 1)
                        ops = [OP("pe", (lambda h=h: PE_.matmul(BK[bank][:TW, h * TW:(h + 1) * TW], lhsT=KE[:, h, c0:c0 + TW],
                                                                rhs=QE[:, h, c0:c0 + TW], start=True, stop=True)),
                                  reads=["KE/%d" % h, "QE/%d" % h], writes=[bn(bank)], inc=(h == 3), cost=c_pe(TW)) for h in range(4)]
                        ops.append(OP("dve", lambda: V.tensor_tensor(out=att[:TW, :, :TW],
                                                                     in0=BK[bank][:TW, 0:4 * TW].rearrange("p (h t) -> p h t", h=4),
                                                                     in1=matt[:TW, :TW].unsqueeze(1).to_broadcast([TW, 4, TW]),
                                                                     op=ALU.mult), reads=[bn(bank), mattn], writes=[attn],
                                      cost=c_dve(4 * TW, True)))
                        yield ("group", ops)
                        yield ("free", [bank])
                    yield ("set", ev("H1", n, j))

        def lane_H2():
            for st in STs:
                n, TW, ntile, ntok, BL, pre = st["n"], st["TW"], st["ntile"], st["ntok"], st["BL"], st["pre"]
                dec, decp = DECB[st["bs"]], DECN[st["bs"]]
                if st["sample"]:
                    if prev_mainkind(n) is not None:
                        yield ("wait", [ev("Zall", prev_mainkind(n)), ev("Hdone", n - 1)])
                    for q in range(2):
                        sid = 1 + q
                        yield DMA("SIN%d" % q, S32[sid], state_smp[q].rearrange("h k v -> k h v"),
                                  writes=[snm(sid, h) for h in range(4)])
                        for h in range(4):
                            yield OP("dve", (lambda h=h, sid=sid: V.tensor_scalar(out=S32[sid][:, h, :], in0=S32[sid][:, h, :],
                                                                                  scalar1=IOML[:, h:h + 1], scalar2=None, op0=ALU.mult)),
                                     reads=[snm(sid, h), "IOML"], writes=[snm(sid, h)], cost=c_dve(P))
                for j in range(ntile):
                    c0 = j * TW
                    par = j % 2
                    vt, vtn, sh, shn, kd, kdn, att, attn = VTOK[par], "VTOK%d" % par, SH[par], "SH%d" % par, KD[par], "KD%d" % par, ATT[par], "ATT%d" % par
                    wl = [ev("H1", n, j)]
                    pm = prev_mainkind(n)
                    if not pre and pm is not None and j < STs[pm]["ntile"]:
                        wl.append(ev("Ymmt", pm, j))
                    yield ("wait", wl)
                    bo, bu = yield ("alloc", 2)
                    for bi, (r0, ln, sid) in enumerate(st["blocks"][j]):
                        blk = (c0 + r0) // BL
                        ops = [OP("pe", (lambda h=h: PE_.matmul(BK[bu][:, h * P:(h + 1) * P], lhsT=kd[r0:r0 + ln, h * P:(h + 1) * P],
                                                                rhs=vt[r0:r0 + ln, h * P:(h + 1) * P], start=True, stop=True)),
                                  reads=[kdn, vtn], writes=[bn(bu)], inc=(h == 3), cost=c_pe(P)) for h in range(4)]
                        yield ("group", ops)
                        if not pre:
                            for h in range(4):
                                if h < 2:
                                    yield OP("pool", (lambda h=h: G_.tensor_tensor(out=SB[sid][:, h, :], in0=S32[sid][:, h, :],
                                                                                   in1=EMID[:, h, blk:blk + 1].to_broadcast([P, P]),
                                                                                   op=ALU.mult)),
                                             reads=[snm(sid, h), "EMID/%d" % h], writes=[sbnm(sid, h)], cost=c_pool(P))
                                else:
                                    yield OP("act", (lambda h=h: A_.activation(out=SB[sid][:, h, :], in_=S32[sid][:, h, :], func=AF.Copy,
                                                                               scale=EMID[:, h, blk:blk + 1])),
                                             reads=[snm(sid, h), "EMID/%d" % h], writes=[sbnm(sid, h)], cost=c_act(P))
                            ops = []
                            for h in range(4):
                                ops.append(OP("pe", (lambda h=h: PE_.matmul(BK[bo][r0:r0 + ln, h * P:(h + 1) * P], lhsT=att[:TW, h, r0:r0 + ln],
                                                                            rhs=vt[:TW, h * P:(h + 1) * P], start=True, stop=False)),
                                              reads=[attn, vtn], writes=[bn(bo)], inc=False, cost=c_pe(P)))
                                ops.append(OP("pe", (lambda h=h: PE_.matmul(BK[bo][r0:r0 + ln, h * P:(h + 1) * P], lhsT=QE[:, h, c0 + r0:c0 + r0 + ln],
                                                                            rhs=SB[sid][:, h, :], start=False, stop=True)),
                                              reads=["QE/%d" % h, sbnm(sid, h)], writes=[bn(bo)], inc=(h == 3), cost=c_pe(P)))
                            yield ("group", ops)
                        for h in range(4):
                            yield OP("dve", (lambda h=h: V.scalar_tensor_tensor(out=S32[sid][:, h, :], in0=S32[sid][:, h, :],
                                                                                scalar=dec[:, h, blk:blk + 1], in1=BK[bu][:, h * P:(h + 1) * P],
                                                                                op0=ALU.mult, op1=ALU.add)),
                                     reads=[snm(sid, h), decp + "/%d" % h, bn(bu)], writes=[snm(sid, h)], cost=c_dve(P, True))
                    if j == ntile - 1:
                        yield ("set", ev("qkdfree", n))
                    if pre:
                        yield ("free", [bo, bu])
                        yield ("set", ev("H", n, j))
                        continue
                    for h in range(4):
                        yield OP("act", (lambda h=h: A_.activation(out=OG[:TW, h * P:(h + 1) * P], in_=BK[bo][:TW, h * P:(h + 1) * P],
                                                                   func=AF.Square, accum_out=SMH[:TW, h:h + 1])),
                                 reads=[bn(bo)], writes=["OG/%d" % h, "SMH/s%d" % h], cost=c_act(P))
                    yield OP("dve", lambda: V.tensor_scalar(out=SMH[:TW, 4:8], in0=SMH[:TW, 0:4], scalar1=1.0 / P, scalar2=EPS,
                                                            op0=ALU.mult, op1=ALU.add), reads=["SMH/s%d" % h for h in range(4)],
                             writes=["SMH/r"], cost=200)
                    yield OP("pool", lambda: G_.tensor_tensor(out=SMH[:TW, 8:12], in0=SMH[:TW, 4:8], in1=MH[:TW, 0:4], op=ALU.pow),
                             reads=["SMH/r", "MH"], writes=["SMH/o"], cost=600)
                    for h in range(4):
                        yield OP("dve", (lambda h=h: V.scalar_tensor_tensor(out=OG[:TW, h * P:(h + 1) * P], in0=BK[bo][:TW, h * P:(h + 1) * P],
                                                                            scalar=SMH[:TW, 8 + h:9 + h], in1=sh[:TW, h * P:(h + 1) * P],
                                                                            op0=ALU.mult, op1=ALU.mult)),
                                 reads=[bn(bo), "SMH/o", shn], writes=["OG/%d" % h], cost=c_dve(P, True))
                    yield ("free", [bo, bu])
                    bank, = yield ("alloc", 1)
                    ops = [OP("pe", (lambda h=h: PE_.transpose(out=BKB[bank][:, h * TW:(h + 1) * TW], in_=OG[:TW, h * P:(h + 1) * P],
                                                               identity=IDB[:TW, :TW])),
                              reads=["OG/%d" % h, "IDB"], writes=[bn(bank)], inc=(h == 3), cost=c_pe(TW)) for h in range(4)]
                    ops.append(OP("act", lambda: A_.copy(out=MIXT[:, 4:8, c0:c0 + TW],
                                                         in_=BKB[bank][:, 0:4 * TW].rearrange("p (h t) -> p h t", h=4)),
                                  reads=[bn(bank)], writes=["MIXT/h%d" % j], cost=c_act(4 * TW)))
                    yield ("group", ops)
                    yield ("free", [bank])
                    yield ("set", ev("H", n, j))
                yield ("set", ev("Hdone", n))

        folded = [False]
        smp_n = [next((st["n"] for st in STs if st["sample"]), None)]

        def lane_Y():
            for st in STs:
                n, TW, ntile, pre = st["n"], st["TW"], st["ntile"], st["pre"]
                if pre:
                    yield ("set", ev("Ymm", n))
                    continue
                if not folded[0]:
                    folded[0] = True
                    yield OP("dve", lambda: V.tensor_scalar(out=WOUT[:, 4:8, :], in0=WOUT[:, 4:8, :], scalar1=HGN[:, 0:1], scalar2=None,
                                                            op0=ALU.mult), reads=["WOUT", "HGN"], writes=["WOUT"], cost=c_dve(4096, fast=4))
                    yield OP("dve", lambda: V.tensor_scalar(out=WPLE[:], in0=WPLE[:], scalar1=0.5, scalar2=None, op0=ALU.mult),
                             reads=["WPLE"], writes=["WPLE"], cost=c_dve(2048, fast=4))
                yield ("wait", [ev("POOLall", n)])
                for j in range(ntile):
                    c0 = j * TW
                    par = j % 2
                    xr, xrn = XR[par], "XR%d" % par
                    x1t, x1tn = X1T[par], "X1T%d" % par
                    yield ("wait", [ev("H", n, j)])
                    tb = tile_back2(n, j, mainonly=True)
                    if tb is not None:
                        yield ("wait", [ev("Z", tb[0], tb[1])])
                    if smp_n[0] is not None and smp_n[0] < n:
                        yield ("wait", [ev("stateout", smp_n[0])])
                    yield DMA("XR%d" % par, xr[:TW], st["x"][c0:c0 + TW, :], writes=[xrn])
                    b0, b1 = yield ("alloc", 2)
                    ops = []
                    for hf, bank in enumerate((b0, b1)):
                        for c in range(8):
                            ops.append(OP("pe", (lambda c=c, hf=hf, bank=bank: PE_.matmul(BK[bank][:TW, :], lhsT=MIXT[:, c, c0:c0 + TW],
                                                                                          rhs=WOUT[:, c, hf * 512:(hf + 1) * 512],
                                                                                          start=(c == 0), stop=(c == 7))),
                                          reads=["MIXT/p%d" % c if c < 4 else "MIXT/h%d" % j, "WOUT"], writes=[bn(bank)], inc=(c == 7),
                                          cost=c_pe(512)))
                    yield ("group", ops)
                    yield ("set", ev("Ymmt", n, j))
                    if j == ntile - 1:
                        yield ("set", ev("Ymm", n))
                    for hf, bank in enumerate((b0, b1)):
                        yield OP("act", (lambda hf=hf, bank=bank: A_.activation(out=X1B[:TW, hf * 512:(hf + 1) * 512], in_=BK[bank][:TW, :],
                                                                                func=AF.Square, accum_out=SMY[:TW, hf:hf + 1])),
                                 reads=[bn(bank)], writes=["X1B/%d" % hf, "SMY/s%d" % hf], cost=c_act(512))
                    yield OP("dve", lambda: V.tensor_tensor(out=SMY[:TW, 2:3], in0=SMY[:TW, 0:1], in1=SMY[:TW, 1:2], op=ALU.add),
                             reads=["SMY/s0", "SMY/s1"], writes=["SMY/a"], cost=200)
                    yield OP("dve", lambda: V.tensor_scalar(out=SMY[:TW, 3:4], in0=SMY[:TW, 2:3], scalar1=1.0 / D, scalar2=EPS,
                                                            op0=ALU.mult, op1=ALU.add), reads=["SMY/a"], writes=["SMY/b"], cost=200)
                    yield OP("pool", lambda: G_.tensor_tensor(out=SMY[:TW, 4:5], in0=SMY[:TW, 3:4], in1=MH[:TW, 0:1], op=ALU.pow),
                             reads=["SMY/b", "MH"], writes=["SMY/c"], cost=500)
                    for hf, bank in enumerate((b0, b1)):
                        yield OP("dve", (lambda hf=hf, bank=bank: V.scalar_tensor_tensor(out=T1[:TW, hf * 512:(hf + 1) * 512], in0=BK[bank][:TW, :],
                                                                                         scalar=SMY[:TW, 4:5], in1=GPOST[:TW, hf * 512:(hf + 1) * 512],
                                                                                         op0=ALU.mult, op1=ALU.mult)),
                                 reads=[bn(bank), "SMY/c", "GPOST"], writes=["T1/%d" % hf], cost=c_dve(512, True))
                    yield ("free", [b0, b1])
                    yield OP("dve", lambda: V.tensor_tensor(out=xr[:TW], in0=T1[:TW], in1=xr[:TW], op=ALU.add),
                             reads=["T1", xrn], writes=[xrn], cost=c_dve(D))
                    yield OP("act", lambda: A_.copy(out=X1B[:TW], in_=xr[:TW]), reads=[xrn], writes=["X1B"],
                             cost=c_act(D))
                    bank, = yield ("alloc", 1)
                    ops = [OP("pe", (lambda k=k: PE_.transpose(out=BKB[bank][:, k * TW:(k + 1) * TW], in_=X1B[:TW, k * P:(k + 1) * P],
                                                               identity=IDB[:TW, :TW])),
                              reads=["X1B", "IDB"], writes=[bn(bank)], inc=(k == KD_ - 1), cost=c_pe(TW)) for k in range(KD_)]
                    ops.append(OP("act", lambda: A_.copy(out=x1t[:, :, :TW], in_=BKB[bank][:, 0:KD_ * TW].rearrange("p (k t) -> p k t", k=KD_)),
                                  reads=[bn(bank)], writes=[x1tn], cost=c_act(KD_ * TW)))
                    yield ("group", ops)
                    yield ("free", [bank])
                    yield ("set", ev("Y", n, j))

        def lane_Z():
            for st in STs:
                n, TW, ntile, pre = st["n"], st["TW"], st["ntile"], st["pre"]
                if pre:
                    yield ("set", ev("Zall", n))
                    continue
                if st["sample"]:
                    yield DMA("PIN", PIN[:TW, 0, :], st["p"], writes=["PIN"])
                else:
                    yield DMA("PIN", PIN[:], st["p"].rearrange("(j p) d -> p j d", p=P), writes=["PIN"])
                for j in range(ntile):
                    c0 = j * TW
                    par = j % 2
                    xr, xrn = XR[par], "XR%d" % par
                    x1t, x1tn = X1T[par], "X1T%d" % par
                    pb, pbn, ptt, pttn = PB[par], "PBF%d" % par, PTT[par], "PTT%d" % par
                    yield OP("pool", lambda: G_.tensor_copy(out=pb[:TW], in_=PIN[:TW, j, :]), reads=["PIN"], writes=[pbn], cost=c_pool(DPLE))
                    bank, = yield ("alloc", 1)
                    ops = [OP("pe", (lambda k=k: PE_.transpose(out=BKB[bank][:, k * TW:(k + 1) * TW], in_=pb[:TW, k * P:(k + 1) * P],
                                                               identity=IDB[:TW, :TW])),
                              reads=[pbn, "IDB"], writes=[bn(bank)], inc=(k == 1), cost=c_pe(TW)) for k in range(2)]
                    ops.append(OP("act", lambda: A_.copy(out=ptt[:, :, :TW], in_=BKB[bank][:, 0:2 * TW].rearrange("p (k t) -> p k t", k=2)),
                                  reads=[bn(bank)], writes=[pttn], cost=c_act(2 * TW)))
                    yield ("group", ops)
                    yield ("free", [bank])
                    yield ("wait", [ev("Y", n, j)])
                    for hf in range(2):
                        bg, bw = yield ("alloc", 2)
                        ops = []
                        for k in range(KD_):
                            ops.append(OP("pe", (lambda k=k: PE_.matmul(BK[bg][:TW, :], lhsT=x1t[:, k, :TW], rhs=WG[:, k, hf * 512:(hf + 1) * 512],
                                                                        start=(k == 0), stop=(k == KD_ - 1))),
                                          reads=[x1tn, "WG"], writes=[bn(bg)], inc=(k == KD_ - 1), cost=c_pe(512)))
                        for k in range(2):
                            ops.append(OP("pe", (lambda k=k: PE_.matmul(BK[bw][:TW, :], lhsT=ptt[:, k, :TW], rhs=WPLE[:, k, hf * 512:(hf + 1) * 512],
                                                                        start=(k == 0), stop=(k == 1))),
                                          reads=[pttn, "WPLE"], writes=[bn(bw)], inc=(k == 1), cost=c_pe(512)))
                        ops.append(OP("act", lambda: A_.activation(out=T2[:TW, hf * 512:(hf + 1) * 512], in_=BK[bg][:TW, :], func=AF.Tanh, scale=0.5),
                                      reads=[bn(bg)], writes=["T2/%d" % hf], cost=c_act(512)))
                        ops.append(OP("dve", lambda: V.scalar_tensor_tensor(out=T2[:TW, hf * 512:(hf + 1) * 512], in0=T2[:TW, hf * 512:(hf + 1) * 512],
                                                                            scalar=1.0, in1=BK[bw][:TW, :], op0=ALU.add, op1=ALU.mult),
                                      reads=["T2/%d" % hf, bn(bw)], writes=["T2/%d" % hf], cost=c_dve(512, True)))
                        yield ("group", ops)
                        yield ("free", [bg, bw])
                    yield OP("pool", lambda: G_.tensor_tensor(out=T2[:TW, 0:512], in0=T2[:TW, 0:512], in1=xr[:TW, 0:512], op=ALU.add),
                             reads=["T2/0", xrn], writes=["T2/0"], cost=c_pool(512))
                    yield OP("dve", lambda: V.tensor_tensor(out=T2[:TW, 512:D], in0=T2[:TW, 512:D], in1=xr[:TW, 512:D], op=ALU.add),
                             reads=["T2/1", xrn], writes=["T2/1"], cost=c_dve(512))
                    yield DMA("OUT", st["y"][c0:c0 + TW, :], T2[:TW], reads=["T2"], eng="pool")
                    yield ("set", ev("Z", n, j))
                if st.get("final_state") or st["sample"]:
                    sids = [1, 2] if st["sample"] else [0]
                    yield ("wait", [ev("Hdone", n)])
                    for qi, sid in enumerate(sids):
                        for h in range(4):
                            yield OP("dve", (lambda h=h, sid=sid: V.tensor_scalar(out=T2[:, h * P:(h + 1) * P], in0=S32[sid][:, h, :],
                                                                                  scalar1=OML[:, h:h + 1], scalar2=None, op0=ALU.mult)),
                                     reads=[snm(sid, h), "OML"], writes=["T2"], cost=c_dve(P))
                        dst = state_s[qi] if st["sample"] else state_p
                        yield DMA("ST%d_%d" % (n, qi), dst.rearrange("h k v -> k h v"),
                                  T2[:, 0:512].rearrange("p (h v) -> p h v", h=4), reads=["T2"])
                    yield ("set", ev("stateout", n))
                yield ("set", ev("Zall", n))

        run = Runner(s, list(range(NB)))
        for nm, tt in WLOAD:
            run.t_w[nm] = tt
        run.run([("H2", lane_H2()), ("F0", lane_F(0)), ("F1", lane_F(1)), ("H1", lane_H1()), ("A0", lane_A(0)), ("A1", lane_A(1)),
                 ("U", lane_U()), ("P0", lane_POOL(0)), ("P1", lane_POOL(1)), ("PJ", lane_POOLJOIN()),
                 ("Y", lane_Y()), ("Z", lane_Z())])
        s.finish("sp")
    return nc, s


def _band_consts():
    W = (2, 4, 8, 16)
    cur = np.zeros((P, 4, P), np.float32)
    prev = np.zeros((P, 4, P), np.float32)
    first = np.zeros((P, 4, P), np.float32)
    for g, w in enumerate(W):
        for t in range(P):
            for sg in range(t - w + 1, t + 1):
                if sg >= 0:
                    cur[sg, g, t] += 1.0 / w
                else:
                    prev[P + sg, g, t] += 1.0 / w
            cur[t, g, t] -= 1.0
            cnt = min(w, t + 1)
            for sg in range(max(0, t - w + 1), t + 1):
                first[sg, g, t] += 1.0 / cnt
            first[t, g, t] -= 1.0
    scur = np.zeros((64, 4, 64), np.float32)
    sprev = np.zeros((64, 4, 64), np.float32)
    for g, w in enumerate(W):
        for q in range(2):
            o = 32 * q
            for i in range(16):
                e = 15 + i
                for ee in range(e - w + 1, e + 1):
                    if ee >= 15:
                        scur[o + ee - 15, g, o + i] += 1.0 / w
                    else:
                        sprev[o + ee, g, o + i] += 1.0 / w
                scur[o + i, g, o + i] -= 1.0
    matt = np.zeros((P, P), np.float32)
    for s_ in range(P):
        for t in range(P):
            if s_ // 64 == t // 64 and s_ <= t:
                matt[s_, t] = 1.0
    msatt = np.zeros((64, 64), np.float32)
    for s_ in range(64):
        for t in range(64):
            if s_ // 16 == t // 16 and s_ <= t and (s_ // 16) in (0, 2):
                msatt[s_, t] = 1.0
    rmask = np.ones((P, 512), np.float32)
    rmask[:, ::64] = 0.0
    rmask_s = np.ones((P, 64), np.float32)
    rmask_s[:, ::16] = 0.0
    return dict(band_cur=cur, band_prev=prev, band_first=first, band_scur=scur, band_sprev=sprev,
                mask_att=matt, mask_satt=msatt, rmask=rmask, rmask_s=rmask_s)


_PROG = {}


def _get_prog(npre, nmain):
    key = (npre, nmain)
    if key not in _PROG:
        _PROG[key] = build_program(npre, nmain)[0]
    return _PROG[key]


def make_in_maps(inputs, npre=HALF, nmain=HALF):
    f = lambda a: np.ascontiguousarray(np.asarray(a, dtype=np.float32))
    xp = f(inputs["x_prompt"])
    pp = f(inputs["p_prompt"])[0]
    xs = f(inputs["x_sample"])
    ps_ = f(inputs["p_sample"])[0]
    cache = f(inputs["cache_pool"])[0]
    st = f(inputs["state_hgrn"])[0]
    consts = _band_consts()
    shared = dict(
        w_in=f(inputs["w_in"])[0], w_pool=f(inputs["w_pool"])[0], pool_scale=f(inputs["pool_scale"])[0],
        hg_norm=f(inputs["hg_norm"])[0], w_out=f(inputs["w_out"])[0], norm_pre=f(inputs["norm_pre"])[0],
        norm_post=f(inputs["norm_post"])[0], w_ple=f(inputs["w_ple"])[0], w_gate=f(inputs["w_ple_gate"])[0],
        lb_logits=f(inputs["lb_logits"]),
    )
    in_maps = []
    for c in range(NCORES):
        b = c % 4
        second = c >= 4
        m = dict(shared)
        m.update({k: v for k, v in consts.items()})
        if second:
            m["x_pre"] = f(xp[b, HALF - npre:HALF]) if npre else np.zeros((P, D), np.float32)
            m["x_main"] = f(xp[b, HALF:HALF + nmain])
            m["p_main"] = f(pp[b, HALF:HALF + nmain])
            m["band_first"] = consts["band_cur"]
        else:
            m["x_pre"] = np.zeros((max(npre, P), D), np.float32)
            m["x_main"] = f(xp[b, 0:nmain])
            m["p_main"] = f(pp[b, 0:nmain])
        xsm = np.zeros((64, D), np.float32)
        psm = np.zeros((64, DPLE), np.float32)
        csm = np.zeros((64, 512), np.float32)
        for q in range(2):
            i = 2 * c + q
            xsm[32 * q:32 * q + 16] = xs[i]
            psm[32 * q:32 * q + 16] = ps_[i]
            csm[32 * q:32 * q + 15] = cache[i]
        m["x_smp"] = xsm
        m["p_smp"] = psm
        m["cache_smp"] = csm
        m["state_smp"] = f(st[2 * c:2 * c + 2])
        in_maps.append(m)
    return in_maps


def kernel(**inputs):
    nc = _get_prog(HALF, HALF)
    in_maps = make_in_maps(inputs)
    res = run_bass_kernel_spmd(nc, in_maps, core_ids=list(range(NCORES)))
    r = res.results
    y_prompt = np.zeros((4, SEQ, D), np.float32)
    y_sample = np.zeros((16, 16, D), np.float32)
    pool_p = np.zeros((1, 4, 15, 512), np.float32)
    hg_p = np.zeros((1, 4, 4, P, P), np.float32)
    pool_s = np.zeros((1, 16, 15, 512), np.float32)
    hg_s = np.zeros((1, 16, 4, P, P), np.float32)
    for c in range(NCORES):
        b = c % 4
        if c < 4:
            y_prompt[b, :HALF] = r[c]["y_main"]
        else:
            y_prompt[b, HALF:] = r[c]["y_main"]
            pool_p[0, b] = r[c]["pool_p"][P - 15:P]
            hg_p[0, b] = r[c]["state_p"]
        for q in range(2):
            i = 2 * c + q
            y_sample[i] = r[c]["y_smp"][32 * q:32 * q + 16]
            pool_s[0, i] = r[c]["pool_s"][32 * q + 1:32 * q + 16]
            hg_s[0, i] = r[c]["state_s"][q]
    return (y_prompt, y_sample, pool_p, hg_p, pool_s, hg_s)
```

```python
import contextlib
import os
import numpy as np
import concourse.bass as bass
import concourse.mybir as mybir
from concourse.bass_utils import run_bass_kernel_spmd

F32 = mybir.dt.float32
BF16 = mybir.dt.bfloat16
AF = mybir.ActivationFunctionType
ALU = mybir.AluOpType

P = 128
D = 1024
KD_ = 8
DIN = 3072
DPLE = 256
EPS = 1e-6
SEQ = 8192
HALF = 4096
NCORES = 8
C_U, C_GP, C_Q, C_F, C_V, C_GH = 0, 512, 1024, 1536, 2048, 2560


class _StopBuild(Exception):
    pass


_KSTOP = float(os.environ.get("KSTOP", "999"))


def ck(n):
    if n >= _KSTOP:
        raise _StopBuild()


class Sched:
    EPOCH = 30000

    def __init__(self, nc):
        self.nc = nc
        self.eng = {"pe": nc.tensor, "act": nc.scalar, "dve": nc.vector,
                    "pool": nc.gpsimd, "sp": nc.sync}
        self.cur_sem = {}
        self.cnt = {}
        self.nsem = 0
        self.sem_objs = {}
        for e in self.eng:
            self._new_epoch(e)
        self.waited = {}
        self.lastw = {}
        self.readers = {}
        self.children = {}
        self.pend_r = {e: [] for e in self.eng}
        self.pend_w = {e: [] for e in self.eng}
        self.dma_sem = {}
        self.dma_cnt = {}
        self.n_wait = 0
        self.n_ops = 0

    def _alloc(self, name):
        h = self.nc.alloc_semaphore(name=name)
        self.nsem += 1
        self.sem_objs[name] = h
        return name

    def _new_epoch(self, e):
        k = self._alloc("pg_%s_%d" % (e, self.nsem))
        self.cur_sem[e] = k
        self.cnt[e] = 0

    def _rel(self, name):
        if "/" in name:
            par = name.split("/")[0]
            ch = self.children.setdefault(par, [])
            if name not in ch:
                ch.append(name)
            return (name, par)
        return (name,) + tuple(self.children.get(name, ()))

    def _wait(self, e, ticket):
        key, val, src = ticket
        if src == e and e == "pe":
            return
        if self.waited.get((e, key), 0) >= val:
            return
        self.eng[e].wait_ge(self.sem_objs[key], val)
        self.waited[(e, key)] = val
        self.n_wait += 1

    def _deps(self, e, reads, writes):
        deps = []
        for r in reads:
            for nm in self._rel(r):
                t = self.lastw.get(nm)
                if t is not None:
                    deps.append(t)
                if nm[0] == "P" and nm[1] == "B" and nm[2:].isdigit():
                    for t in self.readers.get(nm, ()):
                        if t[2] != e:
                            deps.append(t)
        for w in writes:
            for nm in self._rel(w):
                t = self.lastw.get(nm)
                if t is not None:
                    deps.append(t)
                deps.extend(self.readers.get(nm, ()))
        return deps

    def _check_pending(self, e, reads, writes):
        for oe in self.eng:
            if oe == e:
                continue
            for r in list(reads) + list(writes):
                for nm in self._rel(r):
                    if nm in self.pend_w[oe]:
                        raise RuntimeError("dependency on pending write %s (%s)" % (nm, oe))
            for w in writes:
                for nm in self._rel(w):
                    if nm in self.pend_r[oe]:
                        raise RuntimeError("WAR on pending read %s (%s)" % (nm, oe))

    def _record(self, t, reads, writes):
        for w in writes:
            self.lastw[w] = t
            self.readers[w] = []
            if "/" not in w:
                for c in self.children.get(w, ()):
                    self.readers[c] = []
        for r in reads:
            if r not in writes:
                self.readers.setdefault(r, []).append(t)

    def op(self, e, fn, reads=(), writes=(), inc=True):
        self.n_ops += 1
        self._check_pending(e, reads, writes)
        for t in self._deps(e, reads, writes):
            self._wait(e, t)
        ins = fn()
        self.pend_r[e].extend(reads)
        self.pend_w[e].extend(writes)
        if inc:
            if self.cnt[e] >= self.EPOCH:
                self._new_epoch(e)
            self.cnt[e] += 1
            key = self.cur_sem[e]
            ins.then_inc(self.sem_objs[key], 1)
            t = (key, self.cnt[e], e)
            self._record(t, self.pend_r[e], self.pend_w[e])
            self.pend_r[e] = []
            self.pend_w[e] = []
        return ins

    def dma(self, chan, out, in_, reads=(), writes=(), eng="sp", nowait=False, **kw):
        self.n_ops += 1
        if chan not in self.dma_sem:
            self.dma_sem[chan] = self._alloc("dma_%s" % chan)
            self.dma_cnt[chan] = 0
        self._check_pending(eng, reads, writes)
        if not nowait:
            for t in self._deps("dma", reads, writes):
                self._wait(eng, t)
        key = self.dma_sem[chan]
        ins = self.eng[eng].dma_start(out=out, in_=in_, **kw)
        self.dma_cnt[chan] += 16
        ins.then_inc(self.sem_objs[key], 16)
        t = (key, self.dma_cnt[chan], "dma:" + chan)
        self._record(t, reads, writes)
        return t

    def finish(self, e="sp"):
        for chan, key in self.dma_sem.items():
            if self.dma_cnt[chan] > 0:
                self._wait(e, (key, self.dma_cnt[chan], "dma:" + chan))


class Runner:
    LAT = 180.0
    SLACK = 300.0

    def __init__(self, s, banks):
        self.s = s
        self.free = list(banks)
        self.events = set()
        self.t_eng = {e: 0.0 for e in s.eng}
        self.t_w = {}
        self.t_r = {}

    def _est(self, op):
        kind = op[0]
        if kind == "group":
            return self._est(op[1][0])
        if kind == "dma":
            _, chan, out, in_, reads, writes, eng, kw, cost = op
            e = eng
        else:
            _, e, fn, reads, writes, inc, cost = op
        t = self.t_eng[e]
        for r in reads:
            t = max(t, self.t_w.get(r, 0.0))
        for w in writes:
            t = max(t, self.t_w.get(w, 0.0), self.t_r.get(w, 0.0))
        return t

    def _emit(self, op):
        kind = op[0]
        if kind == "group":
            for o in op[1]:
                self._emit(o)
            return
        t = self._est(op)
        if kind == "dma":
            _, chan, out, in_, reads, writes, eng, kw, cost = op
            self.s.dma(chan, out, in_, reads=reads, writes=writes, eng=eng, **kw)
            self.t_eng[eng] = t + 60.0
            end = t + cost
        else:
            _, e, fn, reads, writes, inc, cost = op
            self.s.op(e, fn, reads=reads, writes=writes, inc=inc)
            self.t_eng[e] = t + cost
            end = t + cost + self.LAT
        for w in writes:
            self.t_w[w] = end
        for r in reads:
            self.t_r[r] = max(self.t_r.get(r, 0.0), end)

    def run(self, gens):
        ths = [dict(g=g, head=None, send=None, done=False, name=n) for n, g in gens]

        def step(th):
            try:
                th["head"] = th["g"].send(th["send"])
            except StopIteration:
                th["head"] = None
                th["done"] = True
            th["send"] = None

        for th in ths:
            step(th)
        while True:
            progressed = True
            while progressed:
                progressed = False
                for th in ths:
                    while not th["done"]:
                        h = th["head"]
                        k = h[0]
                        if k == "free":
                            self.free.extend(h[1])
                            step(th)
                            progressed = True
                        elif k == "set":
                            self.events.add(h[1])
                            step(th)
                            progressed = True
                        elif k == "wait":
                            if all(ev in self.events for ev in h[1]):
                                step(th)
                                progressed = True
                            else:
                                break
                        elif k == "alloc":
                            if len(self.free) >= h[1]:
                                th["send"] = [self.free.pop(0) for _ in range(h[1])]
                                step(th)
                                progressed = True
                            else:
                                break
                        else:
                            break
            cands = [th for th in ths if not th["done"] and th["head"][0] in ("op", "dma", "group")]
            if not cands:
                if all(th["done"] for th in ths):
                    return
                raise RuntimeError("runner deadlock: " + str([(th["name"], th["head"][:2]) for th in ths if not th["done"]]))
            ests = [(self._est(th["head"]), i, th) for i, th in enumerate(cands)]
            tmin = min(e[0] for e in ests)
            th = min((e for e in ests if e[0] <= tmin + self.SLACK), key=lambda e: e[1])[2]
            self._emit(th["head"])
            step(th)


def OP(e, fn, reads=(), writes=(), inc=True, cost=300.0):
    return ("op", e, fn, tuple(reads), tuple(writes), inc, cost)


def DMA(chan, out, in_, reads=(), writes=(), eng="sp", cost=3000.0, **kw):
    return ("dma", chan, out, in_, tuple(reads), tuple(writes), eng, kw, cost)


def c_pe(n):
    return 30.0 + 0.45 * n


def c_act(f):
    return 260.0 + 0.85 * f


def c_dve(f, psum=False, fast=1.0):
    return (220.0 if psum else 160.0) + 1.05 * f / fast


def c_pool(f):
    return 300.0 + 2.3 * f


def build_program(npre, nmain, do_sample=True):
    nc = bass.Bass("TRN2", target_bir_lowering=False)

    def din(name, shape):
        return nc.dram_tensor(name, list(shape), F32, kind="ExternalInput").ap()

    def dout(name, shape):
        return nc.dram_tensor(name, list(shape), F32, kind="ExternalOutput").ap()

    x_pre = din("x_pre", [max(npre, P), D])
    x_main = din("x_main", [nmain, D])
    p_main = din("p_main", [nmain, DPLE])
    x_smp = din("x_smp", [64, D])
    p_smp = din("p_smp", [64, DPLE])
    cache_smp = din("cache_smp", [64, 512])
    state_smp = din("state_smp", [2, 4, P, P])
    w_in = din("w_in", [D, DIN])
    w_pool = din("w_pool", [4, P, P])
    pool_scale = din("pool_scale", [512])
    hg_norm = din("hg_norm", [P])
    w_out = din("w_out", [D, D])
    norm_pre = din("norm_pre", [D])
    norm_post = din("norm_post", [D])
    w_ple = din("w_ple", [DPLE, D])
    w_gate = din("w_gate", [D, D])
    lb_logits = din("lb_logits", [2, 512])
    band_cur = din("band_cur", [P, 4, P])
    band_prev = din("band_prev", [P, 4, P])
    band_first = din("band_first", [P, 4, P])
    band_scur = din("band_scur", [64, 4, 64])
    band_sprev = din("band_sprev", [64, 4, 64])
    mask_att = din("mask_att", [P, P])
    mask_satt = din("mask_satt", [64, 64])
    rmask = din("rmask", [P, 512])
    rmask_s = din("rmask_s", [P, 64])

    y_main = dout("y_main", [nmain, D])
    y_smp = dout("y_smp", [64, D])
    pool_p = dout("pool_p", [P, 512])
    pool_s = dout("pool_s", [64, 512])
    state_p = dout("state_p", [4, P, P])
    state_s = dout("state_s", [2, 4, P, P])

    s = Sched(nc)
    es = contextlib.ExitStack()

    def T(nm, shp, dt=F32):
        return es.enter_context(nc.sbuf_tensor(nm, list(shp), dt))

    def PS(nm, shp, dt=F32):
        return es.enter_context(nc.psum_tensor(nm, list(shp), dt))

    V, A_, G_, PE_ = nc.vector, nc.scalar, nc.gpsimd, nc.tensor

    with es:
        WIN = T("WIN", [P, KD_, DIN], BF16)
        WOUT = T("WOUT", [P, KD_, D], BF16)
        WG = T("WG", [P, KD_, D], BF16)
        WPLE = T("WPLE", [P, 2, D], BF16)
        WPOOL = T("WPOOL", [P, 4, P], BF16)
        NPRE = T("NPRE", [P, D])
        GPOST = T("GPOST", [P, D])
        PSC = T("PSC", [P, 4])
        HGN = T("HGN", [P, 1])
        LBT = T("LBT", [P, 8])
        OML = T("OML", [P, 4])
        HSC = T("HSC", [P, 4])
        HBI = T("HBI", [P, 4])
        IOML = T("IOML", [P, 4])
        MH = T("MH", [P, 4])
        IDB = T("IDB", [P, P], BF16)
        BANDC = T("BANDC", [P, 4, P], BF16)
        BANDP = T("BANDP", [P, 4, P], BF16)
        BANDF = T("BANDF", [P, 4, P], BF16)
        BANDSC = T("BANDSC", [64, 4, 64], BF16)
        BANDSP = T("BANDSP", [64, 4, 64], BF16)
        MATT = T("MATT", [P, P], BF16)
        MSATT = T("MSATT", [64, 64], BF16)
        RMASK = T("RMASK", [P, 512], BF16)
        RMASKS = T("RMASKS", [P, 64], BF16)
        CACHE = T("CACHE", [64, 512], BF16)

        X = [T("X%d" % i, [P, D]) for i in range(2)]
        XN = [T("XN%d" % i, [P, D], BF16) for i in range(2)]
        SMA = T("SMA", [P, 8])
        XNT = T("XNT", [P, KD_, 512], BF16)
        UTOK = T("UTOK", [P, 4, 512], BF16)
        UPREV = T("UPREV", [P, 512], BF16)
        TH = [T("TH%d" % i, [P, 512]) for i in range(2)]
        GD = [T("GD%d" % i, [P, 512]) for i in range(2)]
        BA = [T("BA%d" % i, [P, 512]) for i in range(2)]
        QE = T("QE", [P, 4, 512], BF16)
        KE = T("KE", [P, 4, 512], BF16)
        KDT = T("KDT", [P, 4, 512], BF16)
        DEC = T("DEC", [P, 4, 8])
        EMID = T("EMID", [P, 4, 8])
        SGL = [T("SGL%d" % i, [P, 512], BF16) for i in range(2)]
        PTBL = [T("PTBL%d" % i, [P, 512], BF16) for i in range(2)]
        MIXT = T("MIXT", [P, 8, 512], BF16)
        VTOK = [T("VTOK%d" % i, [P, 512], BF16) for i in range(2)]
        SH = [T("SH%d" % i, [P, 512], BF16) for i in range(2)]
        KD = [T("KD%d" % i, [P, 512], BF16) for i in range(2)]
        ATT = [T("ATT%d" % i, [P, 4, P], BF16) for i in range(2)]
        OG = T("OG", [P, 512], BF16)
        SMH = T("SMH", [P, 12])
        XR = [T("XR%d" % i, [P, D]) for i in range(2)]
        T1 = T("T1", [P, D])
        X1B = T("X1B", [P, D], BF16)
        X1T = [T("X1T%d" % i, [P, KD_, P], BF16) for i in range(2)]
        S32 = [T("S32_0", [P, 4, P])[:], XR[1][:, 0:512].rearrange("p (h v) -> p h v", h=4),
               XR[1][:, 512:1024].rearrange("p (h v) -> p h v", h=4)]
        SB = [T("SB_0", [P, 4, P], BF16)[:], X1T[1][:, 0:4, :], X1T[1][:, 4:8, :]]

        def snm(sid, h):
            return "S32_0/h%d" % h if sid == 0 else "XR1/s%dh%d" % (sid, h)

        def sbnm(sid, h):
            return "SB_0/h%d" % h if sid == 0 else "X1T1/s%dh%d" % (sid, h)
        SMY = T("SMY", [P, 8])
        PIN = T("PIN", [P, 4, DPLE])
        PB = [T("PB%d" % i, [P, DPLE], BF16) for i in range(2)]
        PTT = [T("PTT%d" % i, [P, 2, P], BF16) for i in range(2)]
        T2 = T("T2", [P, D])

        NB = 8
        BK = [PS("BK%d" % i, [P, 512]) for i in range(NB)]
        BKB = [b[:].bitcast(BF16) for b in BK]

        def bn(i):
            return "PB%d" % i

        s.op("pool", lambda: G_.memset(IDB[:], 1.0), writes=["IDB"])
        s.op("pool", lambda: G_.affine_select(out=IDB[:], in_=IDB[:], pattern=[[-1, P]], compare_op=ALU.is_equal,
                                              fill=0.0, base=0, channel_multiplier=1), reads=["IDB"], writes=["IDB"])
        s.op("pool", lambda: G_.memset(MH[:], -0.5), writes=["MH"])
        s.op("pool", lambda: G_.memset(S32[0], 0.0), writes=["S32_0"])
        s.op("pool", lambda: G_.memset(UPREV[:], 0.0), writes=["UPREV"])
        for nm, t_, src in (("RMASK", RMASK, rmask), ("RMASKS", RMASKS, rmask_s), ("MATT", MATT, mask_att), ("MSATT", MSATT, mask_satt)):
            s.dma(nm, t_[:], src, writes=[nm], eng="pool", nowait=True)
        SEGN = ["WIN/u", "WIN/gp", "WIN/q", "WIN/f", "WIN/v", "WIN/gh"]
        w_in_v = w_in.rearrange("(k p) n -> p k n", p=P)
        WLOAD = []
        tarr = 0.0
        for seg in (3, 4, 0, 2, 1, 5):
            for k0 in (0, 4):
                s.dma("WIN%d" % seg, WIN[:, k0:k0 + 4, seg * 512:(seg + 1) * 512], w_in_v[:, k0:k0 + 4, seg * 512:(seg + 1) * 512],
                      writes=[SEGN[seg]], eng="pool", nowait=True)
            tarr += 22000.0
            WLOAD.append((SEGN[seg], tarr))
        for nm, t_, src in (("BANDC", BANDC, band_cur), ("BANDP", BANDP, band_prev), ("BANDF", BANDF, band_first),
                            ("BANDSC", BANDSC, band_scur), ("BANDSP", BANDSP, band_sprev), ("CACHE", CACHE, cache_smp)):
            s.dma(nm, t_[:], src, writes=[nm], eng="pool", nowait=True)
        for g in range(4):
            s.dma("WPOOL", WPOOL[:, g, :], w_pool[g], writes=["WPOOL"], eng="pool", nowait=True)
        for nm, t_, src, nk in (("WOUT", WOUT, w_out, KD_), ("WG", WG, w_gate, KD_), ("WPLE", WPLE, w_ple, 2)):
            for k in range(nk):
                s.dma(nm, t_[:, k, :], src[k * P:(k + 1) * P, :], writes=[nm], eng="pool", nowait=True)
            tarr += 5000.0 * nk
            WLOAD.append((nm, tarr))
        s.dma("NPRE", NPRE[:], norm_pre.partition_broadcast(P), writes=["NPRE"])
        s.dma("GPOST", GPOST[:], norm_post.partition_broadcast(P), writes=["GPOST"])
        s.dma("PSC", PSC[:], pool_scale.rearrange("(g c) -> c g", c=P), writes=["PSC"], allow_slow_non_contiguous=True)
        s.dma("HGN", HGN[:], hg_norm.rearrange("(c o) -> c o", o=1), writes=["HGN"])
        s.dma("LBT", LBT[:, 0:4], lb_logits[0].rearrange("(h k) -> k h", k=P), writes=["LBT"], allow_slow_non_contiguous=True)
        s.dma("LBT", LBT[:, 4:8], lb_logits[1].rearrange("(h k) -> k h", k=P), writes=["LBT"], allow_slow_non_contiguous=True)
        s.op("dve", lambda: V.tensor_tensor(out=SMA[:, 0:4], in0=LBT[:, 0:4], in1=LBT[:, 4:8], op=ALU.subtract),
             reads=["LBT"], writes=["SMA"])
        s.op("act", lambda: A_.activation(out=SMA[:, 4:8], in_=SMA[:, 0:4], func=AF.Tanh, scale=0.5),
             reads=["SMA"], writes=["SMA"])
        s.op("dve", lambda: V.tensor_scalar(out=OML[:], in0=SMA[:, 4:8], scalar1=-0.5, scalar2=0.5, op0=ALU.mult, op1=ALU.add),
             reads=["SMA"], writes=["OML"])
        s.op("dve", lambda: V.tensor_scalar(out=HSC[:], in0=OML[:], scalar1=0.5, scalar2=None, op0=ALU.mult),
             reads=["OML"], writes=["HSC"])
        s.op("dve", lambda: V.tensor_scalar(out=HBI[:], in0=OML[:], scalar1=-0.5, scalar2=1.0, op0=ALU.mult, op1=ALU.add),
             reads=["OML"], writes=["HBI"])
        s.op("dve", lambda: V.reciprocal(out=IOML[:], in_=OML[:]), reads=["OML"], writes=["IOML"])
        pre_sts, main_sts, smp_sts = [], [], []
        for i in range(npre // 512):
            pre_sts.append(dict(kind="pre", x=x_pre[i * 512:(i + 1) * 512, :], p=None, y=None, ntok=512, TW=P, BL=64,
                                blocks=[[(0, 64, 0), (64, 64, 0)]] * 4, first=False, sample=False,
                                want_u=(3 if i == npre // 512 - 1 else None), udst=None))
        for i in range(nmain // 512):
            last = i == nmain // 512 - 1
            main_sts.append(dict(kind="main", x=x_main[i * 512:(i + 1) * 512, :], p=p_main[i * 512:(i + 1) * 512, :],
                                 y=y_main[i * 512:(i + 1) * 512, :], ntok=512, TW=P, BL=64,
                                 blocks=[[(0, 64, 0), (64, 64, 0)]] * 4, first=(i == 0), sample=False,
                                 want_u=(3 if last else None), udst=(pool_p if last else None), final_state=last))
        if do_sample:
            smp_sts.append(dict(kind="main", x=x_smp, p=p_smp, y=y_smp, ntok=64, TW=64, BL=16,
                                blocks=[[(0, 16, 1), (32, 16, 2)]], first=False, sample=True, want_u=0, udst=pool_s,
                                final_state=False))
        cut = min(3, len(pre_sts))
        STs = pre_sts[:cut] + smp_sts + pre_sts[cut:] + main_sts
        NS = len(STs)
        for i, st in enumerate(pre_sts):
            st["prei"] = i
        XNTB, XNTN = [XNT, MIXT], ["XNT", "MIXT"]
        KDTB, KDTN = [KDT, QE], ["KDT", "QE"]
        DECB, DECN = [DEC, EMID], ["DEC", "EMID"]
        for n, st in enumerate(STs):
            st["n"] = n
            st["ntile"] = st["ntok"] // st["TW"]
            st["pre"] = st["kind"] == "pre"
            st["bs"] = (st["prei"] % 2) if st["pre"] else 0

        def prev_same(n):
            for m in range(n - 1, -1, -1):
                if STs[m]["bs"] == STs[n]["bs"]:
                    return m
            return None

        def prev_alt(n):
            for m in range(n - 1, -1, -1):
                if STs[m]["bs"] == 1:
                    return m
            return None

        def prev_mainkind(n):
            for m in range(n - 1, -1, -1):
                if not STs[m]["pre"]:
                    return m
            return None

        def tile_back2(n, j, mainonly=False):
            jj = j - 2
            m = n
            while True:
                if jj >= 0:
                    return m, jj
                m -= 1
                while m >= 0 and mainonly and STs[m]["pre"]:
                    m -= 1
                if m < 0:
                    return None
                nt = STs[m]["ntile"]
                jj = nt - 1 if (nt - 1) % 2 == j % 2 else nt - 2

        def ev(*a):
            return "_".join(str(x) for x in a)

        def fm_block(bank, st, col0):
            ntok = st["ntok"]
            return [OP("pe", (lambda k=k: PE_.matmul(BK[bank][:, :ntok], lhsT=WIN[:, k, col0:col0 + P], rhs=XNTB[st["bs"]][:, k, :ntok],
                                                     start=(k == 0), stop=(k == KD_ - 1))),
                       reads=[SEGN[col0 // 512], XNTN[st["bs"]]], writes=[bn(bank)], inc=(k == KD_ - 1), cost=c_pe(ntok)) for k in range(KD_)]

        def tm_block(bank, st, j, col0):
            TW = st["TW"]
            c0 = j * TW
            return [OP("pe", (lambda k=k: PE_.matmul(BK[bank][:TW, :], lhsT=XNTB[st["bs"]][:, k, c0:c0 + TW], rhs=WIN[:, k, col0:col0 + 512],
                                                     start=(k == 0), stop=(k == KD_ - 1))),
                       reads=[SEGN[col0 // 512], XNTN[st["bs"]]], writes=[bn(bank)], inc=(k == KD_ - 1), cost=c_pe(512)) for k in range(KD_)]

        def lane_A(lane):
            for st in STs:
                n, TW, ntile = st["n"], st["TW"], st["ntile"]
                m = prev_same(n)
                if m is not None:
                    yield ("wait", [ev("xntfree", m), ev("POOLall", m), ev("U", m)])
                if st["pre"] and st["bs"] == 1 and prev_mainkind(n) is not None:
                    yield ("wait", [ev("Ymm", prev_mainkind(n)), ev("Hdone", prev_mainkind(n))])
                for j in range(lane, ntile, 2):
                    c0 = j * TW
                    xn, xnn = XN[lane], "XN%d" % lane
                    a0 = 4 * lane
                    ss, rt, rx = SMA[:TW, a0:a0 + 1], SMA[:TW, a0 + 1:a0 + 2], SMA[:TW, a0 + 2:a0 + 3]
                    sn = "SMA/l%d" % lane
                    xx, xxn = X[lane], "X%d" % lane
                    yield DMA("X%d" % lane, xx[:TW], st["x"][c0:c0 + TW, :], writes=[xxn])
                    yield OP("act", lambda: A_.activation(out=xn[:TW], in_=xx[:TW], func=AF.Square, accum_out=ss),
                             reads=[xxn], writes=[xnn, sn + "s"], cost=c_act(D))
                    yield OP("dve", lambda: V.tensor_scalar(out=rt, in0=ss, scalar1=1.0 / D, scalar2=EPS, op0=ALU.mult, op1=ALU.add),
                             reads=[sn + "s"], writes=[sn + "r"], cost=200)
                    yield OP("pool", lambda: G_.tensor_tensor(out=rx, in0=rt, in1=MH[:TW, 0:1], op=ALU.pow),
                             reads=[sn + "r", "MH"], writes=[sn + "x"], cost=500)
                    yield OP("dve", lambda: V.scalar_tensor_tensor(out=xn[:TW], in0=xx[:TW], scalar=rx, in1=NPRE[:TW],
                                                                   op0=ALU.mult, op1=ALU.mult),
                             reads=[xxn, sn + "x", "NPRE"], writes=[xnn], cost=c_dve(D))
                    bank, = yield ("alloc", 1)
                    ops = [OP("pe", (lambda k=k: PE_.transpose(out=BKB[bank][:, k * TW:(k + 1) * TW], in_=xn[:TW, k * P:(k + 1) * P],
                                                               identity=IDB[:TW, :TW])),
                              reads=[xnn, "IDB"], writes=[bn(bank)], inc=(k == KD_ - 1), cost=c_pe(TW)) for k in range(KD_)]
                    ops.append(OP("act", lambda: A_.copy(out=XNTB[st["bs"]][:, :, c0:c0 + TW],
                                                         in_=BKB[bank][:, 0:KD_ * TW].rearrange("p (k t) -> p k t", k=KD_)),
                                  reads=[bn(bank)], writes=[XNTN[st["bs"]]], cost=c_act(KD_ * TW)))
                    yield ("group", ops)
                    yield ("free", [bank])
                    yield ("set", ev("A", n, j))

        def v3(t, st):
            return t[:, :st["ntok"]].rearrange("p (b c) -> p b c", c=st["BL"])

        def lane_F(lane):
            th, gd, ba = TH[lane], GD[lane], BA[lane]
            thn, gdn, ban = "TH%d" % lane, "GD%d" % lane, "BA%d" % lane
            e2, e2n = gd, gdn
            for st in STs:
                n, ntok, BL, pre = st["n"], st["ntok"], st["BL"], st["pre"]
                nblk = ntok // BL
                mid = BL // 2 - 1
                rm, rmn = (RMASKS, "RMASKS") if st["sample"] else (RMASK, "RMASK")
                wl = [ev("A", n, j) for j in range(st["ntile"])]
                if prev_same(n) is not None:
                    wl.append(ev("qkdfree", prev_same(n)))
                if not pre and prev_alt(n) is not None:
                    wl += [ev("qkdfree", prev_alt(n)), ev("xntfree", prev_alt(n)), ev("U", prev_alt(n))]
                if pre and st["bs"] == 1 and prev_mainkind(n) is not None:
                    wl += [ev("Hdone", prev_mainkind(n)), ev("Ymm", prev_mainkind(n))]
                yield ("wait", wl)
                kdt, kdtp, dec, decp = KDTB[st["bs"]], KDTN[st["bs"]], DECB[st["bs"]], DECN[st["bs"]]
                for h in range(lane, 4, 2):
                    bank, = yield ("alloc", 1)
                    ops = fm_block(bank, st, C_F + h * P)
                    ops.append(OP("act", lambda: A_.activation(out=th[:, :ntok], in_=BK[bank][:, :ntok], func=AF.Tanh, scale=0.5),
                                  reads=[bn(bank)], writes=[thn], cost=c_act(ntok)))
                    yield ("group", ops)
                    yield ("free", [bank])
                    yield OP("act", lambda: A_.activation(out=gd[:, :ntok], in_=th[:, :ntok], func=AF.Ln,
                                                          scale=HSC[:, h:h + 1], bias=HBI[:, h:h + 1]),
                             reads=[thn, "HSC", "HBI"], writes=[gdn], cost=c_act(ntok))
                    yield OP("dve", lambda: V.tensor_tensor_scan(out=ba[:, :ntok], data0=rm[:, :ntok], data1=gd[:, :ntok],
                                                                 initial=0.0, op0=ALU.mult, op1=ALU.add),
                             reads=[gdn, rmn], writes=[ban], cost=2.1 * ntok + 160)
                    yield OP("dve", lambda: V.tensor_tensor(out=v3(gd, st), in0=v3(ba, st)[:, :, BL - 1:BL].to_broadcast([P, nblk, BL]),
                                                            in1=v3(ba, st), op=ALU.subtract),
                             reads=[ban, gdn], writes=[gdn], cost=c_dve(ntok))
                    yield OP("act", lambda: A_.activation(out=dec[:, h, 0:nblk], in_=v3(ba, st)[:, :, BL - 1], func=AF.Exp),
                             reads=[ban], writes=[decp + "/%d" % h], cost=300)
                    yield OP("act", lambda: A_.activation(out=gd[:, :ntok], in_=gd[:, :ntok], func=AF.Exp),
                             reads=[gdn], writes=[gdn], cost=c_act(ntok))
                    yield OP("dve", lambda: V.scalar_tensor_tensor(out=kdt[:, h, :ntok], in0=th[:, :ntok], scalar=1.0,
                                                                   in1=gd[:, :ntok], op0=ALU.subtract, op1=ALU.mult),
                             reads=[thn, gdn], writes=[kdtp + "/%d" % h], cost=c_dve(ntok))
                    if not pre:
                        yield OP("act", lambda: A_.activation(out=EMID[:, h, 0:nblk], in_=v3(ba, st)[:, :, mid], func=AF.Exp),
                                 reads=[ban], writes=["EMID/%d" % h], cost=300)
                        yield OP("dve", lambda: V.tensor_tensor(out=v3(gd, st), in0=v3(ba, st),
                                                                in1=v3(ba, st)[:, :, mid:mid + 1].to_broadcast([P, nblk, BL]),
                                                                op=ALU.subtract), reads=[ban, gdn], writes=[gdn], cost=c_dve(ntok))
                        yield OP("act", lambda: A_.activation(out=ba[:, :ntok], in_=gd[:, :ntok], func=AF.Exp, scale=-1.0),
                                 reads=[gdn, ban], writes=[ban], cost=c_act(ntok))
                        yield OP("act", lambda: A_.activation(out=gd[:, :ntok], in_=gd[:, :ntok], func=AF.Exp),
                                 reads=[gdn], writes=[gdn], cost=c_act(ntok))
                        yield OP("dve", lambda: V.scalar_tensor_tensor(out=KE[:, h, :ntok], in0=th[:, :ntok], scalar=1.0,
                                                                       in1=ba[:, :ntok], op0=ALU.subtract, op1=ALU.mult),
                                 reads=[thn, ban], writes=["KE/%d" % h], cost=c_dve(ntok))
                        bank, = yield ("alloc", 1)
                        ops = fm_block(bank, st, C_Q + h * P)
                        ops.append(OP("dve", lambda: V.scalar_tensor_tensor(out=QE[:, h, :ntok], in0=BK[bank][:, :ntok],
                                                                            scalar=OML[:, h:h + 1], in1=gd[:, :ntok],
                                                                            op0=ALU.mult, op1=ALU.mult),
                                      reads=[bn(bank), "OML", gdn], writes=["QE/%d" % h], cost=c_dve(ntok, True)))
                        yield ("group", ops)
                        yield ("free", [bank])
                    yield ("set", ev("F", n, h))

        def lane_U():
            for st in STs:
                n, TW, ntile, pre = st["n"], st["TW"], st["ntile"], st["pre"]
                yield ("wait", [ev("A", n, j) for j in range(ntile)] + ([ev("POOLall", n - 1)] if n > 0 else []))
                for j in range(ntile):
                    if pre and st["want_u"] != j:
                        continue
                    bank, = yield ("alloc", 1)
                    ops = tm_block(bank, st, j, C_U)
                    ops.append(OP("act", lambda: A_.copy(out=UTOK[:TW, j, :], in_=BK[bank][:TW, :]),
                                  reads=[bn(bank)], writes=["UTOK"], cost=c_act(512)))
                    if st["want_u"] == j and st["udst"] is not None:
                        yield ("wait", [ev("Zall", n - 1)] if n > 0 else [])
                        ops.append(OP("act", lambda: A_.copy(out=T2[:TW, 0:512], in_=BK[bank][:TW, :]),
                                      reads=[bn(bank)], writes=["T2"], cost=c_act(512)))
                        ops.append(DMA("UD%d" % n, st["udst"], T2[:TW, 0:512], reads=["T2"]))
                    yield ("group", ops)
                    yield ("free", [bank])
                if pre and st["want_u"] is not None:
                    yield OP("pool", lambda: G_.tensor_copy(out=UPREV[:], in_=UTOK[:, 3, :]), reads=["UTOK"], writes=["UPREV"],
                             cost=c_pool(512))
                yield ("set", ev("U", n))

        def lane_POOL(lane):
            sg, sgn, ptb, ptn = SGL[lane], "SGL%d" % lane, PTBL[lane], "PTBL%d" % lane
            for st in STs:
                n, TW, ntile, ntok = st["n"], st["TW"], st["ntile"], st["ntok"]
                if st["pre"]:
                    yield ("set", ev("POOL", n, lane))
                    continue
                yield ("wait", [ev("U", n)] + ([ev("Ymm", prev_mainkind(n))] if prev_mainkind(n) is not None else []) + ([ev("xntfree", prev_alt(n)), ev("U", prev_alt(n))] if prev_alt(n) is not None else []))
                for g in range(lane, 4, 2):
                    bank, = yield ("alloc", 1)
                    ops = fm_block(bank, st, C_GP + g * P)
                    ops.append(OP("act", lambda: A_.activation(out=sg[:, :ntok], in_=BK[bank][:, :ntok], func=AF.Silu),
                                  reads=[bn(bank)], writes=[sgn], cost=c_act(ntok)))
                    yield ("group", ops)
                    yield ("free", [bank])
                    bank, = yield ("alloc", 1)
                    ops = []
                    for j in range(ntile):
                        c0 = j * TW
                        if st["sample"]:
                            bc_, bcn = BANDSC, "BANDSC"
                        elif st["first"] and j == 0:
                            bc_, bcn = BANDF, "BANDF"
                        else:
                            bc_, bcn = BANDC, "BANDC"
                        ops.append(OP("pe", (lambda j=j, c0=c0, bc_=bc_: PE_.matmul(BK[bank][:, c0:c0 + TW], lhsT=UTOK[:TW, j, g * P:(g + 1) * P],
                                                                                    rhs=bc_[:TW, g, :TW], start=True, stop=False)),
                                      reads=["UTOK", bcn], writes=[bn(bank)], inc=False, cost=c_pe(TW)))
                        if st["sample"]:
                            ops.append(OP("pe", (lambda c0=c0: PE_.matmul(BK[bank][:, c0:c0 + TW], lhsT=CACHE[0:64, g * P:(g + 1) * P],
                                                                          rhs=BANDSP[0:64, g, :TW], start=False, stop=True)),
                                          reads=["CACHE", "BANDSP"], writes=[bn(bank)], inc=(j == ntile - 1), cost=c_pe(TW)))
                        else:
                            if j == 0:
                                pv, pvn = UPREV[64:128, g * P:(g + 1) * P], "UPREV"
                            else:
                                pv, pvn = UTOK[64:128, j - 1, g * P:(g + 1) * P], "UTOK"
                            ops.append(OP("pe", (lambda c0=c0, pv=pv: PE_.matmul(BK[bank][:, c0:c0 + TW], lhsT=pv, rhs=BANDP[64:128, g, :TW],
                                                                                 start=False, stop=True)),
                                          reads=[pvn, "BANDP"], writes=[bn(bank)], inc=(j == ntile - 1), cost=c_pe(TW)))
                    ops.append(OP("act", lambda: A_.copy(out=ptb[:, :ntok], in_=BK[bank][:, :ntok]), reads=[bn(bank)], writes=[ptn],
                                  cost=c_act(ntok)))
                    yield ("group", ops)
                    yield ("free", [bank])
                    bank, = yield ("alloc", 1)
                    yield ("group", [
                        OP("pe", lambda: PE_.matmul(BK[bank][:, :ntok], lhsT=WPOOL[:, g, :], rhs=ptb[:, :ntok], start=True, stop=True),
                           reads=["WPOOL", ptn], writes=[bn(bank)], cost=c_pe(ntok)),
                        OP("dve", lambda: V.scalar_tensor_tensor(out=MIXT[:, g, :ntok], in0=BK[bank][:, :ntok], scalar=PSC[:, g:g + 1],
                                                                 in1=sg[:, :ntok], op0=ALU.mult, op1=ALU.mult),
                           reads=[bn(bank), "PSC", sgn], writes=["MIXT/p%d" % g], cost=c_dve(ntok, True)),
                    ])
                    yield ("free", [bank])
                yield ("set", ev("POOL", n, lane))

        def lane_POOLJOIN():
            for st in STs:
                n = st["n"]
                yield ("wait", [ev("POOL", n, 0), ev("POOL", n, 1)])
                if not st["pre"] and not st["sample"]:
                    yield OP("pool", lambda: G_.tensor_copy(out=UPREV[:], in_=UTOK[:, 3, :]), reads=["UTOK"], writes=["UPREV"],
                             cost=c_pool(512))
                yield ("set", ev("POOLall", n))

        def lane_H1():
            for st in STs:
                n, TW, ntile, ntok, BL, pre = st["n"], st["TW"], st["ntile"], st["ntok"], st["BL"], st["pre"]
                matt, mattn = (MSATT, "MSATT") if st["sample"] else (MATT, "MATT")
                kdt, kdtp = KDTB[st["bs"]], KDTN[st["bs"]]
                yield ("wait", [ev("F", n, h) for h in range(4)])
                for j in range(ntile):
                    c0 = j * TW
                    par = j % 2
                    vt, vtn, sh, shn, kd, kdn, att, attn = VTOK[par], "VTOK%d" % par, SH[par], "SH%d" % par, KD[par], "KD%d" % par, ATT[par], "ATT%d" % par
                    tb = tile_back2(n, j)
                    if tb is not None:
                        yield ("wait", [ev("H", tb[0], tb[1])])
                    bank, = yield ("alloc", 1)
                    ops = tm_block(bank, st, j, C_V)
                    ops.append(OP("act", lambda: A_.mul(out=vt[:TW, :], in_=BK[bank][:TW, :], mul=-0.5), reads=[bn(bank)],
                                  writes=[vtn], cost=c_act(512)))
                    yield ("group", ops)
                    yield ("free", [bank])
                    if not pre:
                        bank, = yield ("alloc", 1)
                        ops = tm_block(bank, st, j, C_GH)
                        ops.append(OP("act", lambda: A_.activation(out=sh[:TW, :], in_=BK[bank][:TW, :], func=AF.Silu),
                                      reads=[bn(bank)], writes=[shn], cost=c_act(512)))
                        yield ("group", ops)
                        yield ("free", [bank])
                    if j == ntile - 1:
                        yield ("set", ev("xntfree", n))
                    bank, = yield ("alloc", 1)
                    ops = [OP("pe", (lambda h=h: PE_.transpose(out=BKB[bank][:TW, h * P:(h + 1) * P], in_=kdt[:, h, c0:c0 + TW], identity=IDB[:, :])),
                              reads=[kdtp + "/%d" % h, "IDB"], writes=[bn(bank)], inc=(h == 3), cost=c_pe(P)) for h in range(4)]
                    ops.append(OP("act", lambda: A_.copy(out=kd[:TW, :], in_=BKB[bank][:TW, 0:512]), reads=[bn(bank)], writes=[kdn],
                                  cost=c_act(512)))
                    yield ("group", ops)
                    yield ("free", [bank])
                    if not pre:
                        bank, = yield ("alloc", 1)
                        ops = [OP("pe", (lambda h=h: PE_.matmul(BK[bank][:TW, h * TW:(h + 1) * TW], lhsT=KE[:, h, c0:c0 + TW],
                                                                rhs=QE[:, h, c0:c0 + TW], start=True, stop=True)),
                                  reads=["KE/%d" % h, "QE/%d" % h], writes=[bn(bank)], inc=(h == 3), cost=c_pe(TW)) for h in range(4)]
                        ops.append(OP("dve", lambda: V.tensor_tensor(out=att[:TW, :, :TW],
                                                                     in0=BK[bank][:TW, 0:4 * TW].rearrange("p (h t) -> p h t", h=4),
                                                                     in1=matt[:TW, :TW].unsqueeze(1).to_broadcast([TW, 4, TW]),
                                                                     op=ALU.mult), reads=[bn(bank), mattn], writes=[attn],
                                      cost=c_dve(4 * TW, True)))
                        yield ("group", ops)
                        yield ("free", [bank])
                    yield ("set", ev("H1", n, j))

        def lane_H2():
            for st in STs:
                n, TW, ntile, ntok, BL, pre = st["n"], st["TW"], st["ntile"], st["ntok"], st["BL"], st["pre"]
                dec, decp = DECB[st["bs"]], DECN[st["bs"]]
                if st["sample"]:
                    if prev_mainkind(n) is not None:
                        yield ("wait", [ev("Zall", prev_mainkind(n)), ev("Hdone", n - 1)])
                    for q in range(2):
                        sid = 1 + q
                        yield DMA("SIN%d" % q, S32[sid], state_smp[q].rearrange("h k v -> k h v"),
                                  writes=[snm(sid, h) for h in range(4)])
                        for h in range(4):
                            yield OP("dve", (lambda h=h, sid=sid: V.tensor_scalar(out=S32[sid][:, h, :], in0=S32[sid][:, h, :],
                                                                                  scalar1=IOML[:, h:h + 1], scalar2=None, op0=ALU.mult)),
                                     reads=[snm(sid, h), "IOML"], writes=[snm(sid, h)], cost=c_dve(P))
                for j in range(ntile):
                    c0 = j * TW
                    par = j % 2
                    vt, vtn, sh, shn, kd, kdn, att, attn = VTOK[par], "VTOK%d" % par, SH[par], "SH%d" % par, KD[par], "KD%d" % par, ATT[par], "ATT%d" % par
                    wl = [ev("H1", n, j)]
                    pm = prev_mainkind(n)
                    if not pre and pm is not None and j < STs[pm]["ntile"]:
                        wl.append(ev("Ymmt", pm, j))
                    yield ("wait", wl)
                    bo, bu = yield ("alloc", 2)
                    for bi, (r0, ln, sid) in enumerate(st["blocks"][j]):
                        blk = (c0 + r0) // BL
                        ops = [OP("pe", (lambda h=h: PE_.matmul(BK[bu][:, h * P:(h + 1) * P], lhsT=kd[r0:r0 + ln, h * P:(h + 1) * P],
                                                                rhs=vt[r0:r0 + ln, h * P:(h + 1) * P], start=True, stop=True)),
                                  reads=[kdn, vtn], writes=[bn(bu)], inc=(h == 3), cost=c_pe(P)) for h in range(4)]
                        yield ("group", ops)
                        if not pre:
                            for h in range(4):
                                if h < 2:
                                    yield OP("pool", (lambda h=h: G_.tensor_tensor(out=SB[sid][:, h, :], in0=S32[sid][:, h, :],
                                                                                   in1=EMID[:, h, blk:blk + 1].to_broadcast([P, P]),
                                                                                   op=ALU.mult)),
                                             reads=[snm(sid, h), "EMID/%d" % h], writes=[sbnm(sid, h)], cost=c_pool(P))
                                else:
                                    yield OP("act", (lambda h=h: A_.activation(out=SB[sid][:, h, :], in_=S32[sid][:, h, :], func=AF.Copy,
                                                                               scale=EMID[:, h, blk:blk + 1])),
                                             reads=[snm(sid, h), "EMID/%d" % h], writes=[sbnm(sid, h)], cost=c_act(P))
                            ops = []
                            for h in range(4):
                                ops.append(OP("pe", (lambda h=h: PE_.matmul(BK[bo][r0:r0 + ln, h * P:(h + 1) * P], lhsT=att[:TW, h, r0:r0 + ln],
                                                                            rhs=vt[:TW, h * P:(h + 1) * P], start=True, stop=False)),
                                              reads=[attn, vtn], writes=[bn(bo)], inc=False, cost=c_pe(P)))
                                ops.append(OP("pe", (lambda h=h: PE_.matmul(BK[bo][r0:r0 + ln, h * P:(h + 1) * P], lhsT=QE[:, h, c0 + r0:c0 + r0 + ln],
                                                                            rhs=SB[sid][:, h, :], start=False, stop=True)),
                                              reads=["QE/%d" % h, sbnm(sid, h)], writes=[bn(bo)], inc=(h == 3), cost=c_pe(P)))
                            yield ("group", ops)
                        for h in range(4):
                            yield OP("dve", (lambda h=h: V.scalar_tensor_tensor(out=S32[sid][:, h, :], in0=S32[sid][:, h, :],
                                                                                scalar=dec[:, h, blk:blk + 1], in1=BK[bu][:, h * P:(h + 1) * P],
                                                                                op0=ALU.mult, op1=ALU.add)),
                                     reads=[snm(sid, h), decp + "/%d" % h, bn(bu)], writes=[snm(sid, h)], cost=c_dve(P, True))
                    if j == ntile - 1:
                        yield ("set", ev("qkdfree", n))
                    if pre:
                        yield ("free", [bo, bu])
                        yield ("set", ev("H", n, j))
                        continue
                    for h in range(4):
                        yield OP("act", (lambda h=h: A_.activation(out=OG[:TW, h * P:(h + 1) * P], in_=BK[bo][:TW, h * P:(h + 1) * P],
                                                                   func=AF.Square, accum_out=SMH[:TW, h:h + 1])),
                                 reads=[bn(bo)], writes=["OG/%d" % h, "SMH/s%d" % h], cost=c_act(P))
                    yield OP("dve", lambda: V.tensor_scalar(out=SMH[:TW, 4:8], in0=SMH[:TW, 0:4], scalar1=1.0 / P, scalar2=EPS,
                                                            op0=ALU.mult, op1=ALU.add), reads=["SMH/s%d" % h for h in range(4)],
                             writes=["SMH/r"], cost=200)
                    yield OP("pool", lambda: G_.tensor_tensor(out=SMH[:TW, 8:12], in0=SMH[:TW, 4:8], in1=MH[:TW, 0:4], op=ALU.pow),
                             reads=["SMH/r", "MH"], writes=["SMH/o"], cost=600)
                    for h in range(4):
                        yield OP("dve", (lambda h=h: V.scalar_tensor_tensor(out=OG[:TW, h * P:(h + 1) * P], in0=BK[bo][:TW, h * P:(h + 1) * P],
                                                                            scalar=SMH[:TW, 8 + h:9 + h], in1=sh[:TW, h * P:(h + 1) * P],
                                                                            op0=ALU.mult, op1=ALU.mult)),
                                 reads=[bn(bo), "SMH/o", shn], writes=["OG/%d" % h], cost=c_dve(P, True))
                    yield ("free", [bo, bu])
                    bank, = yield ("alloc", 1)
                    ops = [OP("pe", (lambda h=h: PE_.transpose(out=BKB[bank][:, h * TW:(h + 1) * TW], in_=OG[:TW, h * P:(h + 1) * P],
                                                               identity=IDB[:TW, :TW])),
                              reads=["OG/%d" % h, "IDB"], writes=[bn(bank)], inc=(h == 3), cost=c_pe(TW)) for h in range(4)]
                    ops.append(OP("act", lambda: A_.copy(out=MIXT[:, 4:8, c0:c0 + TW],
                                                         in_=BKB[bank][:, 0:4 * TW].rearrange("p (h t) -> p h t", h=4)),
                                  reads=[bn(bank)], writes=["MIXT/h%d" % j], cost=c_act(4 * TW)))
                    yield ("group", ops)
                    yield ("free", [bank])
                    yield ("set", ev("H", n, j))
                yield ("set", ev("Hdone", n))

        folded = [False]
        smp_n = [next((st["n"] for st in STs if st["sample"]), None)]

        def lane_Y():
            for st in STs:
                n, TW, ntile, pre = st["n"], st["TW"], st["ntile"], st["pre"]
                if pre:
                    yield ("set", ev("Ymm", n))
                    continue
                if not folded[0]:
                    folded[0] = True
                    yield OP("dve", lambda: V.tensor_scalar(out=WOUT[:, 4:8, :], in0=WOUT[:, 4:8, :], scalar1=HGN[:, 0:1], scalar2=None,
                                                            op0=ALU.mult), reads=["WOUT", "HGN"], writes=["WOUT"], cost=c_dve(4096, fast=4))
                    yield OP("dve", lambda: V.tensor_scalar(out=WPLE[:], in0=WPLE[:], scalar1=0.5, scalar2=None, op0=ALU.mult),
                             reads=["WPLE"], writes=["WPLE"], cost=c_dve(2048, fast=4))
                yield ("wait", [ev("POOLall", n)])
                for j in range(ntile):
                    c0 = j * TW
                    par = j % 2
                    xr, xrn = XR[par], "XR%d" % par
                    x1t, x1tn = X1T[par], "X1T%d" % par
                    yield ("wait", [ev("H", n, j)])
                    tb = tile_back2(n, j, mainonly=True)
                    if tb is not None:
                        yield ("wait", [ev("Z", tb[0], tb[1])])
                    if smp_n[0] is not None and smp_n[0] < n:
                        yield ("wait", [ev("stateout", smp_n[0])])
                    yield DMA("XR%d" % par, xr[:TW], st["x"][c0:c0 + TW, :], writes=[xrn])
                    b0, b1 = yield ("alloc", 2)
                    ops = []
                    for hf, bank in enumerate((b0, b1)):
                        for c in range(8):
                            ops.append(OP("pe", (lambda c=c, hf=hf, bank=bank: PE_.matmul(BK[bank][:TW, :], lhsT=MIXT[:, c, c0:c0 + TW],
                                                                                          rhs=WOUT[:, c, hf * 512:(hf + 1) * 512],
                                                                                          start=(c == 0), stop=(c == 7))),
                                          reads=["MIXT/p%d" % c if c < 4 else "MIXT/h%d" % j, "WOUT"], writes=[bn(bank)], inc=(c == 7),
                                          cost=c_pe(512)))
                    yield ("group", ops)
                    yield ("set", ev("Ymmt", n, j))
                    if j == ntile - 1:
                        yield ("set", ev("Ymm", n))
                    for hf, bank in enumerate((b0, b1)):
                        yield OP("act", (lambda hf=hf, bank=bank: A_.activation(out=X1B[:TW, hf * 512:(hf + 1) * 512], in_=BK[bank][:TW, :],
                                                                                func=AF.Square, accum_out=SMY[:TW, hf:hf + 1])),
                                 reads=[bn(bank)], writes=["X1B/%d" % hf, "SMY/s%d" % hf], cost=c_act(512))
                    yield OP("dve", lambda: V.tensor_tensor(out=SMY[:TW, 2:3], in0=SMY[:TW, 0:1], in1=SMY[:TW, 1:2], op=ALU.add),
                             reads=["SMY/s0", "SMY/s1"], writes=["SMY/a"], cost=200)
                    yield OP("dve", lambda: V.tensor_scalar(out=SMY[:TW, 3:4], in0=SMY[:TW, 2:3], scalar1=1.0 / D, scalar2=EPS,
                                                            op0=ALU.mult, op1=ALU.add), reads=["SMY/a"], writes=["SMY/b"], cost=200)
                    yield OP("pool", lambda: G_.tensor_tensor(out=SMY[:TW, 4:5], in0=SMY[:TW, 3:4], in1=MH[:TW, 0:1], op=ALU.pow),
                             reads=["SMY/b", "MH"], writes=["SMY/c"], cost=500)
                    for hf, bank in enumerate((b0, b1)):
                        yield OP("dve", (lambda hf=hf, bank=bank: V.scalar_tensor_tensor(out=T1[:TW, hf * 512:(hf + 1) * 512], in0=BK[bank][:TW, :],
                                                                                         scalar=SMY[:TW, 4:5], in1=GPOST[:TW, hf * 512:(hf + 1) * 512],
                                                                                         op0=ALU.mult, op1=ALU.mult)),
                                 reads=[bn(bank), "SMY/c", "GPOST"], writes=["T1/%d" % hf], cost=c_dve(512, True))
                    yield ("free", [b0, b1])
                    yield OP("dve", lambda: V.tensor_tensor(out=xr[:TW], in0=T1[:TW], in1=xr[:TW], op=ALU.add),
                             reads=["T1", xrn], writes=[xrn], cost=c_dve(D))
                    yield OP("act", lambda: A_.copy(out=X1B[:TW], in_=xr[:TW]), reads=[xrn], writes=["X1B"],
                             cost=c_act(D))
                    bank, = yield ("alloc", 1)
                    ops = [OP("pe", (lambda k=k: PE_.transpose(out=BKB[bank][:, k * TW:(k + 1) * TW], in_=X1B[:TW, k * P:(k + 1) * P],
                                                               identity=IDB[:TW, :TW])),
                              reads=["X1B", "IDB"], writes=[bn(bank)], inc=(k == KD_ - 1), cost=c_pe(TW)) for k in range(KD_)]
                    ops.append(OP("act", lambda: A_.copy(out=x1t[:, :, :TW], in_=BKB[bank][:, 0:KD_ * TW].rearrange("p (k t) -> p k t", k=KD_)),
                                  reads=[bn(bank)], writes=[x1tn], cost=c_act(KD_ * TW)))
                    yield ("group", ops)
                    yield ("free", [bank])
                    yield ("set", ev("Y", n, j))

        def lane_Z():
            for st in STs:
                n, TW, ntile, pre = st["n"], st["TW"], st["ntile"], st["pre"]
                if pre:
                    yield ("set", ev("Zall", n))
                    continue
                if st["sample"]:
                    yield DMA("PIN", PIN[:TW, 0, :], st["p"], writes=["PIN"])
                else:
                    yield DMA("PIN", PIN[:], st["p"].rearrange("(j p) d -> p j d", p=P), writes=["PIN"])
                for j in range(ntile):
                    c0 = j * TW
                    par = j % 2
                    xr, xrn = XR[par], "XR%d" % par
                    x1t, x1tn = X1T[par], "X1T%d" % par
                    pb, pbn, ptt, pttn = PB[par], "PBF%d" % par, PTT[par], "PTT%d" % par
                    yield OP("pool", lambda: G_.tensor_copy(out=pb[:TW], in_=PIN[:TW, j, :]), reads=["PIN"], writes=[pbn], cost=c_pool(DPLE))
                    bank, = yield ("alloc", 1)
                    ops = [OP("pe", (lambda k=k: PE_.transpose(out=BKB[bank][:, k * TW:(k + 1) * TW], in_=pb[:TW, k * P:(k + 1) * P],
                                                               identity=IDB[:TW, :TW])),
                              reads=[pbn, "IDB"], writes=[bn(bank)], inc=(k == 1), cost=c_pe(TW)) for k in range(2)]
                    ops.append(OP("act", lambda: A_.copy(out=ptt[:, :, :TW], in_=BKB[bank][:, 0:2 * TW].rearrange("p (k t) -> p k t", k=2)),
                                  reads=[bn(bank)], writes=[pttn], cost=c_act(2 * TW)))
                    yield ("group", ops)
                    yield ("free", [bank])
                    yield ("wait", [ev("Y", n, j)])
                    for hf in range(2):
                        bg, bw = yield ("alloc", 2)
                        ops = []
                        for k in range(KD_):
                            ops.append(OP("pe", (lambda k=k: PE_.matmul(BK[bg][:TW, :], lhsT=x1t[:, k, :TW], rhs=WG[:, k, hf * 512:(hf + 1) * 512],
                                                                        start=(k == 0), stop=(k == KD_ - 1))),
                                          reads=[x1tn, "WG"], writes=[bn(bg)], inc=(k == KD_ - 1), cost=c_pe(512)))
                        for k in range(2):
                            ops.append(OP("pe", (lambda k=k: PE_.matmul(BK[bw][:TW, :], lhsT=ptt[:, k, :TW], rhs=WPLE[:, k, hf * 512:(hf + 1) * 512],
                                                                        start=(k == 0), stop=(k == 1))),
                                          reads=[pttn, "WPLE"], writes=[bn(bw)], inc=(k == 1), cost=c_pe(512)))
                        ops.append(OP("act", lambda: A_.activation(out=T2[:TW, hf * 512:(hf + 1) * 512], in_=BK[bg][:TW, :], func=AF.Tanh, scale=0.5),
                                      reads=[bn(bg)], writes=["T2/%d" % hf], cost=c_act(512)))
                        ops.append(OP("dve", lambda: V.scalar_tensor_tensor(out=T2[:TW, hf * 512:(hf + 1) * 512], in0=T2[:TW, hf * 512:(hf + 1) * 512],
                                                                            scalar=1.0, in1=BK[bw][:TW, :], op0=ALU.add, op1=ALU.mult),
                                      reads=["T2/%d" % hf, bn(bw)], writes=["T2/%d" % hf], cost=c_dve(512, True)))
                        yield ("group", ops)
                        yield ("free", [bg, bw])
                    yield OP("pool", lambda: G_.tensor_tensor(out=T2[:TW, 0:512], in0=T2[:TW, 0:512], in1=xr[:TW, 0:512], op=ALU.add),
                             reads=["T2/0", xrn], writes=["T2/0"], cost=c_pool(512))
                    yield OP("dve", lambda: V.tensor_tensor(out=T2[:TW, 512:D], in0=T2[:TW, 512:D], in1=xr[:TW, 512:D], op=ALU.add),
                             reads=["T2/1", xrn], writes=["T2/1"], cost=c_dve(512))
                    yield DMA("OUT", st["y"][c0:c0 + TW, :], T2[:TW], reads=["T2"], eng="pool")
                    yield ("set", ev("Z", n, j))
                if st.get("final_state") or st["sample"]:
                    sids = [1, 2] if st["sample"] else [0]
                    yield ("wait", [ev("Hdone", n)])
                    for qi, sid in enumerate(sids):
                        for h in range(4):
                            yield OP("dve", (lambda h=h, sid=sid: V.tensor_scalar(out=T2[:, h * P:(h + 1) * P], in0=S32[sid][:, h, :],
                                                                                  scalar1=OML[:, h:h + 1], scalar2=None, op0=ALU.mult)),
                                     reads=[snm(sid, h), "OML"], writes=["T2"], cost=c_dve(P))
                        dst = state_s[qi] if st["sample"] else state_p
                        yield DMA("ST%d_%d" % (n, qi), dst.rearrange("h k v -> k h v"),
                                  T2[:, 0:512].rearrange("p (h v) -> p h v", h=4), reads=["T2"])
                    yield ("set", ev("stateout", n))
                yield ("set", ev("Zall", n))

        run = Runner(s, list(range(NB)))
        for nm, tt in WLOAD:
            run.t_w[nm] = tt
        run.run([("H2", lane_H2()), ("F0", lane_F(0)), ("F1", lane_F(1)), ("H1", lane_H1()), ("A0", lane_A(0)), ("A1", lane_A(1)),
                 ("U", lane_U()), ("P0", lane_POOL(0)), ("P1", lane_POOL(1)), ("PJ", lane_POOLJOIN()),
                 ("Y", lane_Y()), ("Z", lane_Z())])
        s.finish("sp")
    return nc, s


def _band_consts():
    W = (2, 4, 8, 16)
    cur = np.zeros((P, 4, P), np.float32)
    prev = np.zeros((P, 4, P), np.float32)
    first = np.zeros((P, 4, P), np.float32)
    for g, w in enumerate(W):
        for t in range(P):
            for sg in range(t - w + 1, t + 1):
                if sg >= 0:
                    cur[sg, g, t] += 1.0 / w
                else:
                    prev[P + sg, g, t] += 1.0 / w
            cur[t, g, t] -= 1.0
            cnt = min(w, t + 1)
            for sg in range(max(0, t - w + 1), t + 1):
                first[sg, g, t] += 1.0 / cnt
            first[t, g, t] -= 1.0
    scur = np.zeros((64, 4, 64), np.float32)
    sprev = np.zeros((64, 4, 64), np.float32)
    for g, w in enumerate(W):
        for q in range(2):
            o = 32 * q
            for i in range(16):
                e = 15 + i
                for ee in range(e - w + 1, e + 1):
                    if ee >= 15:
                        scur[o + ee - 15, g, o + i] += 1.0 / w
                    else:
                        sprev[o + ee, g, o + i] += 1.0 / w
                scur[o + i, g, o + i] -= 1.0
    matt = np.zeros((P, P), np.float32)
    for s_ in range(P):
        for t in range(P):
            if s_ // 64 == t // 64 and s_ <= t:
                matt[s_, t] = 1.0
    msatt = np.zeros((64, 64), np.float32)
    for s_ in range(64):
        for t in range(64):
            if s_ // 16 == t // 16 and s_ <= t and (s_ // 16) in (0, 2):
                msatt[s_, t] = 1.0
    rmask = np.ones((P, 512), np.float32)
    rmask[:, ::64] = 0.0
    rmask_s = np.ones((P, 64), np.float32)
    rmask_s[:, ::16] = 0.0
    return dict(band_cur=cur, band_prev=prev, band_first=first, band_scur=scur, band_sprev=sprev,
                mask_att=matt, mask_satt=msatt, rmask=rmask, rmask_s=rmask_s)


_PROG = {}


def _get_prog(npre, nmain):
    key = (npre, nmain)
    if key not in _PROG:
        _PROG[key] = build_program(npre, nmain)[0]
    return _PROG[key]


def make_in_maps(inputs, npre=HALF, nmain=HALF):
    f = lambda a: np.ascontiguousarray(np.asarray(a, dtype=np.float32))
    xp = f(inputs["x_prompt"])
    pp = f(inputs["p_prompt"])[0]
    xs = f(inputs["x_sample"])
    ps_ = f(inputs["p_sample"])[0]
    cache = f(inputs["cache_pool"])[0]
    st = f(inputs["state_hgrn"])[0]
    consts = _band_consts()
    shared = dict(
        w_in=f(inputs["w_in"])[0], w_pool=f(inputs["w_pool"])[0], pool_scale=f(inputs["pool_scale"])[0],
        hg_norm=f(inputs["hg_norm"])[0], w_out=f(inputs["w_out"])[0], norm_pre=f(inputs["norm_pre"])[0],
        norm_post=f(inputs["norm_post"])[0], w_ple=f(inputs["w_ple"])[0], w_gate=f(inputs["w_ple_gate"])[0],
        lb_logits=f(inputs["lb_logits"]),
    )
    in_maps = []
    for c in range(NCORES):
        b = c % 4
        second = c >= 4
        m = dict(shared)
        m.update({k: v for k, v in consts.items()})
        if second:
            m["x_pre"] = f(xp[b, HALF - npre:HALF]) if npre else np.zeros((P, D), np.float32)
            m["x_main"] = f(xp[b, HALF:HALF + nmain])
            m["p_main"] = f(pp[b, HALF:HALF + nmain])
            m["band_first"] = consts["band_cur"]
        else:
            m["x_pre"] = np.zeros((max(npre, P), D), np.float32)
            m["x_main"] = f(xp[b, 0:nmain])
            m["p_main"] = f(pp[b, 0:nmain])
        xsm = np.zeros((64, D), np.float32)
        psm = np.zeros((64, DPLE), np.float32)
        csm = np.zeros((64, 512), np.float32)
        for q in range(2):
            i = 2 * c + q
            xsm[32 * q:32 * q + 16] = xs[i]
            psm[32 * q:32 * q + 16] = ps_[i]
            csm[32 * q:32 * q + 15] = cache[i]
        m["x_smp"] = xsm
        m["p_smp"] = psm
        m["cache_smp"] = csm
        m["state_smp"] = f(st[2 * c:2 * c + 2])
        in_maps.append(m)
    return in_maps


def kernel(**inputs):
    nc = _get_prog(HALF, HALF)
    in_maps = make_in_maps(inputs)
    res = run_bass_kernel_spmd(nc, in_maps, core_ids=list(range(NCORES)))
    r = res.results
    y_prompt = np.zeros((4, SEQ, D), np.float32)
    y_sample = np.zeros((16, 16, D), np.float32)
    pool_p = np.zeros((1, 4, 15, 512), np.float32)
    hg_p = np.zeros((1, 4, 4, P, P), np.float32)
    pool_s = np.zeros((1, 16, 15, 512), np.float32)
    hg_s = np.zeros((1, 16, 4, P, P), np.float32)
    for c in range(NCORES):
        b = c % 4
        if c < 4:
            y_prompt[b, :HALF] = r[c]["y_main"]
        else:
            y_prompt[b, HALF:] = r[c]["y_main"]
            pool_p[0, b] = r[c]["pool_p"][P - 15:P]
            hg_p[0, b] = r[c]["state_p"]
        for q in range(2):
            i = 2 * c + q
            y_sample[i] = r[c]["y_smp"][32 * q:32 * q + 16]
            pool_s[0, i] = r[c]["pool_s"][32 * q + 1:32 * q + 16]
            hg_s[0, i] = r[c]["state_s"][q]
    return (y_prompt, y_sample, pool_p, hg_p, pool_s, hg_s)
```

```python
import contextlib
import os
import numpy as np
import concourse.bass as bass
import concourse.mybir as mybir
from concourse.bass_utils import run_bass_kernel_spmd

F32 = mybir.dt.float32
BF16 = mybir.dt.bfloat16
AF = mybir.ActivationFunctionType
ALU = mybir.AluOpType

P = 128
D = 1024
KD_ = 8
DIN = 3072
DPLE = 256
EPS = 1e-6
SEQ = 8192
HALF = 4096
NCORES = 8
C_U, C_GP, C_Q, C_F, C_V, C_GH = 0, 512, 1024, 1536, 2048, 2560


class _StopBuild(Exception):
    pass


_KSTOP = float(os.environ.get("KSTOP", "999"))


def ck(n):
    if n >= _KSTOP:
        raise _StopBuild()


class Sched:
    EPOCH = 30000

    def __init__(self, nc):
        self.nc = nc
        self.eng = {"pe": nc.tensor, "act": nc.scalar, "dve": nc.vector,
                    "pool": nc.gpsimd, "sp": nc.sync}
        self.cur_sem = {}
        self.cnt = {}
        self.nsem = 0
        self.sem_objs = {}
        for e in self.eng:
            self._new_epoch(e)
        self.waited = {}
        self.lastw = {}
        self.readers = {}
        self.children = {}
        self.pend_r = {e: [] for e in self.eng}
        self.pend_w = {e: [] for e in self.eng}
        self.dma_sem = {}
        self.dma_cnt = {}
        self.n_wait = 0
        self.n_ops = 0

    def _alloc(self, name):
        h = self.nc.alloc_semaphore(name=name)
        self.nsem += 1
        self.sem_objs[name] = h
        return name

    def _new_epoch(self, e):
        k = self._alloc("pg_%s_%d" % (e, self.nsem))
        self.cur_sem[e] = k
        self.cnt[e] = 0

    def _rel(self, name):
        if "/" in name:
            par = name.split("/")[0]
            ch = self.children.setdefault(par, [])
            if name not in ch:
                ch.append(name)
            return (name, par)
        return (name,) + tuple(self.children.get(name, ()))

    def _wait(self, e, ticket):
        key, val, src = ticket
        if src == e and e == "pe":
            return
        if self.waited.get((e, key), 0) >= val:
            return
        self.eng[e].wait_ge(self.sem_objs[key], val)
        self.waited[(e, key)] = val
        self.n_wait += 1

    def _deps(self, e, reads, writes):
        deps = []
        for r in reads:
            for nm in self._rel(r):
                t = self.lastw.get(nm)
                if t is not None:
                    deps.append(t)
                if nm[0] == "P" and nm[1] == "B" and nm[2:].isdigit():
                    for t in self.readers.get(nm, ()):
                        if t[2] != e:
                            deps.append(t)
        for w in writes:
            for nm in self._rel(w):
                t = self.lastw.get(nm)
                if t is not None:
                    deps.append(t)
                deps.extend(self.readers.get(nm, ()))
        return deps

    def _check_pending(self, e, reads, writes):
        for oe in self.eng:
            if oe == e:
                continue
            for r in list(reads) + list(writes):
                for nm in self._rel(r):
                    if nm in self.pend_w[oe]:
                        raise RuntimeError("dependency on pending write %s (%s)" % (nm, oe))
            for w in writes:
                for nm in self._rel(w):
                    if nm in self.pend_r[oe]:
                        raise RuntimeError("WAR on pending read %s (%s)" % (nm, oe))

    def _record(self, t, reads, writes):
        for w in writes:
            self.lastw[w] = t
            self.readers[w] = []
            if "/" not in w:
                for c in self.children.get(w, ()):
                    self.readers[c] = []
        for r in reads:
            if r not in writes:
                self.readers.setdefault(r, []).append(t)

    def op(self, e, fn, reads=(), writes=(), inc=True):
        self.n_ops += 1
        self._check_pending(e, reads, writes)
        for t in self._deps(e, reads, writes):
            self._wait(e, t)
        ins = fn()
        self.pend_r[e].extend(reads)
        self.pend_w[e].extend(writes)
        if inc:
            if self.cnt[e] >= self.EPOCH:
                self._new_epoch(e)
            self.cnt[e] += 1
            key = self.cur_sem[e]
            ins.then_inc(self.sem_objs[key], 1)
            t = (key, self.cnt[e], e)
            self._record(t, self.pend_r[e], self.pend_w[e])
            self.pend_r[e] = []
            self.pend_w[e] = []
        return ins

    def dma(self, chan, out, in_, reads=(), writes=(), eng="sp", nowait=False, **kw):
        self.n_ops += 1
        if chan not in self.dma_sem:
            self.dma_sem[chan] = self._alloc("dma_%s" % chan)
            self.dma_cnt[chan] = 0
        self._check_pending(eng, reads, writes)
        if not nowait:
            for t in self._deps("dma", reads, writes):
                self._wait(eng, t)
        key = self.dma_sem[chan]
        ins = self.eng[eng].dma_start(out=out, in_=in_, **kw)
        self.dma_cnt[chan] += 16
        ins.then_inc(self.sem_objs[key], 16)
        t = (key, self.dma_cnt[chan], "dma:" + chan)
        self._record(t, reads, writes)
        return t

    def finish(self, e="sp"):
        for chan, key in self.dma_sem.items():
            if self.dma_cnt[chan] > 0:
                self._wait(e, (key, self.dma_cnt[chan], "dma:" + chan))


class Runner:
    LAT = 320.0
    SLACK = 600.0

    def __init__(self, s, banks):
        self.s = s
        self.free = list(banks)
        self.events = set()
        self.t_eng = {e: 0.0 for e in s.eng}
        self.t_w = {}
        self.t_r = {}

    def _est(self, op):
        kind = op[0]
        if kind == "group":
            return self._est(op[1][0])
        if kind == "dma":
            _, chan, out, in_, reads, writes, eng, kw, cost = op
            e = eng
        else:
            _, e, fn, reads, writes, inc, cost = op
        t = self.t_eng[e]
        for r in reads:
            t = max(t, self.t_w.get(r, 0.0))
        for w in writes:
            t = max(t, self.t_w.get(w, 0.0), self.t_r.get(w, 0.0))
        return t

    def _emit(self, op):
        kind = op[0]
        if kind == "group":
            for o in op[1]:
                self._emit(o)
            return
        t = self._est(op)
        if kind == "dma":
            _, chan, out, in_, reads, writes, eng, kw, cost = op
            self.s.dma(chan, out, in_, reads=reads, writes=writes, eng=eng, **kw)
            self.t_eng[eng] = t + 60.0
            end = t + cost
        else:
            _, e, fn, reads, writes, inc, cost = op
            self.s.op(e, fn, reads=reads, writes=writes, inc=inc)
            self.t_eng[e] = t + cost
            end = t + cost + self.LAT
        for w in writes:
            self.t_w[w] = end
        for r in reads:
            self.t_r[r] = max(self.t_r.get(r, 0.0), end)

    def run(self, gens):
        ths = [dict(g=g, head=None, send=None, done=False, name=n) for n, g in gens]

        def step(th):
            try:
                th["head"] = th["g"].send(th["send"])
            except StopIteration:
                th["head"] = None
                th["done"] = True
            th["send"] = None

        for th in ths:
            step(th)
        while True:
            progressed = True
            while progressed:
                progressed = False
                for th in ths:
                    while not th["done"]:
                        h = th["head"]
                        k = h[0]
                        if k == "free":
                            self.free.extend(h[1])
                            step(th)
                            progressed = True
                        elif k == "set":
                            self.events.add(h[1])
                            step(th)
                            progressed = True
                        elif k == "wait":
                            if all(ev in self.events for ev in h[1]):
                                step(th)
                                progressed = True
                            else:
                                break
                        elif k == "alloc":
                            if len(self.free) >= h[1]:
                                th["send"] = [self.free.pop(0) for _ in range(h[1])]
                                step(th)
                                progressed = True
                            else:
                                break
                        else:
                            break
            cands = [th for th in ths if not th["done"] and th["head"][0] in ("op", "dma", "group")]
            if not cands:
                if all(th["done"] for th in ths):
                    return
                raise RuntimeError("runner deadlock: " + str([(th["name"], th["head"][:2]) for th in ths if not th["done"]]))
            ests = [(self._est(th["head"]), i, th) for i, th in enumerate(cands)]
            tmin = min(e[0] for e in ests)
            th = min((e for e in ests if e[0] <= tmin + self.SLACK), key=lambda e: e[1])[2]
            self._emit(th["head"])
            step(th)


def OP(e, fn, reads=(), writes=(), inc=True, cost=300.0):
    return ("op", e, fn, tuple(reads), tuple(writes), inc, cost)


def DMA(chan, out, in_, reads=(), writes=(), eng="sp", cost=3000.0, **kw):
    return ("dma", chan, out, in_, tuple(reads), tuple(writes), eng, kw, cost)


def c_pe(n):
    return 30.0 + 0.45 * n


def c_act(f):
    return 260.0 + 0.85 * f


def c_dve(f, psum=False, fast=1.0):
    return (220.0 if psum else 160.0) + 1.05 * f / fast


def c_pool(f):
    return 300.0 + 2.3 * f


def build_program(npre, nmain, do_sample=True):
    nc = bass.Bass("TRN2", target_bir_lowering=False)

    def din(name, shape):
        return nc.dram_tensor(name, list(shape), F32, kind="ExternalInput").ap()

    def dout(name, shape):
        return nc.dram_tensor(name, list(shape), F32, kind="ExternalOutput").ap()

    x_pre = din("x_pre", [max(npre, P), D])
    x_main = din("x_main", [nmain, D])
    p_main = din("p_main", [nmain, DPLE])
    x_smp = din("x_smp", [64, D])
    p_smp = din("p_smp", [64, DPLE])
    cache_smp = din("cache_smp", [64, 512])
    state_smp = din("state_smp", [2, 4, P, P])
    w_in = din("w_in", [D, DIN])
    w_pool = din("w_pool", [4, P, P])
    pool_scale = din("pool_scale", [512])
    hg_norm = din("hg_norm", [P])
    w_out = din("w_out", [D, D])
    norm_pre = din("norm_pre", [D])
    norm_post = din("norm_post", [D])
    w_ple = din("w_ple", [DPLE, D])
    w_gate = din("w_gate", [D, D])
    lb_logits = din("lb_logits", [2, 512])
    band_cur = din("band_cur", [P, 4, P])
    band_prev = din("band_prev", [P, 4, P])
    band_first = din("band_first", [P, 4, P])
    band_scur = din("band_scur", [64, 4, 64])
    band_sprev = din("band_sprev", [64, 4, 64])
    mask_att = din("mask_att", [P, P])
    mask_satt = din("mask_satt", [64, 64])
    rmask = din("rmask", [P, 512])
    rmask_s = din("rmask_s", [P, 64])

    y_main = dout("y_main", [nmain, D])
    y_smp = dout("y_smp", [64, D])
    pool_p = dout("pool_p", [P, 512])
    pool_s = dout("pool_s", [64, 512])
    state_p = dout("state_p", [4, P, P])
    state_s = dout("state_s", [2, 4, P, P])

    s = Sched(nc)
    es = contextlib.ExitStack()

    def T(nm, shp, dt=F32):
        return es.enter_context(nc.sbuf_tensor(nm, list(shp), dt))

    def PS(nm, shp, dt=F32):
        return es.enter_context(nc.psum_tensor(nm, list(shp), dt))

    V, A_, G_, PE_ = nc.vector, nc.scalar, nc.gpsimd, nc.tensor

    with es:
        WIN = T("WIN", [P, KD_, DIN], BF16)
        WOUT = T("WOUT", [P, KD_, D], BF16)
        WG = T("WG", [P, KD_, D], BF16)
        WPLE = T("WPLE", [P, 2, D], BF16)
        WPOOL = T("WPOOL", [P, 4, P], BF16)
        NPRE = T("NPRE", [P, D])
        GPOST = T("GPOST", [P, D])
        PSC = T("PSC", [P, 4])
        HGN = T("HGN", [P, 1])
        LBT = T("LBT", [P, 8])
        OML = T("OML", [P, 4])
        HSC = T("HSC", [P, 4])
        HBI = T("HBI", [P, 4])
        IOML = T("IOML", [P, 4])
        MH = T("MH", [P, 4])
        IDB = T("IDB", [P, P], BF16)
        BANDC = T("BANDC", [P, 4, P], BF16)
        BANDP = T("BANDP", [P, 4, P], BF16)
        BANDF = T("BANDF", [P, 4, P], BF16)
        BANDSC = T("BANDSC", [64, 4, 64], BF16)
        BANDSP = T("BANDSP", [64, 4, 64], BF16)
        MATT = T("MATT", [P, P], BF16)
        MSATT = T("MSATT", [64, 64], BF16)
        RMASK = T("RMASK", [P, 512], BF16)
        RMASKS = T("RMASKS", [P, 64], BF16)
        CACHE = T("CACHE", [64, 512], BF16)

        X = [T("X%d" % i, [P, D]) for i in range(2)]
        XN = [T("XN%d" % i, [P, D], BF16) for i in range(2)]
        SMA = T("SMA", [P, 8])
        XNT = T("XNT", [P, KD_, 512], BF16)
        UTOK = T("UTOK", [P, 4, 512], BF16)
        UPREV = T("UPREV", [P, 512], BF16)
        TH = [T("TH%d" % i, [P, 512]) for i in range(2)]
        GD = [T("GD%d" % i, [P, 512]) for i in range(2)]
        BA = [T("BA%d" % i, [P, 512]) for i in range(2)]
        QE = T("QE", [P, 4, 512], BF16)
        KE = T("KE", [P, 4, 512], BF16)
        KDT = T("KDT", [P, 4, 512], BF16)
        DEC = T("DEC", [P, 4, 8])
        EMID = T("EMID", [P, 4, 8])
        SGL = [T("SGL%d" % i, [P, 512], BF16) for i in range(2)]
        PTBL = [T("PTBL%d" % i, [P, 512], BF16) for i in range(2)]
        MIXT = T("MIXT", [P, 8, 512], BF16)
        VTOK = [T("VTOK%d" % i, [P, 512], BF16) for i in range(2)]
        SH = [T("SH%d" % i, [P, 512], BF16) for i in range(2)]
        KD = [T("KD%d" % i, [P, 512], BF16) for i in range(2)]
        ATT = [T("ATT%d" % i, [P, 4, P], BF16) for i in range(2)]
        OG = T("OG", [P, 512], BF16)
        SMH = T("SMH", [P, 12])
        XR = [T("XR%d" % i, [P, D]) for i in range(2)]
        T1 = T("T1", [P, D])
        X1B = T("X1B", [P, D], BF16)
        X1T = [T("X1T%d" % i, [P, KD_, P], BF16) for i in range(2)]
        S32 = [T("S32_0", [P, 4, P])[:], XR[1][:, 0:512].rearrange("p (h v) -> p h v", h=4),
               XR[1][:, 512:1024].rearrange("p (h v) -> p h v", h=4)]
        SB = [T("SB_0", [P, 4, P], BF16)[:], X1T[1][:, 0:4, :], X1T[1][:, 4:8, :]]

        def snm(sid, h):
            return "S32_0/h%d" % h if sid == 0 else "XR1/s%dh%d" % (sid, h)

        def sbnm(sid, h):
            return "SB_0/h%d" % h if sid == 0 else "X1T1/s%dh%d" % (sid, h)
        SMY = T("SMY", [P, 8])
        PIN = T("PIN", [P, 4, DPLE])
        PB = [T("PB%d" % i, [P, DPLE], BF16) for i in range(2)]
        PTT = [T("PTT%d" % i, [P, 2, P], BF16) for i in range(2)]
        T2 = T("T2", [P, D])

        NB = 8
        BK = [PS("BK%d" % i, [P, 512]) for i in range(NB)]
        BKB = [b[:].bitcast(BF16) for b in BK]

        def bn(i):
            return "PB%d" % i

        s.op("pool", lambda: G_.memset(IDB[:], 1.0), writes=["IDB"])
        s.op("pool", lambda: G_.affine_select(out=IDB[:], in_=IDB[:], pattern=[[-1, P]], compare_op=ALU.is_equal,
                                              fill=0.0, base=0, channel_multiplier=1), reads=["IDB"], writes=["IDB"])
        s.op("pool", lambda: G_.memset(MH[:], -0.5), writes=["MH"])
        s.op("pool", lambda: G_.memset(S32[0], 0.0), writes=["S32_0"])
        s.op("pool", lambda: G_.memset(UPREV[:], 0.0), writes=["UPREV"])
        for nm, t_, src in (("RMASK", RMASK, rmask), ("RMASKS", RMASKS, rmask_s), ("MATT", MATT, mask_att), ("MSATT", MSATT, mask_satt)):
            s.dma(nm, t_[:], src, writes=[nm], eng="pool", nowait=True)
        SEGN = ["WIN/u", "WIN/gp", "WIN/q", "WIN/f", "WIN/v", "WIN/gh"]
        w_in_v = w_in.rearrange("(k p) n -> p k n", p=P)
        WLOAD = []
        tarr = 0.0
        for seg in (3, 4, 0, 2, 1, 5):
            for k0 in (0, 4):
                s.dma("WIN%d" % seg, WIN[:, k0:k0 + 4, seg * 512:(seg + 1) * 512], w_in_v[:, k0:k0 + 4, seg * 512:(seg + 1) * 512],
                      writes=[SEGN[seg]], eng="pool", nowait=True)
            tarr += 22000.0
            WLOAD.append((SEGN[seg], tarr))
        for nm, t_, src in (("BANDC", BANDC, band_cur), ("BANDP", BANDP, band_prev), ("BANDF", BANDF, band_first),
                            ("BANDSC", BANDSC, band_scur), ("BANDSP", BANDSP, band_sprev), ("CACHE", CACHE, cache_smp)):
            s.dma(nm, t_[:], src, writes=[nm], eng="pool", nowait=True)
        for g in range(4):
            s.dma("WPOOL", WPOOL[:, g, :], w_pool[g], writes=["WPOOL"], eng="pool", nowait=True)
        for nm, t_, src, nk in (("WOUT", WOUT, w_out, KD_), ("WG", WG, w_gate, KD_), ("WPLE", WPLE, w_ple, 2)):
            for k in range(nk):
                s.dma(nm, t_[:, k, :], src[k * P:(k + 1) * P, :], writes=[nm], eng="pool", nowait=True)
            tarr += 5000.0 * nk
            WLOAD.append((nm, tarr))
        s.dma("NPRE", NPRE[:], norm_pre.partition_broadcast(P), writes=["NPRE"])
        s.dma("GPOST", GPOST[:], norm_post.partition_broadcast(P), writes=["GPOST"])
        s.dma("PSC", PSC[:], pool_scale.rearrange("(g c) -> c g", c=P), writes=["PSC"], allow_slow_non_contiguous=True)
        s.dma("HGN", HGN[:], hg_norm.rearrange("(c o) -> c o", o=1), writes=["HGN"])
        s.dma("LBT", LBT[:, 0:4], lb_logits[0].rearrange("(h k) -> k h", k=P), writes=["LBT"], allow_slow_non_contiguous=True)
        s.dma("LBT", LBT[:, 4:8], lb_logits[1].rearrange("(h k) -> k h", k=P), writes=["LBT"], allow_slow_non_contiguous=True)
        s.op("dve", lambda: V.tensor_tensor(out=SMA[:, 0:4], in0=LBT[:, 0:4], in1=LBT[:, 4:8], op=ALU.subtract),
             reads=["LBT"], writes=["SMA"])
        s.op("act", lambda: A_.activation(out=SMA[:, 4:8], in_=SMA[:, 0:4], func=AF.Tanh, scale=0.5),
             reads=["SMA"], writes=["SMA"])
        s.op("dve", lambda: V.tensor_scalar(out=OML[:], in0=SMA[:, 4:8], scalar1=-0.5, scalar2=0.5, op0=ALU.mult, op1=ALU.add),
             reads=["SMA"], writes=["OML"])
        s.op("dve", lambda: V.tensor_scalar(out=HSC[:], in0=OML[:], scalar1=0.5, scalar2=None, op0=ALU.mult),
             reads=["OML"], writes=["HSC"])
        s.op("dve", lambda: V.tensor_scalar(out=HBI[:], in0=OML[:], scalar1=-0.5, scalar2=1.0, op0=ALU.mult, op1=ALU.add),
             reads=["OML"], writes=["HBI"])
        s.op("dve", lambda: V.reciprocal(out=IOML[:], in_=OML[:]), reads=["OML"], writes=["IOML"])
        pre_sts, main_sts, smp_sts = [], [], []
        for i in range(npre // 512):
            pre_sts.append(dict(kind="pre", x=x_pre[i * 512:(i + 1) * 512, :], p=None, y=None, ntok=512, TW=P, BL=64,
                                blocks=[[(0, 64, 0), (64, 64, 0)]] * 4, first=False, sample=False,
                                want_u=(3 if i == npre // 512 - 1 else None), udst=None))
        for i in range(nmain // 512):
            last = i == nmain // 512 - 1
            main_sts.append(dict(kind="main", x=x_main[i * 512:(i + 1) * 512, :], p=p_main[i * 512:(i + 1) * 512, :],
                                 y=y_main[i * 512:(i + 1) * 512, :], ntok=512, TW=P, BL=64,
                                 blocks=[[(0, 64, 0), (64, 64, 0)]] * 4, first=(i == 0), sample=False,
                                 want_u=(3 if last else None), udst=(pool_p if last else None), final_state=last))
        if do_sample:
            smp_sts.append(dict(kind="main", x=x_smp, p=p_smp, y=y_smp, ntok=64, TW=64, BL=16,
                                blocks=[[(0, 16, 1), (32, 16, 2)]], first=False, sample=True, want_u=0, udst=pool_s,
                                final_state=False))
        cut = min(3, len(pre_sts))
        STs = pre_sts[:cut] + smp_sts + pre_sts[cut:] + main_sts
        NS = len(STs)
        for i, st in enumerate(pre_sts):
            st["prei"] = i
        XNTB, XNTN = [XNT, MIXT], ["XNT", "MIXT"]
        KDTB, KDTN = [KDT, QE], ["KDT", "QE"]
        DECB, DECN = [DEC, EMID], ["DEC", "EMID"]
        for n, st in enumerate(STs):
            st["n"] = n
            st["ntile"] = st["ntok"] // st["TW"]
            st["pre"] = st["kind"] == "pre"
            st["bs"] = (st["prei"] % 2) if st["pre"] else 0

        def prev_same(n):
            for m in range(n - 1, -1, -1):
                if STs[m]["bs"] == STs[n]["bs"]:
                    return m
            return None

        def prev_alt(n):
            for m in range(n - 1, -1, -1):
                if STs[m]["bs"] == 1:
                    return m
            return None

        def prev_mainkind(n):
            for m in range(n - 1, -1, -1):
                if not STs[m]["pre"]:
                    return m
            return None

        def tile_back2(n, j, mainonly=False):
            jj = j - 2
            m = n
            while True:
                if jj >= 0:
                    return m, jj
                m -= 1
                while m >= 0 and mainonly and STs[m]["pre"]:
                    m -= 1
                if m < 0:
                    return None
                nt = STs[m]["ntile"]
                jj = nt - 1 if (nt - 1) % 2 == j % 2 else nt - 2

        def ev(*a):
            return "_".join(str(x) for x in a)

        def fm_block(bank, st, col0):
            ntok = st["ntok"]
            return [OP("pe", (lambda k=k: PE_.matmul(BK[bank][:, :ntok], lhsT=WIN[:, k, col0:col0 + P], rhs=XNTB[st["bs"]][:, k, :ntok],
                                                     start=(k == 0), stop=(k == KD_ - 1))),
                       reads=[SEGN[col0 // 512], XNTN[st["bs"]]], writes=[bn(bank)], inc=(k == KD_ - 1), cost=c_pe(ntok)) for k in range(KD_)]

        def tm_block(bank, st, j, col0):
            TW = st["TW"]
            c0 = j * TW
            return [OP("pe", (lambda k=k: PE_.matmul(BK[bank][:TW, :], lhsT=XNTB[st["bs"]][:, k, c0:c0 + TW], rhs=WIN[:, k, col0:col0 + 512],
                                                     start=(k == 0), stop=(k == KD_ - 1))),
                       reads=[SEGN[col0 // 512], XNTN[st["bs"]]], writes=[bn(bank)], inc=(k == KD_ - 1), cost=c_pe(512)) for k in range(KD_)]

        def lane_A(lane):
            for st in STs:
                n, TW, ntile = st["n"], st["TW"], st["ntile"]
                m = prev_same(n)
                if m is not None:
                    yield ("wait", [ev("xntfree", m), ev("POOLall", m), ev("U", m)])
                if st["pre"] and st["bs"] == 1 and prev_mainkind(n) is not None:
                    yield ("wait", [ev("Ymm", prev_mainkind(n)), ev("Hdone", prev_mainkind(n))])
                for j in range(lane, ntile, 2):
                    c0 = j * TW
                    xn, xnn = XN[lane], "XN%d" % lane
                    a0 = 4 * lane
                    ss, rt, rx = SMA[:TW, a0:a0 + 1], SMA[:TW, a0 + 1:a0 + 2], SMA[:TW, a0 + 2:a0 + 3]
                    sn = "SMA/l%d" % lane
                    xx, xxn = X[lane], "X%d" % lane
                    yield DMA("X%d" % lane, xx[:TW], st["x"][c0:c0 + TW, :], writes=[xxn])
                    yield OP("act", lambda: A_.activation(out=xn[:TW], in_=xx[:TW], func=AF.Square, accum_out=ss),
                             reads=[xxn], writes=[xnn, sn + "s"], cost=c_act(D))
                    yield OP("dve", lambda: V.tensor_scalar(out=rt, in0=ss, scalar1=1.0 / D, scalar2=EPS, op0=ALU.mult, op1=ALU.add),
                             reads=[sn + "s"], writes=[sn + "r"], cost=200)
                    yield OP("pool", lambda: G_.tensor_tensor(out=rx, in0=rt, in1=MH[:TW, 0:1], op=ALU.pow),
                             reads=[sn + "r", "MH"], writes=[sn + "x"], cost=500)
                    yield OP("dve", lambda: V.scalar_tensor_tensor(out=xn[:TW], in0=xx[:TW], scalar=rx, in1=NPRE[:TW],
                                                                   op0=ALU.mult, op1=ALU.mult),
                             reads=[xxn, sn + "x", "NPRE"], writes=[xnn], cost=c_dve(D))
                    bank, = yield ("alloc", 1)
                    ops = [OP("pe", (lambda k=k: PE_.transpose(out=BKB[bank][:, k * TW:(k + 1) * TW], in_=xn[:TW, k * P:(k + 1) * P],
                                                               identity=IDB[:TW, :TW])),
                              reads=[xnn, "IDB"], writes=[bn(bank)], inc=(k == KD_ - 1), cost=c_pe(TW)) for k in range(KD_)]
                    ops.append(OP("act", lambda: A_.copy(out=XNTB[st["bs"]][:, :, c0:c0 + TW],
                                                         in_=BKB[bank][:, 0:KD_ * TW].rearrange("p (k t) -> p k t", k=KD_)),
                                  reads=[bn(bank)], writes=[XNTN[st["bs"]]], cost=c_act(KD_ * TW)))
                    yield ("group", ops)
                    yield ("free", [bank])
                    yield ("set", ev("A", n, j))

        def v3(t, st):
            return t[:, :st["ntok"]].rearrange("p (b c) -> p b c", c=st["BL"])

        def lane_F(lane):
            th, gd, ba = TH[lane], GD[lane], BA[lane]
            thn, gdn, ban = "TH%d" % lane, "GD%d" % lane, "BA%d" % lane
            e2, e2n = gd, gdn
            for st in STs:
                n, ntok, BL, pre = st["n"], st["ntok"], st["BL"], st["pre"]
                nblk = ntok // BL
                mid = BL // 2 - 1
                rm, rmn = (RMASKS, "RMASKS") if st["sample"] else (RMASK, "RMASK")
                wl = [ev("A", n, j) for j in range(st["ntile"])]
                if prev_same(n) is not None:
                    wl.append(ev("qkdfree", prev_same(n)))
                if not pre and prev_alt(n) is not None:
                    wl += [ev("qkdfree", prev_alt(n)), ev("xntfree", prev_alt(n)), ev("U", prev_alt(n))]
                if pre and st["bs"] == 1 and prev_mainkind(n) is not None:
                    wl += [ev("Hdone", prev_mainkind(n)), ev("Ymm", prev_mainkind(n))]
                yield ("wait", wl)
                kdt, kdtp, dec, decp = KDTB[st["bs"]], KDTN[st["bs"]], DECB[st["bs"]], DECN[st["bs"]]
                for h in range(lane, 4, 2):
                    bank, = yield ("alloc", 1)
                    ops = fm_block(bank, st, C_F + h * P)
                    ops.append(OP("act", lambda: A_.activation(out=th[:, :ntok], in_=BK[bank][:, :ntok], func=AF.Tanh, scale=0.5),
                                  reads=[bn(bank)], writes=[thn], cost=c_act(ntok)))
                    yield ("group", ops)
                    yield ("free", [bank])
                    yield OP("act", lambda: A_.activation(out=gd[:, :ntok], in_=th[:, :ntok], func=AF.Ln,
                                                          scale=HSC[:, h:h + 1], bias=HBI[:, h:h + 1]),
                             reads=[thn, "HSC", "HBI"], writes=[gdn], cost=c_act(ntok))
                    yield OP("dve", lambda: V.tensor_tensor_scan(out=ba[:, :ntok], data0=rm[:, :ntok], data1=gd[:, :ntok],
                                                                 initial=0.0, op0=ALU.mult, op1=ALU.add),
                             reads=[gdn, rmn], writes=[ban], cost=2.1 * ntok + 160)
                    yield OP("dve", lambda: V.tensor_tensor(out=v3(gd, st), in0=v3(ba, st)[:, :, BL - 1:BL].to_broadcast([P, nblk, BL]),
                                                            in1=v3(ba, st), op=ALU.subtract),
                             reads=[ban, gdn], writes=[gdn], cost=c_dve(ntok))
                    yield OP("act", lambda: A_.activation(out=dec[:, h, 0:nblk], in_=v3(ba, st)[:, :, BL - 1], func=AF.Exp),
                             reads=[ban], writes=[decp + "/%d" % h], cost=300)
                    yield OP("act", lambda: A_.activation(out=gd[:, :ntok], in_=gd[:, :ntok], func=AF.Exp),
                             reads=[gdn], writes=[gdn], cost=c_act(ntok))
                    yield OP("dve", lambda: V.scalar_tensor_tensor(out=kdt[:, h, :ntok], in0=th[:, :ntok], scalar=1.0,
                                                                   in1=gd[:, :ntok], op0=ALU.subtract, op1=ALU.mult),
                             reads=[thn, gdn], writes=[kdtp + "/%d" % h], cost=c_dve(ntok))
                    if not pre:
                        yield OP("act", lambda: A_.activation(out=EMID[:, h, 0:nblk], in_=v3(ba, st)[:, :, mid], func=AF.Exp),
                                 reads=[ban], writes=["EMID/%d" % h], cost=300)
                        yield OP("dve", lambda: V.tensor_tensor(out=v3(gd, st), in0=v3(ba, st),
                                                                in1=v3(ba, st)[:, :, mid:mid + 1].to_broadcast([P, nblk, BL]),
                                                                op=ALU.subtract), reads=[ban, gdn], writes=[gdn], cost=c_dve(ntok))
                        yield OP("act", lambda: A_.activation(out=ba[:, :ntok], in_=gd[:, :ntok], func=AF.Exp, scale=-1.0),
                                 reads=[gdn, ban], writes=[ban], cost=c_act(ntok))
                        yield OP("act", lambda: A_.activation(out=gd[:, :ntok], in_=gd[:, :ntok], func=AF.Exp),
                                 reads=[gdn], writes=[gdn], cost=c_act(ntok))
                        yield OP("dve", lambda: V.scalar_tensor_tensor(out=KE[:, h, :ntok], in0=th[:, :ntok], scalar=1.0,
                                                                       in1=ba[:, :ntok], op0=ALU.subtract, op1=ALU.mult),
                                 reads=[thn, ban], writes=["KE/%d" % h], cost=c_dve(ntok))
                        bank, = yield ("alloc", 1)
                        ops = fm_block(bank, st, C_Q + h * P)
                        ops.append(OP("dve", lambda: V.scalar_tensor_tensor(out=QE[:, h, :ntok], in0=BK[bank][:, :ntok],
                                                                            scalar=OML[:, h:h + 1], in1=gd[:, :ntok],
                                                                            op0=ALU.mult, op1=ALU.mult),
                                      reads=[bn(bank), "OML", gdn], writes=["QE/%d" % h], cost=c_dve(ntok, True)))
                        yield ("group", ops)
                        yield ("free", [bank])
                    yield ("set", ev("F", n, h))

        def lane_U():
            for st in STs:
                n, TW, ntile, pre = st["n"], st["TW"], st["ntile"], st["pre"]
                yield ("wait", [ev("A", n, j) for j in range(ntile)] + ([ev("POOLall", n - 1)] if n > 0 else []))
                for j in range(ntile):
                    if pre and st["want_u"] != j:
                        continue
                    bank, = yield ("alloc", 1)
                    ops = tm_block(bank, st, j, C_U)
                    ops.append(OP("act", lambda: A_.copy(out=UTOK[:TW, j, :], in_=BK[bank][:TW, :]),
                                  reads=[bn(bank)], writes=["UTOK"], cost=c_act(512)))
                    if st["want_u"] == j and st["udst"] is not None:
                        yield ("wait", [ev("Zall", n - 1)] if n > 0 else [])
                        ops.append(OP("act", lambda: A_.copy(out=T2[:TW, 0:512], in_=BK[bank][:TW, :]),
                                      reads=[bn(bank)], writes=["T2"], cost=c_act(512)))
                        ops.append(DMA("UD%d" % n, st["udst"], T2[:TW, 0:512], reads=["T2"]))
                    yield ("group", ops)
                    yield ("free", [bank])
                if pre and st["want_u"] is not None:
                    yield OP("pool", lambda: G_.tensor_copy(out=UPREV[:], in_=UTOK[:, 3, :]), reads=["UTOK"], writes=["UPREV"],
                             cost=c_pool(512))
                yield ("set", ev("U", n))

        def lane_POOL(lane):
            sg, sgn, ptb, ptn = SGL[lane], "SGL%d" % lane, PTBL[lane], "PTBL%d" % lane
            for st in STs:
                n, TW, ntile, ntok = st["n"], st["TW"], st["ntile"], st["ntok"]
                if st["pre"]:
                    yield ("set", ev("POOL", n, lane))
                    continue
                yield ("wait", [ev("U", n)] + ([ev("Ymm", prev_mainkind(n))] if prev_mainkind(n) is not None else []) + ([ev("xntfree", prev_alt(n)), ev("U", prev_alt(n))] if prev_alt(n) is not None else []))
                for g in range(lane, 4, 2):
                    bank, = yield ("alloc", 1)
                    ops = fm_block(bank, st, C_GP + g * P)
                    ops.append(OP("act", lambda: A_.activation(out=sg[:, :ntok], in_=BK[bank][:, :ntok], func=AF.Silu),
                                  reads=[bn(bank)], writes=[sgn], cost=c_act(ntok)))
                    yield ("group", ops)
                    yield ("free", [bank])
                    bank, = yield ("alloc", 1)
                    ops = []
                    for j in range(ntile):
                        c0 = j * TW
                        if st["sample"]:
                            bc_, bcn = BANDSC, "BANDSC"
                        elif st["first"] and j == 0:
                            bc_, bcn = BANDF, "BANDF"
                        else:
                            bc_, bcn = BANDC, "BANDC"
                        ops.append(OP("pe", (lambda j=j, c0=c0, bc_=bc_: PE_.matmul(BK[bank][:, c0:c0 + TW], lhsT=UTOK[:TW, j, g * P:(g + 1) * P],
                                                                                    rhs=bc_[:TW, g, :TW], start=True, stop=False)),
                                      reads=["UTOK", bcn], writes=[bn(bank)], inc=False, cost=c_pe(TW)))
                        if st["sample"]:
                            ops.append(OP("pe", (lambda c0=c0: PE_.matmul(BK[bank][:, c0:c0 + TW], lhsT=CACHE[0:64, g * P:(g + 1) * P],
                                                                          rhs=BANDSP[0:64, g, :TW], start=False, stop=True)),
                                          reads=["CACHE", "BANDSP"], writes=[bn(bank)], inc=(j == ntile - 1), cost=c_pe(TW)))
                        else:
                            if j == 0:
                                pv, pvn = UPREV[64:128, g * P:(g + 1) * P], "UPREV"
                            else:
                                pv, pvn = UTOK[64:128, j - 1, g * P:(g + 1) * P], "UTOK"
                            ops.append(OP("pe", (lambda c0=c0, pv=pv: PE_.matmul(BK[bank][:, c0:c0 + TW], lhsT=pv, rhs=BANDP[64:128, g, :TW],
                                                                                 start=False, stop=True)),
                                          reads=[pvn, "BANDP"], writes=[bn(bank)], inc=(j == ntile - 1), cost=c_pe(TW)))
                    ops.append(OP("act", lambda: A_.copy(out=ptb[:, :ntok], in_=BK[bank][:, :ntok]), reads=[bn(bank)], writes=[ptn],
                                  cost=c_act(ntok)))
                    yield ("group", ops)
                    yield ("free", [bank])
                    bank, = yield ("alloc", 1)
                    yield ("group", [
                        OP("pe", lambda: PE_.matmul(BK[bank][:, :ntok], lhsT=WPOOL[:, g, :], rhs=ptb[:, :ntok], start=True, stop=True),
                           reads=["WPOOL", ptn], writes=[bn(bank)], cost=c_pe(ntok)),
                        OP("dve", lambda: V.scalar_tensor_tensor(out=MIXT[:, g, :ntok], in0=BK[bank][:, :ntok], scalar=PSC[:, g:g + 1],
                                                                 in1=sg[:, :ntok], op0=ALU.mult, op1=ALU.mult),
                           reads=[bn(bank), "PSC", sgn], writes=["MIXT/p%d" % g], cost=c_dve(ntok, True)),
                    ])
                    yield ("free", [bank])
                yield ("set", ev("POOL", n, lane))

        def lane_POOLJOIN():
            for st in STs:
                n = st["n"]
                yield ("wait", [ev("POOL", n, 0), ev("POOL", n, 1)])
                if not st["pre"] and not st["sample"]:
                    yield OP("pool", lambda: G_.tensor_copy(out=UPREV[:], in_=UTOK[:, 3, :]), reads=["UTOK"], writes=["UPREV"],
                             cost=c_pool(512))
                yield ("set", ev("POOLall", n))

        def lane_H1():
            for st in STs:
                n, TW, ntile, ntok, BL, pre = st["n"], st["TW"], st["ntile"], st["ntok"], st["BL"], st["pre"]
                matt, mattn = (MSATT, "MSATT") if st["sample"] else (MATT, "MATT")
                kdt, kdtp = KDTB[st["bs"]], KDTN[st["bs"]]
                yield ("wait", [ev("F", n, h) for h in range(4)])
                for j in range(ntile):
                    c0 = j * TW
                    par = j % 2
                    vt, vtn, sh, shn, kd, kdn, att, attn = VTOK[par], "VTOK%d" % par, SH[par], "SH%d" % par, KD[par], "KD%d" % par, ATT[par], "ATT%d" % par
                    tb = tile_back2(n, j)
                    if tb is not None:
                        yield ("wait", [ev("H", tb[0], tb[1])])
                    bank, = yield ("alloc", 1)
                    ops = tm_block(bank, st, j, C_V)
                    ops.append(OP("act", lambda: A_.mul(out=vt[:TW, :], in_=BK[bank][:TW, :], mul=-0.5), reads=[bn(bank)],
                                  writes=[vtn], cost=c_act(512)))
                    yield ("group", ops)
                    yield ("free", [bank])
                    if not pre:
                        bank, = yield ("alloc", 1)
                        ops = tm_block(bank, st, j, C_GH)
                        ops.append(OP("act", lambda: A_.activation(out=sh[:TW, :], in_=BK[bank][:TW, :], func=AF.Silu),
                                      reads=[bn(bank)], writes=[shn], cost=c_act(512)))
                        yield ("group", ops)
                        yield ("free", [bank])
                    if j == ntile - 1:
                        yield ("set", ev("xntfree", n))
                    bank, = yield ("alloc", 1)
                    ops = [OP("pe", (lambda h=h: PE_.transpose(out=BKB[bank][:TW, h * P:(h + 1) * P], in_=kdt[:, h, c0:c0 + TW], identity=IDB[:, :])),
                              reads=[kdtp + "/%d" % h, "IDB"], writes=[bn(bank)], inc=(h == 3), cost=c_pe(P)) for h in range(4)]
                    ops.append(OP("act", lambda: A_.copy(out=kd[:TW, :], in_=BKB[bank][:TW, 0:512]), reads=[bn(bank)], writes=[kdn],
                                  cost=c_act(512)))
                    yield ("group", ops)
                    yield ("free", [bank])
                    if not pre:
                        bank, = yield ("alloc", 1)
                        ops = [OP("pe", (lambda h=h: PE_.matmul(BK[bank][:TW, h * TW:(h + 1) * TW], lhsT=KE[:, h, c0:c0 + TW],
                                                                rhs=QE[:, h, c0:c0 + TW], start=True, stop=True)),
                                  reads=["KE/%d" % h, "QE/%d" % h], writes=[bn(bank)], inc=(h == 3), cost=c_pe(TW)) for h in range(4)]
                        ops.append(OP("dve", lambda: V.tensor_tensor(out=att[:TW, :, :TW],
                                                                     in0=BK[bank][:TW, 0:4 * TW].rearrange("p (h t) -> p h t", h=4),
                                                                     in1=matt[:TW, :TW].unsqueeze(1).to_broadcast([TW, 4, TW]),
                                                                     op=ALU.mult), reads=[bn(bank), mattn], writes=[attn],
                                      cost=c_dve(4 * TW, True)))
                        yield ("group", ops)
                        yield ("free", [bank])
                    yield ("set", ev("H1", n, j))

        def lane_H2():
            for st in STs:
                n, TW, ntile, ntok, BL, pre = st["n"], st["TW"], st["ntile"], st["ntok"], st["BL"], st["pre"]
                dec, decp = DECB[st["bs"]], DECN[st["bs"]]
                if st["sample"]:
                    if prev_mainkind(n) is not None:
                        yield ("wait", [ev("Zall", prev_mainkind(n)), ev("Hdone", n - 1)])
                    for q in range(2):
                        sid = 1 + q
                        yield DMA("SIN%d" % q, S32[sid], state_smp[q].rearrange("h k v -> k h v"),
                                  writes=[snm(sid, h) for h in range(4)])
                        for h in range(4):
                            yield OP("dve", (lambda h=h, sid=sid: V.tensor_scalar(out=S32[sid][:, h, :], in0=S32[sid][:, h, :],
                                                                                  scalar1=IOML[:, h:h + 1], scalar2=None, op0=ALU.mult)),
                                     reads=[snm(sid, h), "IOML"], writes=[snm(sid, h)], cost=c_dve(P))
                for j in range(ntile):
                    c0 = j * TW
                    par = j % 2
                    vt, vtn, sh, shn, kd, kdn, att, attn = VTOK[par], "VTOK%d" % par, SH[par], "SH%d" % par, KD[par], "KD%d" % par, ATT[par], "ATT%d" % par
                    wl = [ev("H1", n, j)]
                    pm = prev_mainkind(n)
                    if not pre and pm is not None and j < STs[pm]["ntile"]:
                        wl.append(ev("Ymmt", pm, j))
                    yield ("wait", wl)
                    bo, bu = yield ("alloc", 2)
                    for bi, (r0, ln, sid) in enumerate(st["blocks"][j]):
                        blk = (c0 + r0) // BL
                        ops = [OP("pe", (lambda h=h: PE_.matmul(BK[bu][:, h * P:(h + 1) * P], lhsT=kd[r0:r0 + ln, h * P:(h + 1) * P],
                                                                rhs=vt[r0:r0 + ln, h * P:(h + 1) * P], start=True, stop=True)),
                                  reads=[kdn, vtn], writes=[bn(bu)], inc=(h == 3), cost=c_pe(P)) for h in range(4)]
                        yield ("group", ops)
                        if not pre:
                            for h in range(4):
                                if h < 2:
                                    yield OP("pool", (lambda h=h: G_.tensor_tensor(out=SB[sid][:, h, :], in0=S32[sid][:, h, :],
                                                                                   in1=EMID[:, h, blk:blk + 1].to_broadcast([P, P]),
                                                                                   op=ALU.mult)),
                                             reads=[snm(sid, h), "EMID/%d" % h], writes=[sbnm(sid, h)], cost=c_pool(P))
                                else:
                                    yield OP("act", (lambda h=h: A_.activation(out=SB[sid][:, h, :], in_=S32[sid][:, h, :], func=AF.Copy,
                                                                               scale=EMID[:, h, blk:blk + 1])),
                                             reads=[snm(sid, h), "EMID/%d" % h], writes=[sbnm(sid, h)], cost=c_act(P))
                            ops = []
                            for h in range(4):
                                ops.append(OP("pe", (lambda h=h: PE_.matmul(BK[bo][r0:r0 + ln, h * P:(h + 1) * P], lhsT=att[:TW, h, r0:r0 + ln],
                                                                            rhs=vt[:TW, h * P:(h + 1) * P], start=True, stop=False)),
                                              reads=[attn, vtn], writes=[bn(bo)], inc=False, cost=c_pe(P)))
                                ops.append(OP("pe", (lambda h=h: PE_.matmul(BK[bo][r0:r0 + ln, h * P:(h + 1) * P], lhsT=QE[:, h, c0 + r0:c0 + r0 + ln],
                                                                            rhs=SB[sid][:, h, :], start=False, stop=True)),
                                              reads=["QE/%d" % h, sbnm(sid, h)], writes=[bn(bo)], inc=(h == 3), cost=c_pe(P)))
                            yield ("group", ops)
                        for h in range(4):
                            yield OP("dve", (lambda h=h: V.scalar_tensor_tensor(out=S32[sid][:, h, :], in0=S32[sid][:, h, :],
                                                                                scalar=dec[:, h, blk:blk + 1], in1=BK[bu][:, h * P:(h + 1) * P],
                                                                                op0=ALU.mult, op1=ALU.add)),
                                     reads=[snm(sid, h), decp + "/%d" % h, bn(bu)], writes=[snm(sid, h)], cost=c_dve(P, True))
                    if j == ntile - 1:
                        yield ("set", ev("qkdfree", n))
                    if pre:
                        yield ("free", [bo, bu])
                        yield ("set", ev("H", n, j))
                        continue
                    for h in range(4):
                        yield OP("act", (lambda h=h: A_.activation(out=OG[:TW, h * P:(h + 1) * P], in_=BK[bo][:TW, h * P:(h + 1) * P],
                                                                   func=AF.Square, accum_out=SMH[:TW, h:h + 1])),
                                 reads=[bn(bo)], writes=["OG/%d" % h, "SMH/s%d" % h], cost=c_act(P))
                    yield OP("dve", lambda: V.tensor_scalar(out=SMH[:TW, 4:8], in0=SMH[:TW, 0:4], scalar1=1.0 / P, scalar2=EPS,
                                                            op0=ALU.mult, op1=ALU.add), reads=["SMH/s%d" % h for h in range(4)],
                             writes=["SMH/r"], cost=200)
                    yield OP("pool", lambda: G_.tensor_tensor(out=SMH[:TW, 8:12], in0=SMH[:TW, 4:8], in1=MH[:TW, 0:4], op=ALU.pow),
                             reads=["SMH/r", "MH"], writes=["SMH/o"], cost=600)
                    for h in range(4):
                        yield OP("dve", (lambda h=h: V.scalar_tensor_tensor(out=OG[:TW, h * P:(h + 1) * P], in0=BK[bo][:TW, h * P:(h + 1) * P],
                                                                            scalar=SMH[:TW, 8 + h:9 + h], in1=sh[:TW, h * P:(h + 1) * P],
                                                                            op0=ALU.mult, op1=ALU.mult)),
                                 reads=[bn(bo), "SMH/o", shn], writes=["OG/%d" % h], cost=c_dve(P, True))
                    yield ("free", [bo, bu])
                    bank, = yield ("alloc", 1)
                    ops = [OP("pe", (lambda h=h: PE_.transpose(out=BKB[bank][:, h * TW:(h + 1) * TW], in_=OG[:TW, h * P:(h + 1) * P],
                                                               identity=IDB[:TW, :TW])),
                              reads=["OG/%d" % h, "IDB"], writes=[bn(bank)], inc=(h == 3), cost=c_pe(TW)) for h in range(4)]
                    ops.append(OP("act", lambda: A_.copy(out=MIXT[:, 4:8, c0:c0 + TW],
                                                         in_=BKB[bank][:, 0:4 * TW].rearrange("p (h t) -> p h t", h=4)),
                                  reads=[bn(bank)], writes=["MIXT/h%d" % j], cost=c_act(4 * TW)))
                    yield ("group", ops)
                    yield ("free", [bank])
                    yield ("set", ev("H", n, j))
                yield ("set", ev("Hdone", n))

        folded = [False]
        smp_n = [next((st["n"] for st in STs if st["sample"]), None)]

        def lane_Y():
            for st in STs:
                n, TW, ntile, pre = st["n"], st["TW"], st["ntile"], st["pre"]
                if pre:
                    yield ("set", ev("Ymm", n))
                    continue
                if not folded[0]:
                    folded[0] = True
                    yield OP("dve", lambda: V.tensor_scalar(out=WOUT[:, 4:8, :], in0=WOUT[:, 4:8, :], scalar1=HGN[:, 0:1], scalar2=None,
                                                            op0=ALU.mult), reads=["WOUT", "HGN"], writes=["WOUT"], cost=c_dve(4096, fast=4))
                    yield OP("dve", lambda: V.tensor_scalar(out=WPLE[:], in0=WPLE[:], scalar1=0.5, scalar2=None, op0=ALU.mult),
                             reads=["WPLE"], writes=["WPLE"], cost=c_dve(2048, fast=4))
                yield ("wait", [ev("POOLall", n)])
                for j in range(ntile):
                    c0 = j * TW
                    par = j % 2
                    xr, xrn = XR[par], "XR%d" % par
                    x1t, x1tn = X1T[par], "X1T%d" % par
                    yield ("wait", [ev("H", n, j)])
                    tb = tile_back2(n, j, mainonly=True)
                    if tb is not None:
                        yield ("wait", [ev("Z", tb[0], tb[1])])
                    if smp_n[0] is not None and smp_n[0] < n:
                        yield ("wait", [ev("stateout", smp_n[0])])
                    yield DMA("XR%d" % par, xr[:TW], st["x"][c0:c0 + TW, :], writes=[xrn])
                    b0, b1 = yield ("alloc", 2)
                    ops = []
                    for hf, bank in enumerate((b0, b1)):
                        for c in range(8):
                            ops.append(OP("pe", (lambda c=c, hf=hf, bank=bank: PE_.matmul(BK[bank][:TW, :], lhsT=MIXT[:, c, c0:c0 + TW],
                                                                                          rhs=WOUT[:, c, hf * 512:(hf + 1) * 512],
                                                                                          start=(c == 0), stop=(c == 7))),
                                          reads=["MIXT/p%d" % c if c < 4 else "MIXT/h%d" % j, "WOUT"], writes=[bn(bank)], inc=(c == 7),
                                          cost=c_pe(512)))
                    yield ("group", ops)
                    yield ("set", ev("Ymmt", n, j))
                    if j == ntile - 1:
                        yield ("set", ev("Ymm", n))
                    for hf, bank in enumerate((b0, b1)):
                        yield OP("act", (lambda hf=hf, bank=bank: A_.activation(out=X1B[:TW, hf * 512:(hf + 1) * 512], in_=BK[bank][:TW, :],
                                                                                func=AF.Square, accum_out=SMY[:TW, hf:hf + 1])),
                                 reads=[bn(bank)], writes=["X1B/%d" % hf, "SMY/s%d" % hf], cost=c_act(512))
                    yield OP("dve", lambda: V.tensor_tensor(out=SMY[:TW, 2:3], in0=SMY[:TW, 0:1], in1=SMY[:TW, 1:2], op=ALU.add),
                             reads=["SMY/s0", "SMY/s1"], writes=["SMY/a"], cost=200)
                    yield OP("dve", lambda: V.tensor_scalar(out=SMY[:TW, 3:4], in0=SMY[:TW, 2:3], scalar1=1.0 / D, scalar2=EPS,
                                                            op0=ALU.mult, op1=ALU.add), reads=["SMY/a"], writes=["SMY/b"], cost=200)
                    yield OP("pool", lambda: G_.tensor_tensor(out=SMY[:TW, 4:5], in0=SMY[:TW, 3:4], in1=MH[:TW, 0:1], op=ALU.pow),
                             reads=["SMY/b", "MH"], writes=["SMY/c"], cost=500)
                    for hf, bank in enumerate((b0, b1)):
                        yield OP("dve", (lambda hf=hf, bank=bank: V.scalar_tensor_tensor(out=T1[:TW, hf * 512:(hf + 1) * 512], in0=BK[bank][:TW, :],
                                                                                         scalar=SMY[:TW, 4:5], in1=GPOST[:TW, hf * 512:(hf + 1) * 512],
                                                                                         op0=ALU.mult, op1=ALU.mult)),
                                 reads=[bn(bank), "SMY/c", "GPOST"], writes=["T1/%d" % hf], cost=c_dve(512, True))
                    yield ("free", [b0, b1])
                    yield OP("dve", lambda: V.tensor_tensor(out=xr[:TW], in0=T1[:TW], in1=xr[:TW], op=ALU.add),
                             reads=["T1", xrn], writes=[xrn], cost=c_dve(D))
                    yield OP("act", lambda: A_.copy(out=X1B[:TW], in_=xr[:TW]), reads=[xrn], writes=["X1B"],
                             cost=c_act(D))
                    bank, = yield ("alloc", 1)
                    ops = [OP("pe", (lambda k=k: PE_.transpose(out=BKB[bank][:, k * TW:(k + 1) * TW], in_=X1B[:TW, k * P:(k + 1) * P],
                                                               identity=IDB[:TW, :TW])),
                              reads=["X1B", "IDB"], writes=[bn(bank)], inc=(k == KD_ - 1), cost=c_pe(TW)) for k in range(KD_)]
                    ops.append(OP("act", lambda: A_.copy(out=x1t[:, :, :TW], in_=BKB[bank][:, 0:KD_ * TW].rearrange("p (k t) -> p k t", k=KD_)),
                                  reads=[bn(bank)], writes=[x1tn], cost=c_act(KD_ * TW)))
                    yield ("group", ops)
                    yield ("free", [bank])
                    yield ("set", ev("Y", n, j))

        def lane_Z():
            for st in STs:
                n, TW, ntile, pre = st["n"], st["TW"], st["ntile"], st["pre"]
                if pre:
                    yield ("set", ev("Zall", n))
                    continue
                if st["sample"]:
                    yield DMA("PIN", PIN[:TW, 0, :], st["p"], writes=["PIN"])
                else:
                    yield DMA("PIN", PIN[:], st["p"].rearrange("(j p) d -> p j d", p=P), writes=["PIN"])
                for j in range(ntile):
                    c0 = j * TW
                    par = j % 2
                    xr, xrn = XR[par], "XR%d" % par
                    x1t, x1tn = X1T[par], "X1T%d" % par
                    pb, pbn, ptt, pttn = PB[par], "PBF%d" % par, PTT[par], "PTT%d" % par
                    yield OP("pool", lambda: G_.tensor_copy(out=pb[:TW], in_=PIN[:TW, j, :]), reads=["PIN"], writes=[pbn], cost=c_pool(DPLE))
                    bank, = yield ("alloc", 1)
                    ops = [OP("pe", (lambda k=k: PE_.transpose(out=BKB[bank][:, k * TW:(k + 1) * TW], in_=pb[:TW, k * P:(k + 1) * P],
                                                               identity=IDB[:TW, :TW])),
                              reads=[pbn, "IDB"], writes=[bn(bank)], inc=(k == 1), cost=c_pe(TW)) for k in range(2)]
                    ops.append(OP("act", lambda: A_.copy(out=ptt[:, :, :TW], in_=BKB[bank][:, 0:2 * TW].rearrange("p (k t) -> p k t", k=2)),
                                  reads=[bn(bank)], writes=[pttn], cost=c_act(2 * TW)))
                    yield ("group", ops)
                    yield ("free", [bank])
                    yield ("wait", [ev("Y", n, j)])
                    for hf in range(2):
                        bg, bw = yield ("alloc", 2)
                        ops = []
                        for k in range(KD_):
                            ops.append(OP("pe", (lambda k=k: PE_.matmul(BK[bg][:TW, :], lhsT=x1t[:, k, :TW], rhs=WG[:, k, hf * 512:(hf + 1) * 512],
                                                                        start=(k == 0), stop=(k == KD_ - 1))),
                                          reads=[x1tn, "WG"], writes=[bn(bg)], inc=(k == KD_ - 1), cost=c_pe(512)))
                        for k in range(2):
                            ops.append(OP("pe", (lambda k=k: PE_.matmul(BK[bw][:TW, :], lhsT=ptt[:, k, :TW], rhs=WPLE[:, k, hf * 512:(hf + 1) * 512],
                                                                        start=(k == 0), stop=(k == 1))),
                                          reads=[pttn, "WPLE"], writes=[bn(bw)], inc=(k == 1), cost=c_pe(512)))
                        ops.append(OP("act", lambda: A_.activation(out=T2[:TW, hf * 512:(hf + 1) * 512], in_=BK[bg][:TW, :], func=AF.Tanh, scale=0.5),
                                      reads=[bn(bg)], writes=["T2/%d" % hf], cost=c_act(512)))
                        ops.append(OP("dve", lambda: V.scalar_tensor_tensor(out=T2[:TW, hf * 512:(hf + 1) * 512], in0=T2[:TW, hf * 512:(hf + 1) * 512],
                                                                            scalar=1.0, in1=BK[bw][:TW, :], op0=ALU.add, op1=ALU.mult),
                                      reads=["T2/%d" % hf, bn(bw)], writes=["T2/%d" % hf], cost=c_dve(512, True)))
                        yield ("group", ops)
                        yield ("free", [bg, bw])
                    yield OP("pool", lambda: G_.tensor_tensor(out=T2[:TW, 0:512], in0=T2[:TW, 0:512], in1=xr[:TW, 0:512], op=ALU.add),
                             reads=["T2/0", xrn], writes=["T2/0"], cost=c_pool(512))
                    yield OP("dve", lambda: V.tensor_tensor(out=T2[:TW, 512:D], in0=T2[:TW, 512:D], in1=xr[:TW, 512:D], op=ALU.add),
                             reads=["T2/1", xrn], writes=["T2/1"], cost=c_dve(512))
                    yield DMA("OUT", st["y"][c0:c0 + TW, :], T2[:TW], reads=["T2"], eng="pool")
                    yield ("set", ev("Z", n, j))
                if st.get("final_state") or st["sample"]:
                    sids = [1, 2] if st["sample"] else [0]
                    yield ("wait", [ev("Hdone", n)])
                    for qi, sid in enumerate(sids):
                        for h in range(4):
                            yield OP("dve", (lambda h=h, sid=sid: V.tensor_scalar(out=T2[:, h * P:(h + 1) * P], in0=S32[sid][:, h, :],
                                                                                  scalar1=OML[:, h:h + 1], scalar2=None, op0=ALU.mult)),
                                     reads=[snm(sid, h), "OML"], writes=["T2"], cost=c_dve(P))
                        dst = state_s[qi] if st["sample"] else state_p
                        yield DMA("ST%d_%d" % (n, qi), dst.rearrange("h k v -> k h v"),
                                  T2[:, 0:512].rearrange("p (h v) -> p h v", h=4), reads=["T2"])
                    yield ("set", ev("stateout", n))
                yield ("set", ev("Zall", n))

        run = Runner(s, list(range(NB)))
        for nm, tt in WLOAD:
            run.t_w[nm] = tt
        run.run([("H2", lane_H2()), ("F0", lane_F(0)), ("F1", lane_F(1)), ("H1", lane_H1()), ("A0", lane_A(0)), ("A1", lane_A(1)),
                 ("U", lane_U()), ("P0", lane_POOL(0)), ("P1", lane_POOL(1)), ("PJ", lane_POOLJOIN()),
                 ("Y", lane_Y()), ("Z", lane_Z())])
        s.finish("sp")
    return nc, s


def _band_consts():
    W = (2, 4, 8, 16)
    cur = np.zeros((P, 4, P), np.float32)
    prev = np.zeros((P, 4, P), np.float32)
    first = np.zeros((P, 4, P), np.float32)
    for g, w in enumerate(W):
        for t in range(P):
            for sg in range(t - w + 1, t + 1):
                if sg >= 0:
                    cur[sg, g, t] += 1.0 / w
                else:
                    prev[P + sg, g, t] += 1.0 / w
            cur[t, g, t] -= 1.0
            cnt = min(w, t + 1)
            for sg in range(max(0, t - w + 1), t + 1):
                first[sg, g, t] += 1.0 / cnt
            first[t, g, t] -= 1.0
    scur = np.zeros((64, 4, 64), np.float32)
    sprev = np.zeros((64, 4, 64), np.float32)
    for g, w in enumerate(W):
        for q in range(2):
            o = 32 * q
            for i in range(16):
                e = 15 + i
                for ee in range(e - w + 1, e + 1):
                    if ee >= 15:
                        scur[o + ee - 15, g, o + i] += 1.0 / w
                    else:
                        sprev[o + ee, g, o + i] += 1.0 / w
                scur[o + i, g, o + i] -= 1.0
    matt = np.zeros((P, P), np.float32)
    for s_ in range(P):
        for t in range(P):
            if s_ // 64 == t // 64 and s_ <= t:
                matt[s_, t] = 1.0
    msatt = np.zeros((64, 64), np.float32)
    for s_ in range(64):
        for t in range(64):
            if s_ // 16 == t // 16 and s_ <= t and (s_ // 16) in (0, 2):
                msatt[s_, t] = 1.0
    rmask = np.ones((P, 512), np.float32)
    rmask[:, ::64] = 0.0
    rmask_s = np.ones((P, 64), np.float32)
    rmask_s[:, ::16] = 0.0
    return dict(band_cur=cur, band_prev=prev, band_first=first, band_scur=scur, band_sprev=sprev,
                mask_att=matt, mask_satt=msatt, rmask=rmask, rmask_s=rmask_s)


_PROG = {}


def _get_prog(npre, nmain):
    key = (npre, nmain)
    if key not in _PROG:
        _PROG[key] = build_program(npre, nmain)[0]
    return _PROG[key]


def make_in_maps(inputs, npre=HALF, nmain=HALF):
    f = lambda a: np.ascontiguousarray(np.asarray(a, dtype=np.float32))
    xp = f(inputs["x_prompt"])
    pp = f(inputs["p_prompt"])[0]
    xs = f(inputs["x_sample"])
    ps_ = f(inputs["p_sample"])[0]
    cache = f(inputs["cache_pool"])[0]
    st = f(inputs["state_hgrn"])[0]
    consts = _band_consts()
    shared = dict(
        w_in=f(inputs["w_in"])[0], w_pool=f(inputs["w_pool"])[0], pool_scale=f(inputs["pool_scale"])[0],
        hg_norm=f(inputs["hg_norm"])[0], w_out=f(inputs["w_out"])[0], norm_pre=f(inputs["norm_pre"])[0],
        norm_post=f(inputs["norm_post"])[0], w_ple=f(inputs["w_ple"])[0], w_gate=f(inputs["w_ple_gate"])[0],
        lb_logits=f(inputs["lb_logits"]),
    )
    in_maps = []
    for c in range(NCORES):
        b = c % 4
        second = c >= 4
        m = dict(shared)
        m.update({k: v for k, v in consts.items()})
        if second:
            m["x_pre"] = f(xp[b, HALF - npre:HALF]) if npre else np.zeros((P, D), np.float32)
            m["x_main"] = f(xp[b, HALF:HALF + nmain])
            m["p_main"] = f(pp[b, HALF:HALF + nmain])
            m["band_first"] = consts["band_cur"]
        else:
            m["x_pre"] = np.zeros((max(npre, P), D), np.float32)
            m["x_main"] = f(xp[b, 0:nmain])
            m["p_main"] = f(pp[b, 0:nmain])
        xsm = np.zeros((64, D), np.float32)
        psm = np.zeros((64, DPLE), np.float32)
        csm = np.zeros((64, 512), np.float32)
        for q in range(2):
            i = 2 * c + q
            xsm[32 * q:32 * q + 16] = xs[i]
            psm[32 * q:32 * q + 16] = ps_[i]
            csm[32 * q:32 * q + 15] = cache[i]
        m["x_smp"] = xsm
        m["p_smp"] = psm
        m["cache_smp"] = csm
        m["state_smp"] = f(st[2 * c:2 * c + 2])
        in_maps.append(m)
    return in_maps


def kernel(**inputs):
    nc = _get_prog(HALF, HALF)
    in_maps = make_in_maps(inputs)
    res = run_bass_kernel_spmd(nc, in_maps, core_ids=list(range(NCORES)))
    r = res.results
    y_prompt = np.zeros((4, SEQ, D), np.float32)
    y_sample = np.zeros((16, 16, D), np.float32)
    pool_p = np.zeros((1, 4, 15, 512), np.float32)
    hg_p = np.zeros((1, 4, 4, P, P), np.float32)
    pool_s = np.zeros((1, 16, 15, 512), np.float32)
    hg_s = np.zeros((1, 16, 4, P, P), np.float32)
    for c in range(NCORES):
        b = c % 4
        if c < 4:
            y_prompt[b, :HALF] = r[c]["y_main"]
        else:
            y_prompt[b, HALF:] = r[c]["y_main"]
            pool_p[0, b] = r[c]["pool_p"][P - 15:P]
            hg_p[0, b] = r[c]["state_p"]
        for q in range(2):
            i = 2 * c + q
            y_sample[i] = r[c]["y_smp"][32 * q:32 * q + 16]
            pool_s[0, i] = r[c]["pool_s"][32 * q + 1:32 * q + 16]
            hg_s[0, i] = r[c]["state_s"][q]
    return (y_prompt, y_sample, pool_p, hg_p, pool_s, hg_s)
```
